# Optimizing a Trainium2 kernel written in Bass

```python
import math
import jax, jax.numpy as jnp
from jax import lax
import numpy as np

D_MODEL = 1024
BATCH = 16
SEQ = 256
DEPTH = 4
DEC_BATCH = 2
DEC_SEQ = 2048
PAST_LEN = 512

GRID_W = 64
N_MIXERS = 2
N_HYENA = (DEPTH + 1) // 2
N_MLA = DEPTH // 2
EPS = 1e-6
HY_WIDTH = D_MODEL
SHORT_CONV = 3
FILTER_BANDS = 16
FILTER_EMB = 1 + 2 * FILTER_BANDS
FILTER_HIDDEN = 64
FAST_DECAY_PCT = 0.3
SLOW_DECAY_PCT = 1.5
DECAY_TARGET = 1e-2
N_HEADS = 16
Q_LORA = 384
KV_LORA = 256
QK_NOPE = 64
QK_ROPE = 32
V_HEAD = 64
ROPE_THETA = 10000.0
Q_BLOCK = 128
MLA_IN = Q_LORA + KV_LORA + QK_ROPE + N_HEADS * V_HEAD

kernel_name = 'hybrid_hyena_mla_diffusion_step'


def rmsnorm(x, g):
    xf = x.astype(jnp.float32)
    y = xf * lax.rsqrt(jnp.mean(xf * xf, axis=-1, keepdims=True) + EPS)
    return (y * g.astype(jnp.float32)).astype(x.dtype)


def adaln(cond, w, b):
    m = (jax.nn.silu(cond) @ w + b)[:, None, :]
    return jnp.split(m, 3, axis=-1)


def short_conv(u, w, b):
    L = u.shape[1]
    pad = SHORT_CONV // 2
    up = jnp.pad(u, ((0, 0), (pad, pad), (0, 0)))
    out = b
    for k in range(SHORT_CONV):
        out = out + up[:, k:k + L] * w[k]
    return out


def hyena_filters(L, w1, b1, freq, w2, b2, w3):
    f32 = jnp.float32
    t = jnp.linspace(0.0, 1.0, L, dtype=f32)[:, None]
    w = (2.0 * math.pi / L) * jnp.arange(L, dtype=f32)[:, None]
    bands = jnp.linspace(1e-4, FILTER_BANDS - 1, FILTER_BANDS, dtype=f32)[None, :]
    z = jnp.concatenate([t, jnp.cos(bands * w), -jnp.sin(bands * w)], axis=-1)
    fr = freq.astype(f32)
    hdn = jnp.sin(fr * (z @ w1.astype(f32) + b1.astype(f32)))
    hdn = jnp.sin(fr * (hdn @ w2.astype(f32) + b2.astype(f32)))
    h = hdn @ w3.astype(f32)
    max_decay = math.log(DECAY_TARGET) / FAST_DECAY_PCT
    min_decay = math.log(DECAY_TARGET) / SLOW_DECAY_PCT
    deltas = jnp.abs(jnp.linspace(min_decay, max_decay, HY_WIDTH, dtype=f32))
    deltas = jnp.concatenate([deltas, deltas])
    h = h * jnp.exp(-t * deltas)
    return h / jnp.sum(jnp.abs(h), axis=0, keepdims=True)


def hyena_mix(h, w_in, conv_w, conv_b, f_w1, f_b1, f_freq, f_w2, f_b2, f_w3, f_bias, w_out):
    L = h.shape[1]
    n = 2 * L
    proj = h @ w_in
    u = short_conv(proj[..., :3 * HY_WIDTH], conv_w, conv_b)
    gate = proj[..., 3 * HY_WIDTH:]
    x0, x1, v = jnp.split(u, 3, axis=-1)
    z = (v * x1).astype(jnp.float32)
    filt = hyena_filters(L, f_w1, f_b1, f_freq, f_w2, f_b2, f_w3)
    hf = jnp.fft.rfft(filt, n=n, axis=0)
    hf = hf[:, :HY_WIDTH] + jnp.conj(hf[:, HY_WIDTH:])
    y = jnp.fft.irfft(jnp.fft.rfft(z, n=n, axis=1) * hf, n=n, axis=1)[:, :L]
    y = (y + z * f_bias.astype(jnp.float32)).astype(h.dtype) * x0
    return (y * jax.nn.silu(gate)) @ w_out


def axial_rope_angles(L):
    rows = L // GRID_W
    axis_dim = QK_ROPE // 2
    inv = ROPE_THETA ** (-jnp.arange(0, axis_dim, 2, dtype=jnp.float32) / axis_dim)
    row = jnp.repeat(jnp.arange(rows, dtype=jnp.float32), GRID_W)
    col = jnp.tile(jnp.arange(GRID_W, dtype=jnp.float32), rows)
    return row[:, None] * inv, col[:, None] * inv


def rope_1d(x, ang):
    half = x.shape[-1] // 2
    cos = jnp.cos(ang).astype(x.dtype)
    sin = jnp.sin(ang).astype(x.dtype)
    x1, x2 = x[..., :half], x[..., half:]
    return jnp.concatenate([x1 * cos - x2 * sin, x2 * cos + x1 * sin], axis=-1)


def apply_axial_rope(x, ang_r, ang_c):
    half = QK_ROPE // 2
    return jnp.concatenate([rope_1d(x[..., :half], ang_r), rope_1d(x[..., half:], ang_c)], axis=-1)


def mla_project(h, w_in, q_norm, w_qb, kv_norm):
    B, L, _ = h.shape
    proj = h @ w_in
    q_a, kv_a, k_pe, gate = jnp.split(
        proj, [Q_LORA, Q_LORA + KV_LORA, Q_LORA + KV_LORA + QK_ROPE], axis=-1)
    q = (rmsnorm(q_a, q_norm) @ w_qb).reshape(B, L, N_HEADS, QK_NOPE + QK_ROPE)
    ckv = rmsnorm(kv_a, kv_norm)
    return q[..., :QK_NOPE], q[..., QK_NOPE:], ckv, k_pe, gate


def mla_expand(ckv, w_kvb):
    B, L, _ = ckv.shape
    kv = (ckv @ w_kvb).reshape(B, L, N_HEADS, QK_NOPE + V_HEAD)
    return kv[..., :QK_NOPE], kv[..., QK_NOPE:]


def mla_attention(q_nope, q_pe, k_nope, k_pe, v):
    B, Lq = q_nope.shape[:2]
    nb = Lq // Q_BLOCK
    scale = 1.0 / math.sqrt(QK_NOPE + QK_ROPE)

    def to_blocks(t):
        return jnp.moveaxis(t.reshape(B, nb, Q_BLOCK, *t.shape[2:]), 1, 0)

    def block(args):
        qn, qp = args
        s = jnp.einsum('bqhd,bkhd->bhqk', qn, k_nope) + jnp.einsum('bqhr,bkr->bhqk', qp, k_pe)
        p = jax.nn.softmax(s.astype(jnp.float32) * scale, axis=-1).astype(v.dtype)
        return jnp.einsum('bhqk,bkhd->bqhd', p, v)

    o = lax.map(block, (to_blocks(q_nope), to_blocks(q_pe)))
    return jnp.moveaxis(o, 0, 1).reshape(B, Lq, N_HEADS * V_HEAD)


def mla_context(h, w_in, q_norm, w_qb, kv_norm, w_kvb, w_o):
    q_nope, q_pe, ckv, k_pe, gate = mla_project(h, w_in, q_norm, w_qb, kv_norm)
    k_nope, v = mla_expand(ckv, w_kvb)
    o = mla_attention(q_nope, q_pe, k_nope, k_pe, v)
    return (o * jax.nn.silu(gate)) @ w_o, ckv, k_pe


def mla_latent(h, ckv_ctx, kpe_ctx, w_in, q_norm, w_qb, kv_norm, w_kvb, w_o):
    L = h.shape[1]
    q_nope, q_pe, ckv, k_pe, gate = mla_project(h, w_in, q_norm, w_qb, kv_norm)
    ang_r, ang_c = axial_rope_angles(L)
    q_pe = apply_axial_rope(q_pe, ang_r[:, None, :], ang_c[:, None, :])
    k_pe = apply_axial_rope(k_pe, ang_r, ang_c)
    k_nope_l, v_l = mla_expand(ckv, w_kvb)
    k_nope_c, v_c = mla_expand(ckv_ctx, w_kvb)
    k_nope = jnp.concatenate([k_nope_l, k_nope_c], axis=1)
    k_pe_all = jnp.concatenate([k_pe, kpe_ctx], axis=1)
    v = jnp.concatenate([v_l, v_c], axis=1)
    o = mla_attention(q_nope, q_pe, k_nope, k_pe_all, v)
    return (o * jax.nn.silu(gate)) @ w_o


def setup_inputs(seed: int = 0) -> dict:
    key = jax.random.key(seed)
    ks = iter(jax.random.split(key, 40))
    f32 = jnp.float32

    def nrm(shape, s=1.0):
        return jax.random.normal(next(ks), shape, f32) * s

    D = D_MODEL
    return {
        'x_prompt': nrm((BATCH, SEQ, D)),
        'x_sample': nrm((DEC_BATCH, DEC_SEQ, D)),
        'cache_ckv': nrm((DEC_BATCH, N_MLA, PAST_LEN, KV_LORA)),
        'cache_kpe': nrm((DEC_BATCH, N_MLA, PAST_LEN, QK_ROPE)),
        'c': nrm((DEC_BATCH, D)),
        'c_ctx': nrm((D,)),
        'norm_w': 1.0 + nrm((DEPTH, D), 0.02),
        'ada_w': nrm((DEPTH, D, 3 * D), 0.5 * D ** -0.5),
        'ada_b': nrm((DEPTH, 3 * D), 0.02),
        'hy_w_in': nrm((N_HYENA, D, 4 * HY_WIDTH), D ** -0.5),
        'hy_conv_w': nrm((N_HYENA, SHORT_CONV, 3 * HY_WIDTH), 0.5),
        'hy_conv_b': nrm((N_HYENA, 3 * HY_WIDTH), 0.02),
        'hy_f_w1': nrm((N_HYENA, FILTER_EMB, FILTER_HIDDEN), FILTER_EMB ** -0.5),
        'hy_f_b1': nrm((N_HYENA, FILTER_HIDDEN), 0.02),
        'hy_f_freq': 1.0 + nrm((N_HYENA, FILTER_HIDDEN), 0.1),
        'hy_f_w2': nrm((N_HYENA, FILTER_HIDDEN, FILTER_HIDDEN), FILTER_HIDDEN ** -0.5),
        'hy_f_b2': nrm((N_HYENA, FILTER_HIDDEN), 0.02),
        'hy_f_w3': nrm((N_HYENA, FILTER_HIDDEN, 2 * HY_WIDTH), FILTER_HIDDEN ** -0.5),
        'hy_f_bias': nrm((N_HYENA, HY_WIDTH), 0.1),
        'hy_w_out': nrm((N_HYENA, HY_WIDTH, D), HY_WIDTH ** -0.5),
        'mla_w_in': nrm((N_MLA, D, MLA_IN), D ** -0.5),
        'mla_q_norm': 1.0 + nrm((N_MLA, Q_LORA), 0.02),
        'mla_w_qb': nrm((N_MLA, Q_LORA, N_HEADS * (QK_NOPE + QK_ROPE)), Q_LORA ** -0.5),
        'mla_kv_norm': 1.0 + nrm((N_MLA, KV_LORA), 0.02),
        'mla_w_kvb': nrm((N_MLA, KV_LORA, N_HEADS * (QK_NOPE + V_HEAD)), KV_LORA ** -0.5),
        'mla_w_o': nrm((N_MLA, N_HEADS * V_HEAD, D), (N_HEADS * V_HEAD) ** -0.5),
        'final_norm': 1.0 + nrm((D,), 0.02),
    }


def reference(x_prompt, x_sample, cache_ckv, cache_kpe, c, c_ctx, norm_w, ada_w, ada_b,
              hy_w_in, hy_conv_w, hy_conv_b, hy_f_w1, hy_f_b1, hy_f_freq, hy_f_w2, hy_f_b2,
              hy_f_w3, hy_f_bias, hy_w_out, mla_w_in, mla_q_norm, mla_w_qb, mla_kv_norm,
              mla_w_kvb, mla_w_o, final_norm):
    xp, xs = x_prompt, x_sample
    new_ckv, new_kpe = [], []
    for i in range(DEPTH):
        sh_p, sc_p, g_p = adaln(c_ctx[None, :], ada_w[i], ada_b[i])
        sh_s, sc_s, g_s = adaln(c, ada_w[i], ada_b[i])
        hp = rmsnorm(xp, norm_w[i]) * (1.0 + sc_p) + sh_p
        hs = rmsnorm(xs, norm_w[i]) * (1.0 + sc_s) + sh_s
        j = i // N_MIXERS
        if i % N_MIXERS == 0:
            hy = (hy_w_in[j], hy_conv_w[j], hy_conv_b[j], hy_f_w1[j], hy_f_b1[j], hy_f_freq[j],
                  hy_f_w2[j], hy_f_b2[j], hy_f_w3[j], hy_f_bias[j], hy_w_out[j])
            op = hyena_mix(hp, *hy)
            os_ = hyena_mix(hs, *hy)
        else:
            mla = (mla_w_in[j], mla_q_norm[j], mla_w_qb[j], mla_kv_norm[j], mla_w_kvb[j], mla_w_o[j])
            op, ckv, kpe = mla_context(hp, *mla)
            os_ = mla_latent(hs, cache_ckv[:, j], cache_kpe[:, j], *mla)
            new_ckv.append(ckv)
            new_kpe.append(kpe)
        xp = xp + g_p * op
        xs = xs + g_s * os_
    y_prompt = rmsnorm(xp, final_norm)
    y_sample = rmsnorm(xs, final_norm)
    state_ckv = jnp.stack(new_ckv, axis=1)
    state_kpe = jnp.stack(new_kpe, axis=1)
    return (y_prompt, y_sample, state_ckv, state_kpe)
```

```python
import math
from contextlib import ExitStack

import numpy as np
import ml_dtypes

import concourse.bass as bass
import concourse.mybir as mybir
from concourse.bass_utils import run_bass_kernel_spmd

F32 = mybir.dt.float32
BF16 = mybir.dt.bfloat16
I32 = mybir.dt.int32
AF = mybir.ActivationFunctionType
ALU = mybir.AluOpType

NCORES = 8
D = 1024
DEPTH = 4
BATCH, SEQ = 16, 256
DEC_BATCH, DEC_SEQ = 2, 2048
PAST = 512
EPS = 1e-6
NH = 16
Q_LORA, KV_LORA, QK_NOPE, QK_ROPE, V_HEAD = 384, 256, 64, 32, 64
GRID_W = 64
ROPE_THETA = 10000.0
ENGS = ("pe", "act", "dve", "pool", "sp")
GROUP4 = [[0, 1, 2, 3], [4, 5, 6, 7]]
GROUP8 = [[0, 1, 2, 3, 4, 5, 6, 7]]


class _Op:
    __slots__ = ("eng", "fn", "deps", "signals", "val", "dma_sem", "inc", "persist", "idx")

    def __init__(self, eng, fn, deps, dma_sem=None):
        self.eng = eng
        self.fn = fn
        self.deps = deps
        self.signals = False
        self.val = None
        self.dma_sem = dma_sem
        self.inc = 16
        self.persist = False


class _Rec:
    def __init__(self):
        self.call = None

    def __getattr__(self, name):
        def f(*a, **kw):
            self.call = (name, a, kw)
            return self
        return f


class _Reg:
    __slots__ = ("w", "r")

    def __init__(self):
        self.w = None
        self.r = []


class Prog:
    def __init__(self, nc):
        self.nc = nc
        self.ops = {e: [] for e in ENGS}
        self.regs = {}
        self.dma_counts = {}
        self.last = {e: None for e in ENGS}
        self.dma_since_bar = {}
        self.tile_keys = {}
        self.hazards = {}

    def _reg(self, k):
        r = self.regs.get(k)
        if r is None:
            r = self.regs[k] = _Reg()
            tk = k if isinstance(k, str) else k[0]
            self.tile_keys.setdefault(tk, set()).add(k)
        return r

    def tile_ops(self, tkey):
        out = []
        for k in self.tile_keys.get(tkey, ()):
            rg = self.regs[k]
            if rg.w is not None:
                out.append(rg.w)
            out.extend(rg.r)
        return out

    muted = False

    def op(self, eng, fn, reads=(), writes=(), dma_sem=None, inc=16, persist=False):
        if self.muted:
            return None
        deps = []
        seen = set()

        def add(d):
            if d is None or id(d) in seen:
                return
            if eng == "pe" and d.eng == "pe" and d.dma_sem is None:
                return
            seen.add(id(d))
            deps.append(d)

        if self.hazards:
            for k in list(reads) + list(writes):
                hz = self.hazards.get(k if isinstance(k, str) else k[0])
                if hz:
                    for d in hz:
                        add(d)
        for k in reads:
            add(self._reg(k).w)
        for k in writes:
            rg = self._reg(k)
            w = rg.w
            if w is not None and dma_sem is not None and w.dma_sem == dma_sem:
                for d in w.deps:
                    add(d)
            else:
                add(w)
            for d in rg.r:
                add(d)
        rec = _Rec()
        fn(rec)
        call = rec.call
        import sys as _sys
        fr = _sys._getframe(2)
        where = "%s:%d" % (fr.f_code.co_name, fr.f_lineno)

        def _do(e, c=call, where=where):
            try:
                return getattr(e, c[0])(*c[1], **c[2])
            except Exception as ex:
                raise RuntimeError("emit failed for op recorded at %s: %s %s" % (where, c[0], ex)) from ex
        o = _Op(eng, _do, deps, dma_sem)
        self.nops = getattr(self, "nops", 0) + 1
        o.idx = self.nops
        if dma_sem is not None:
            v = self.dma_counts.get(dma_sem, 0) + inc
            self.dma_counts[dma_sem] = v
            o.val = v
            o.inc = inc
            o.persist = persist
            if not persist:
                self.dma_since_bar[dma_sem] = o
        for k in reads:
            self._reg(k).r.append(o)
        for k in writes:
            rg = self._reg(k)
            rg.w = o
            rg.r = []
        self.ops[eng].append(o)
        if dma_sem is None:
            self.last[eng] = o
        return o

    def barrier(self):
        lasts = [o for o in self.last.values() if o is not None]
        dmas = list(self.dma_since_bar.values())
        self.dma_since_bar = {}
        self.hazards = {}
        new = {}
        for e in ENGS:
            deps = [o for o in lasts if o.eng != e or e != "pe"] + dmas
            if e == "pe":
                deps = [o for o in deps if not (o.eng == "pe" and o.dma_sem is None)]
            b = _Op(e, None, deps)
            self.ops[e].append(b)
            new[e] = b

    def emit(self, final_dma_sems=()):
        nc = self.nc
        for e in ENGS:
            for o in self.ops[e]:
                for d in o.deps:
                    d.signals = True
        with ExitStack() as st:
            esem = {e: st.enter_context(nc.semaphore("s_" + e)) for e in ENGS}
            dsem = {k: st.enter_context(nc.semaphore("d_%d" % i))
                    for i, k in enumerate(self.dma_counts)}
            for e in ENGS:
                c = 0
                for o in self.ops[e]:
                    if o.dma_sem is None and o.signals:
                        c += 1
                        o.val = c
            block = st.enter_context(nc.Block())
            engobj = {"pe": "tensor", "act": "scalar", "dve": "vector", "pool": "gpsimd", "sp": "sync"}

            def run(e, eng):
                waited = {}
                for o in self.ops[e]:
                    need = {}
                    for d in o.deps:
                        if d.dma_sem is not None:
                            key = ("d", d.dma_sem)
                            sem = dsem[d.dma_sem]
                        else:
                            key = ("e", d.eng)
                            sem = esem[d.eng]
                        if waited.get(key, 0) >= d.val:
                            continue
                        if key not in need or need[key][1] < d.val:
                            need[key] = (sem, d.val)
                    for key, (sem, v) in need.items():
                        eng.wait_ge(sem, v)
                        waited[key] = v
                    if o.fn is None:
                        continue
                    ins = o.fn(eng)
                    if o.dma_sem is not None:
                        ins.then_inc(dsem[o.dma_sem], o.inc)
                    elif o.signals:
                        ins.then_inc(esem[e], 1)
                if e == "sp":
                    for k in final_dma_sems:
                        if k in dsem:
                            eng.wait_ge(dsem[k], self.dma_counts[k])

            for e in ENGS:
                def mk(e):
                    def f(eng):
                        run(e, eng)
                    return f
                getattr(block, engobj[e])(mk(e))


class T:
    _n = 0

    def __init__(self, ap, name):
        T._n += 1
        self.ap = ap
        self.key = "%s#%d" % (name, T._n)

    def __getitem__(self, idx):
        return self.ap[idx]

    def k(self, *sub):
        return (self.key,) + tuple(sub) if sub else self.key


class Arena:
    def __init__(self, nc, st, words):
        self.t = st.enter_context(nc.sbuf_tensor("arena", [128, words], F32))
        self.words = words
        self.top = 0
        self.stack = []
        self.peak = 0
        self.live = []
        self.dead = []
        self.prog = None

    def alloc(self, name, free_shape, dt=F32):
        n = int(np.prod(free_shape))
        w = n if dt in (F32, I32) else (n + 1) // 2
        w = (w + 7) // 8 * 8
        off = self.top
        self.top += w
        self.peak = max(self.peak, self.top)
        assert self.top <= self.words, "SBUF arena overflow: %s needs %d words, top %d" % (name, w, self.top)
        ap = self.t[:, off:off + w]
        if dt != F32:
            ap = ap.bitcast(dt)
        ap = ap[:, 0:n]
        if len(free_shape) == 2:
            ap = ap.rearrange("p (a b) -> p a b", a=free_shape[0])
        elif len(free_shape) == 3:
            ap = ap.rearrange("p (a b c) -> p a b c", a=free_shape[0], b=free_shape[1])
        elif len(free_shape) == 4:
            ap = ap.rearrange("p (a b c d) -> p a b c d", a=free_shape[0], b=free_shape[1], c=free_shape[2])
        t = T(ap, name)
        if self.prog is not None:
            hz = []
            seen = set()
            for (s0, e0, old) in self.dead:
                if s0 < off + w and off < e0:
                    for o in self.prog.tile_ops(old.key):
                        if id(o) not in seen:
                            seen.add(id(o))
                            hz.append(o)
            if hz:
                best = {}
                for o in hz:
                    kk = ("d", o.dma_sem) if o.dma_sem is not None else ("e", o.eng)
                    if kk not in best or best[kk].idx < o.idx:
                        best[kk] = o
                self.prog.hazards[t.key] = list(best.values())
        self.live.append((off, off + w, t))
        return t

    def push(self):
        self.stack.append(self.top)

    def pop(self):
        self.top = self.stack.pop()
        keep = []
        for it in self.live:
            if it[0] >= self.top:
                self.dead.append(it)
            else:
                keep.append(it)
        self.live = keep

    def clear_dead(self):
        self.dead = []


def _bf(a):
    return np.ascontiguousarray(np.asarray(a, np.float32).astype(ml_dtypes.bfloat16))


def _f32(a):
    return np.ascontiguousarray(np.asarray(a, np.float32))


def _dft(L):
    n = 2 * L
    t = np.arange(L, dtype=np.float64)[:, None]
    kre = np.arange(0, L + 1, dtype=np.float64)[None, :]
    kim = np.arange(1, L, dtype=np.float64)[None, :]
    return np.concatenate([np.cos(2 * np.pi * kre * t / n), -np.sin(2 * np.pi * kim * t / n)], axis=1)


def _zemb(L):
    f32 = np.float32
    t = np.linspace(0.0, 1.0, L, dtype=f32)[:, None]
    w = (f32(2.0 * math.pi / L) * np.arange(L, dtype=f32))[:, None]
    bands = np.linspace(1e-4, 15, 16, dtype=f32)[None, :]
    z = np.concatenate([t, np.cos(bands * w), -np.sin(bands * w)], axis=-1)
    return z.T


def _chunk(w):
    K = w.shape[0]
    return w.reshape(K // 128, 128, *w.shape[1:]).swapaxes(0, 1)


_CONST = {}


def _consts():
    if _CONST:
        return _CONST
    Ls, Lp = DEC_SEQ, SEQ
    F = _dft(Ls)
    _CONST["Fs"] = _bf(F.reshape(16, 128, 32, 128).transpose(2, 1, 0, 3))
    G = F.T
    _CONST["Gs"] = _bf(G.reshape(4, 8, 128, 4, 512).transpose(3, 0, 2, 1, 4))
    Fq = _dft(Lp)
    _CONST["Fp"] = _bf(Fq.reshape(2, 128, 512).transpose(1, 0, 2))
    _CONST["Gp"] = _bf(Fq.T.reshape(4, 128, 256).transpose(1, 0, 2))
    _CONST["zemb_s"] = _bf(_zemb(Ls))
    _CONST["zemb_p"] = _bf(_zemb(Lp))
    _CONST["tv_s"] = _f32(np.linspace(0.0, 1.0, Ls, dtype=np.float32).reshape(16, 128).T)
    _CONST["tv_p"] = _f32(np.linspace(0.0, 1.0, Lp, dtype=np.float32).reshape(2, 128).T)
    maxd = math.log(1e-2) / 0.3
    mind = math.log(1e-2) / 1.5
    _CONST["ndelta"] = -np.abs(np.linspace(mind, maxd, D, dtype=np.float32))
    ident = np.eye(128, dtype=np.float32)
    _CONST["ident"] = _bf(ident)
    _CONST["ones"] = _bf(np.ones((128, 128), np.float32))
    _CONST["swap"] = _bf(np.roll(ident, 64, axis=1))
    rows = Ls // GRID_W
    axis_dim = QK_ROPE // 2
    inv = (ROPE_THETA ** (-np.arange(0, axis_dim, 2, dtype=np.float32) / axis_dim)).astype(np.float32)
    row = np.repeat(np.arange(rows, dtype=np.float32), GRID_W)
    col = np.tile(np.arange(GRID_W, dtype=np.float32), rows)
    ar = (row[:, None] * inv).astype(np.float32)
    ac = (col[:, None] * inv).astype(np.float32)
    cs = np.zeros((128, 2, Ls), np.float32)
    cs[64:72, 0] = np.cos(ar).T
    cs[72:80, 0] = np.cos(ar).T
    cs[80:88, 0] = np.cos(ac).T
    cs[88:96, 0] = np.cos(ac).T
    cs[64:72, 1] = -np.sin(ar).T
    cs[72:80, 1] = np.sin(ar).T
    cs[80:88, 1] = -np.sin(ac).T
    cs[88:96, 1] = np.sin(ac).T
    _CONST["rope"] = cs
    _CONST["ropesw"] = np.concatenate([np.arange(8, 16), np.arange(0, 8), np.arange(24, 32), np.arange(16, 24)])
    return _CONST


class Grp:
    def __init__(self, name, NT, nseq, ncb, nh):
        self.name = name
        self.NT = NT
        self.nseq = nseq
        self.L = NT // nseq
        self.TB = max(1, NT // 512)
        self.NB = self.L // 128
        self.ncb = ncb
        self.C = ncb * 128
        self.nh = nh
        self.SBW = min(512, self.C)
        self.nsb = self.C // self.SBW
        self.NFB = 2 * self.L // 128
        self.samp = name == "s"


GP = Grp("p", 512, 2, 8, 16)
GS = Grp("s", 2048, 1, 2, 4)


class Builder:
    def __init__(self, depth=DEPTH):
        self.depth = depth
        self.nc = bass.Bass("TRN2", target_bir_lowering=False)
        self.P = Prog(self.nc)
        self.dr = {}
        self._bank = 0
        self._b4 = 0
        self.outs = []

    def din(self, name, shape, dt=F32):
        self.dr[name] = self.nc.dram_tensor(name, list(shape), dt, kind="ExternalInput").ap()
        return self.dr[name]

    def dout(self, name, shape, dt=F32):
        self.dr[name] = self.nc.dram_tensor(name, list(shape), dt, kind="ExternalOutput").ap()
        return self.dr[name]

    def dint(self, name, shape, dt=F32):
        self.dr[name] = self.nc.dram_tensor(name, list(shape), dt).ap()
        return self.dr[name]

    def bank(self):
        b = self._bank
        self._bank = (self._bank + 1) % 8
        return b

    def bank4(self):
        b = self._b4 * 4
        self._b4 ^= 1
        return b

    def psb(self, b, rows=slice(0, 128), n=512, off=0):
        return self.ps[rows, b * 512 + off:b * 512 + off + n]

    def pk(self, b):
        return ("ps", b)

    def load(self, q, dst, src, writes, sem, reads=(), persist=False):
        return self.P.op(q, lambda e: e.dma_start(out=dst, in_=src), reads=reads, writes=writes,
                         dma_sem=sem, persist=persist)

    def mm(self, out, lhsT, rhs, start, stop, reads, writes):
        self.P.op("pe", lambda e: e.matmul(out, lhsT, rhs, start=start, stop=stop), reads=reads, writes=writes)

    stop_at = None
    _cp = 0
    use_barriers = False

    def cp(self, name=""):
        import os
        self._cp += 1
        if self.stop_at is not None and self._cp >= self.stop_at:
            self.P.muted = True
        mr = os.environ.get("K_DBG_MUTE")
        if mr:
            a, b = [int(v) for v in mr.split(",")]
            self.P.muted = a <= self._cp < b

    def scope(self):
        b = self

        class _S:
            def __enter__(s):
                b.A.push()

            def __exit__(s, *a):
                if b.use_barriers:
                    b.P.barrier()
                    b.A.pop()
                    b.A.clear_dead()
                else:
                    b.A.pop()
        return _S()

    def declare(self):
        d = self.din
        d("xp", [128, 8, 512]); d("xs", [128, 8, 2048])
        d("cond", [128, 8, 3]); d("ada_w", [4, 128, 8, 768]); d("ada_b", [128, 4, 6])
        d("sel", [128, 2]); d("normw", [128, 4, 8]); d("fnorm", [128, 8])
        d("ident", [128, 128], BF16); d("ones", [128, 128], BF16); d("swap", [128, 128], BF16)
        for g in (GP, GS):
            n = g.name
            d("hy_win_" + n, [2, g.ncb, 128, 8, 512])
            d("hy_conv_" + n, [2, 128, g.ncb, 3, 4])
            d("hy_fw3_" + n, [2, 64, 2 * g.C])
            d("hy_fbias_" + n, [2, 128, g.C])
            d("ndelta_" + n, [128, g.C])
            d("zemb_" + n, [33, g.L], BF16)
            d("tv_" + n, [128, g.NB])
            d("mla_wg_" + n, [2, 128, 8, g.nh * 64])
            d("mla_wk_" + n, [2, 128, 2, g.nh, 64])
            d("mla_wv_" + n, [2, 128, 2, g.nh * 64])
        d("mla_wq_p", [2, 128, 3, 16, 96]); d("mla_wq_s", [2, 128, 3, 4, 2, 96])
        d("hy_fw1", [2, 33, 64]); d("hy_fvec", [2, 64, 4]); d("hy_fw2", [2, 64, 64])
        d("hy_wout", [2, 128, 8, 1024])
        d("Fs", [32, 128, 16, 128], BF16); d("Gs", [4, 4, 128, 8, 512], BF16)
        d("Fp", [128, 2, 512], BF16); d("Gp", [128, 4, 256], BF16)
        d("rope", [128, 2, 2048])
        d("mla_wa", [2, 128, 8, 832]); d("mla_qn", [128, 2, 3]); d("mla_kvn", [128, 2, 2])
        d("mla_wo", [2, 128, 8, 1024])
        d("cckv", [2, 128, 2, 512]); d("ckpe", [2, 128, 512])
        self.dout("yp", [128, 8, 512]); self.dout("ys", [128, 8, 2048])
        self.dout("ckv_out", [2, 128, 2, 512]); self.dout("kpe_out", [2, 32, 512])
        self.dint("cc_ada_in", [128, 72]); self.dint("cc_ada_out", [512, 72])
        for i in range(DEPTH):
            self.dint("cc_in%d" % i, [256, 2048], BF16)
            self.dint("cc_out%d" % i, [1024, 2048], BF16)

    def build(self):
        nc, P = self.nc, self.P
        self.declare()
        dr = self.dr
        with ExitStack() as st:
            self.A = A = Arena(nc, st, 51200)
            A.prog = P
            self.ps = st.enter_context(nc.psum_tensor("ps", [128, 4096], F32))
            self.xg_ = {}
            self.x = {"p": A.alloc("xp", [8, 512]), "s": A.alloc("xs", [8, 2048])}
            self.ident = A.alloc("ident", [128], BF16)
            self.ones = A.alloc("ones", [128], BF16)
            self.swap = A.alloc("swap", [128], BF16)
            self.modt = A.alloc("modt", [24, 4, 3])
            self.mods = A.alloc("mods", [24, 4])
            self.modA = A.alloc("modA", [2, 4, 8])
            self.normw = A.alloc("normw", [4, 8])
            self.fnorm = A.alloc("fnorm", [8])
            self.cst = A.alloc("cst", [8])
            self.sel = A.alloc("sel", [2])
            self.Fp = A.alloc("Fp", [2, 512], BF16)
            self.Gp = A.alloc("Gp", [4, 256], BF16)
            self.qn_w = A.alloc("qn_w", [2, 3])
            self.kvn_w = A.alloc("kvn_w", [2, 2])
            self.tv = {"p": A.alloc("tv_p", [2]), "s": A.alloc("tv_s", [16])}
            ld = self.load
            ld("sp", self.x["p"][:], dr["xp"], [self.x["p"].k(k, 0) for k in range(8)], "ld_xp")
            ld("sp", self.x["s"][:], dr["xs"], [self.x["s"].k(k, tb) for k in range(8) for tb in range(4)], "ld_xs")
            for t, n in ((self.ident, "ident"), (self.ones, "ones"), (self.swap, "swap"), (self.normw, "normw"),
                         (self.fnorm, "fnorm"), (self.sel, "sel"), (self.Fp, "Fp"), (self.Gp, "Gp"),
                         (self.qn_w, "mla_qn"), (self.kvn_w, "mla_kvn"), (self.tv["p"], "tv_p"), (self.tv["s"], "tv_s")):
                ld("sp", t[:], dr[n], [t.k()], "ld_" + n)
            P.op("dve", lambda e: e.memset(self.cst[:, 0:1], EPS), writes=[self.cst.k()])
            P.op("dve", lambda e: e.memset(self.cst[:, 1:2], 0.0), reads=[self.cst.k()], writes=[self.cst.k()])
            self.adaln()
            for i in range(self.depth):
                j = i // 2
                fn = self.hyena if i % 2 == 0 else self.mla
                fn(GS, i, j, part=1)
                self.A.push()
                self._wout = A.alloc("wout", [8, 1024], BF16)
                self.load("pool", self._wout[:], dr["hy_wout" if i % 2 == 0 else "mla_wo"][j], [self._wout.k()], "ld_wout")
                fn(GP, i, j, part="pre")
                fn(GS, i, j, part=2)
                fn(GP, i, j, part="post")
                self.A.pop()
            self.P.muted = False
            self.final(GP, "yp")
            self.final(GS, "ys")
            P.emit(final_dma_sems=self.outs)
        return nc

    def adaln(self):
        P, A, dr = self.P, self.A, self.dr
        with self.scope():
            cnd = A.alloc("cond", [8, 3])
            cb = A.alloc("condb", [8, 3], BF16)
            adb = A.alloc("adb", [4, 6])
            res = A.alloc("adares", [6, 4, 3])
            W = [A.alloc("adaW%d" % i, [8, 768], BF16) for i in range(2)]
            self.load("sp", cnd[:], dr["cond"], [cnd.k()], "ld_cond")
            self.load("sp", adb[:], dr["ada_b"], [adb.k()], "ld_adb")
            P.op("act", lambda e: e.activation(cb[:], cnd[:], AF.Silu), reads=[cnd.k()], writes=[cb.k()])
            b = self.bank()
            for i in range(4):
                Wt = W[i % 2]
                self.load("pool", Wt[:], dr["ada_w"][i], [Wt.k()], "ld_adaW%d" % (i % 2))
                for m in range(6):
                    c0 = (i * 6 + m) * 3
                    for k in range(8):
                        self.mm(self.psb(b, n=3, off=c0), Wt[:, k, 128 * m:128 * m + 128], cb[:, k, :],
                                k == 0, k == 7, [Wt.k(), cb.k()], [self.pk(b)])
            for i in range(4):
                for m in range(6):
                    c0 = (i * 6 + m) * 3
                    P.op("dve", lambda e, i=i, m=m, c0=c0: e.tensor_scalar(
                        res[:, m, i, :], self.psb(b, n=3, off=c0), adb[:, i, m:m + 1], None, ALU.add),
                        reads=[self.pk(b), adb.k()], writes=[res.k(i, m)])
            rk = [res.k(i, m) for i in range(4) for m in range(6)]
            self.load("sp", dr["cc_ada_in"], res[:].rearrange("p a b c -> p (a b c)"), ["cc_ada_in"], "st_ada", reads=rk)
            P.op("pool", lambda e: e.collective_compute("AllGather", ALU.bypass, replica_groups=GROUP4,
                                                        ins=[dr["cc_ada_in"]], outs=[dr["cc_ada_out"]]),
                 reads=["cc_ada_in"], writes=["cc_ada_out"], dma_sem="cc_ada", inc=1)
            self.load("sp", self.modt[:].rearrange("p (j m) i c -> p j (m i c)", j=4),
                      dr["cc_ada_out"].rearrange("(j p) f -> p j f", p=128), [self.modt.k()], "ld_modt",
                      reads=["cc_ada_out"])
            mt, ms = self.modt, self.mods
            P.op("dve", lambda e: e.tensor_scalar(ms[:], mt[:, :, :, 1], self.sel[:, 0:1], None, ALU.mult),
                 reads=[mt.k(), self.sel.k()], writes=[ms.k()])
            P.op("dve", lambda e: e.scalar_tensor_tensor(ms[:], mt[:, :, :, 2], self.sel[:, 1:2], ms[:], ALU.mult, ALU.add),
                 reads=[mt.k(), self.sel.k(), ms.k()], writes=[ms.k()])
            mA = self.modA
            for gi in range(2):
                for i in range(4):
                    src = mt[:, 8:16, i, 0] if gi == 0 else ms[:, 8:16, i]
                    P.op("dve", lambda e, gi=gi, i=i, src=src: e.scalar_tensor_tensor(
                        mA[:, gi, i, :], src, 1.0, self.normw[:, i, :], ALU.add, ALU.mult),
                        reads=[mt.k(), ms.k(), self.normw.k(), mA.k()], writes=[mA.k()])

    def mod(self, g, i, part, k):
        c = part * 8 + k
        if g.samp:
            return self.mods[:, c, i:i + 1]
        return self.modt[:, c, i, 0:1]

    def modkeys(self):
        return [self.modt.k(), self.mods.k(), self.modA.k()]

    def norm_scratch(self):
        A = self.A
        return {"sq": A.alloc("sq", [8, 512], BF16), "rs": [A.alloc("rstd%d" % q, [512]) for q in range(2)],
                "tmp": [A.alloc("nt%d" % q, [512]) for q in range(2)], "n": 0}

    def norm_tb(self, g, i, tb, out_fn, wk, sc):
        P = self.P
        x = self.x[g.name]
        gi = 1 if g.samp else 0
        ts = slice(tb * 512, (tb + 1) * 512)
        s_, r_ = sc["sq"], sc["rs"][tb % 2]
        P.op("act", lambda e: e.activation(s_[:, 0:4, :], x[:, 0:4, ts], AF.Square),
             reads=[x.k(k, tb) for k in range(4)], writes=[s_.k(0)])
        P.op("dve", lambda e: e.tensor_tensor(s_[:, 4:8, :], x[:, 4:8, ts], x[:, 4:8, ts], ALU.mult),
             reads=[x.k(k, tb) for k in range(4, 8)], writes=[s_.k(1)])
        b = self.bank()
        for k in range(8):
            self.mm(self.psb(b), self.ones[:], s_[:, k, :], k == 0, k == 7, [self.ones.k(), s_.k(k // 4)], [self.pk(b)])
        P.op("act", lambda e: e.activation(r_[:], self.psb(b), AF.Ln, bias=self.cst[:, 0:1], scale=1.0 / D),
             reads=[self.pk(b), self.cst.k()], writes=[r_.k()])
        P.op("act", lambda e: e.activation(r_[:], r_[:], AF.Exp, scale=-0.5), reads=[r_.k()], writes=[r_.k()])
        for k in range(8):
            t_ = sc["tmp"][sc["n"] % 2]
            sc["n"] += 1
            P.op("dve", lambda e, k=k, t_=t_: e.tensor_tensor(t_[:], x[:, k, ts], r_[:], ALU.mult),
                 reads=[x.k(k, tb), r_.k()], writes=[t_.k()])
            P.op("act", lambda e, k=k, t_=t_: e.activation(out_fn(k), t_[:], AF.Identity,
                                                          bias=self.mod(g, i, 0, k), scale=self.modA[:, gi, i, k:k + 1]),
                 reads=[t_.k()] + self.modkeys(), writes=wk(k))

    def norm_mod(self, g, i, h):
        def body():
            sc = self.norm_scratch()
            for tb in range(g.TB):
                self.norm_tb(g, i, tb, lambda k, tb=tb: h[:, k, tb * 512:(tb + 1) * 512],
                             lambda k, tb=tb: [h.k(k, tb)], sc)
        if g.samp:
            with self.scope():
                body()
        else:
            body()

    def resid_update(self, g, i, yga, w, TBs):
        P = self.P
        x = self.x[g.name]
        for tb in range(TBs):
            for m in range(8):
                b = self.bank()
                for k in range(8):
                    self.mm(self.psb(b), w[:, k, 128 * m:128 * m + 128], yga[:, k, tb * 512:(tb + 1) * 512],
                            k == 0, k == 7, [w.k(), yga.k(k, tb)], [self.pk(b)])
                P.op("dve", lambda e, b=b, m=m, tb=tb: e.scalar_tensor_tensor(
                    x[:, m, tb * 512:(tb + 1) * 512], self.psb(b), self.mod(g, i, 2, m),
                    x[:, m, tb * 512:(tb + 1) * 512], ALU.mult, ALU.add),
                    reads=[self.pk(b), x.k(m, tb)] + self.modkeys(), writes=[x.k(m, tb)])

    def gather_and_project(self, g, i, j, yg, wname, part):
        P, A, dr = self.P, self.A, self.dr
        if part == 1:
            self.load("sp", dr["cc_in%d" % i].rearrange("(c p) t -> p c t", p=128), yg[:],
                      ["cc_in%d" % i], "st_cc%d" % i, reads=[yg.k(c, tb) for c in range(2) for tb in range(4)])
            P.op("pool", lambda e: e.collective_compute("AllGather", ALU.bypass, replica_groups=GROUP4,
                                                        ins=[dr["cc_in%d" % i]], outs=[dr["cc_out%d" % i]]),
                 reads=["cc_in%d" % i], writes=["cc_out%d" % i], dma_sem="cc%d" % i, inc=1, persist=True)
            return
        with self.scope():
            yga = A.alloc("yga", [8, 2048], BF16)
            w = self._wout
            src = dr["cc_out%d" % i].rearrange("(k p) t -> p k t", p=128)
            for tb in range(4):
                self.load("sp", yga[:, :, tb * 512:(tb + 1) * 512], src[:, :, tb * 512:(tb + 1) * 512],
                          [yga.k(k, tb) for k in range(8)], "ld_yga%d" % tb, reads=["cc_out%d" % i])
            import os
            if os.environ.get("K_DBG_P2") == "loads":
                return
            self.resid_update(g, i, yga, w, 4)

    def sin_layer(self, ps_ap, rows, n, bvec, fvec, out_ap, reads, writes, scr):
        P = self.P
        t1, ki, kf = scr
        P.op("dve", lambda e: e.tensor_scalar(t1[rows, 0:n], ps_ap, bvec, fvec, ALU.add, ALU.mult),
             reads=reads, writes=[t1.k()])
        P.op("dve", lambda e: e.tensor_copy(ki[rows, 0:n], t1[rows, 0:n]), reads=[t1.k()], writes=[ki.k()])
        P.op("dve", lambda e: e.tensor_copy(kf[rows, 0:n], ki[rows, 0:n]), reads=[ki.k()], writes=[kf.k()])
        P.op("dve", lambda e: e.tensor_tensor(t1[rows, 0:n], t1[rows, 0:n], kf[rows, 0:n], ALU.subtract),
             reads=[t1.k(), kf.k()], writes=[t1.k()])
        P.op("act", lambda e: e.activation(out_ap, t1[rows, 0:n], AF.Sin, scale=2 * math.pi * (1 - 1e-6)),
             reads=[t1.k()], writes=writes)

    def hyena(self, g, i, j, part):
        P, A, dr = self.P, self.A, self.dr
        n = g.name
        if part == 2:
            self.gather_and_project(g, i, j, None, "hy_wout", 2)
            return
        NT, L, NB, TB, C, SBW, NFB = g.NT, g.L, g.NB, g.TB, g.C, g.SBW, g.NFB
        nseq = g.nseq
        HP = NFB // 2
        if part == "post":
            yg = self._carry
            with self.scope():
                w = self._wout
                ygk = _KeyAll(yg, [(cb, s) for cb in range(g.ncb) for s in range(nseq)])
                self.resid_update(g, i, ygk, w, 1)
            self.cp("Wp")
            if self.use_barriers:
                self.P.barrier()
            self.A.pop()
            return
        if not g.samp:
            self.A.push()
            self._carry = A.alloc("yg", [g.ncb, NT], BF16)
        with self.scope():
            xg = A.alloc("xg", [g.ncb, NT])
            zT = A.alloc("zT", [nseq * NB, C], BF16)
            with self.scope():
                h = A.alloc("h", [8, NT], BF16)
                Wt = [A.alloc("win%d" % q, [8, 512], BF16) for q in range(2)]
                cw = A.alloc("convw", [g.ncb, 3, 4])
                U = [A.alloc("u%d" % q, [NT]) for q in range(2)]
                zc = A.alloc("zc", [NT], BF16)
                self.load("sp", cw[:], dr["hy_conv_" + n][j], [cw.k()], "ld_convw")
                for cb in range(min(2, g.ncb)):
                    self.load("pool", Wt[cb][:], dr["hy_win_" + n][j, cb], [Wt[cb].k()], "ld_win%d" % cb)
                self.norm_mod(g, i, h)
                for cb in range(g.ncb):
                    W = Wt[cb % 2]
                    if cb >= 2:
                        self.load("pool", W[:], dr["hy_win_" + n][j, cb], [W.k()], "ld_win%d" % (cb % 2))

                    def proj(gi):
                        b0 = self.bank4() if TB == 4 else self.bank()
                        for tb in range(TB):
                            for k in range(8):
                                self.mm(self.psb(b0 + tb), W[:, k, 128 * gi:128 * gi + 128],
                                        h[:, k, tb * 512:(tb + 1) * 512], k == 0, k == 7,
                                        [W.k(), h.k(k, tb)], [self.pk(b0 + tb)])
                        return b0

                    def conv(gi, b0, Ut):
                        pk = [self.pk(b0 + tb) for tb in range(TB)]
                        Pf = self.ps[:, b0 * 512:b0 * 512 + NT]
                        c_ = lambda q: cw[:, cb, gi, q:q + 1]
                        P.op("act", lambda e: e.activation(Ut[:], Pf, AF.Identity, bias=c_(3), scale=c_(1)),
                             reads=pk + [cw.k()], writes=[Ut.k()])
                        Pv = Pf.rearrange("p (s l) -> p s l", s=nseq)
                        Uv = Ut[:].rearrange("p (s l) -> p s l", s=nseq)
                        P.op("dve", lambda e: e.scalar_tensor_tensor(Uv[:, :, 1:L], Pv[:, :, 0:L - 1], c_(0), Uv[:, :, 1:L],
                                                                     ALU.mult, ALU.add),
                             reads=pk + [cw.k(), Ut.k()], writes=[Ut.k()])
                        P.op("dve", lambda e: e.scalar_tensor_tensor(Uv[:, :, 0:L - 1], Pv[:, :, 1:L], c_(2), Uv[:, :, 0:L - 1],
                                                                     ALU.mult, ALU.add),
                             reads=pk + [cw.k(), Ut.k()], writes=[Ut.k()])

                    bv = proj(2)
                    bx1 = proj(1)
                    conv(2, bv, U[0])
                    conv(1, bx1, U[1])
                    bx0 = proj(0)
                    P.op("dve", lambda e: e.tensor_tensor(zc[:], U[0][:], U[1][:], ALU.mult),
                         reads=[U[0].k(), U[1].k()], writes=[zc.k()])
                    bg = proj(3)
                    conv(0, bx0, U[0])
                    P.op("act", lambda e, bg=bg: e.activation(U[1][:], self.ps[:, bg * 512:bg * 512 + NT], AF.Silu),
                         reads=[self.pk(bg + tb) for tb in range(TB)], writes=[U[1].k()])
                    P.op("dve", lambda e, cb=cb: e.tensor_tensor(xg[:, cb, :], U[0][:], U[1][:], ALU.mult),
                         reads=[U[0].k(), U[1].k()], writes=[xg.k(cb)])
                    nblk = NT // 128
                    for b8 in range(0, nblk, 8):
                        nb_ = min(8, nblk - b8)
                        bt = self.bank()
                        pt = self.psb(bt).bitcast(BF16)
                        for q in range(nb_):
                            P.op("pe", lambda e, q=q, b8=b8, pt=pt: e.transpose(
                                pt[:, q * 128:(q + 1) * 128], zc[:, (b8 + q) * 128:(b8 + q + 1) * 128], self.ident[:]),
                                reads=[zc.k(), self.ident.k()], writes=[self.pk(bt)])
                        P.op("act", lambda e, b8=b8, nb_=nb_, pt=pt, cb=cb: e.activation(
                            zT[:, b8:b8 + nb_, cb * 128:(cb + 1) * 128],
                            pt[:, 0:nb_ * 128].rearrange("p (a b) -> p a b", a=nb_), AF.Copy),
                            reads=[self.pk(bt)], writes=[zT.k(cb, b8)])
            zTk = [zT.k(cb, b8) for cb in range(g.ncb) for b8 in range(0, NT // 128, 8)]
            self.cp("A")
            with self.scope():
                FC = 2 * C
                filtT = A.alloc("filtT", [NB, FC], BF16)
                rn2 = A.alloc("rn2", [FC])
                bias2 = A.alloc("bias2", [C])
                Y = A.alloc("Y", [NFB, nseq, C], BF16)
                yg = A.alloc("yg", [g.ncb, NT], BF16) if g.samp else self._carry
                with self.scope():
                    zemb = A.alloc("zemb", [L], BF16)
                    w1 = A.alloc("fw1", [64], BF16)
                    w2 = A.alloc("fw2", [64], BF16)
                    w3 = A.alloc("fw3", [FC], BF16)
                    fv = A.alloc("fvec", [4])
                    nd = A.alloc("ndelta", [C])
                    hd1 = A.alloc("hd1", [L], BF16)
                    hd2 = A.alloc("hd2", [L], BF16)
                    scr = (A.alloc("sn_t", [512]), A.alloc("sn_i", [512], I32), A.alloc("sn_f", [512]))
                    dec = [A.alloc("dec%d" % q, [C]) for q in range(2)]
                    absb = [A.alloc("absb%d" % q, [512], BF16) for q in range(3)]
                    self.load("sp", zemb[0:33], dr["zemb_" + n], [zemb.k()], "ld_zemb")
                    self.load("pool", w1[0:33], dr["hy_fw1"][j], [w1.k()], "ld_fw1")
                    self.load("pool", w2[0:64], dr["hy_fw2"][j], [w2.k()], "ld_fw2")
                    self.load("pool", w3[0:64], dr["hy_fw3_" + n][j], [w3.k()], "ld_fw3")
                    self.load("sp", fv[0:64], dr["hy_fvec"][j], [fv.k()], "ld_fvec")
                    self.load("sp", nd[:], dr["ndelta_" + n], [nd.k()], "ld_nd")
                    self.load("sp", bias2[:], dr["hy_fbias_" + n][j], [bias2.k()], "ld_fbias")
                    P.op("dve", lambda e: e.tensor_scalar(fv[0:64, 3:4], fv[0:64, 1:2], 1.0 / (2 * math.pi), None, ALU.mult),
                         reads=[fv.k()], writes=[fv.k()])
                    P.op("dve", lambda e: e.tensor_scalar(bias2[:], bias2[:], 2.0 / (2 * L), None, ALU.mult),
                         reads=[bias2.k()], writes=[bias2.k()])
                    R64 = slice(0, 64)
                    for tq in range(0, L, 512):
                        nn = min(512, L - tq)
                        b = self.bank()
                        self.mm(self.psb(b, R64, nn), w1[0:33, :], zemb[0:33, tq:tq + nn], True, True,
                                [w1.k(), zemb.k()], [self.pk(b)])
                        self.sin_layer(self.psb(b, R64, nn), R64, nn, fv[0:64, 0:1], fv[0:64, 3:4], hd1[0:64, tq:tq + nn],
                                       [self.pk(b), fv.k()], [hd1.k(tq)], scr)
                        b = self.bank()
                        self.mm(self.psb(b, R64, nn), w2[0:64, :], hd1[0:64, tq:tq + nn], True, True,
                                [w2.k(), hd1.k(tq)], [self.pk(b)])
                        self.sin_layer(self.psb(b, R64, nn), R64, nn, fv[0:64, 2:3], fv[0:64, 3:4], hd2[0:64, tq:tq + nn],
                                       [self.pk(b), fv.k()], [hd2.k(tq)], scr)
                    SW = min(512, C)
                    nseg = FC // SW
                    nbank = [self.bank() for _ in range(nseg)]
                    pend = []
                    na = 0

                    def nsum(cq_, blk_, ab_):
                        self.mm(self.psb(nbank[cq_], n=SW), self.ones[:], ab_[:, 0:SW], blk_ == 0, blk_ == NB - 1,
                                [self.ones.k(), ab_.k()], [self.pk(nbank[cq_])])
                    for blk in range(NB):
                        dc = dec[blk % 2]
                        P.op("act", lambda e, dc=dc, blk=blk: e.activation(dc[:], nd[:], AF.Exp, scale=self.tv[n][:, blk:blk + 1]),
                             reads=[nd.k(), self.tv[n].k()], writes=[dc.k()])
                        for cq in range(nseg):
                            b = self.bank()
                            while b in nbank:
                                b = self.bank()
                            self.mm(self.psb(b, n=SW), hd2[0:64, blk * 128:(blk + 1) * 128], w3[0:64, cq * SW:(cq + 1) * SW],
                                    True, True, [hd2.k((blk * 128) // 512 * 512), w3.k()], [self.pk(b)])
                            dcol = (cq * SW) % C
                            P.op("dve", lambda e, b=b, blk=blk, cq=cq, dc=dc, dcol=dcol: e.tensor_tensor(
                                filtT[:, blk, cq * SW:(cq + 1) * SW], self.psb(b, n=SW), dc[:, dcol:dcol + SW], ALU.mult),
                                reads=[self.pk(b), dc.k()], writes=[filtT.k(blk, cq)])
                            ab = absb[na % 3]
                            na += 1
                            P.op("act", lambda e, ab=ab, blk=blk, cq=cq: e.activation(
                                ab[:, 0:SW], filtT[:, blk, cq * SW:(cq + 1) * SW], AF.Abs),
                                reads=[filtT.k(blk, cq)], writes=[ab.k()])
                            pend.append((cq, blk, ab))
                            if len(pend) > 2:
                                nsum(*pend.pop(0))
                    for pp in pend:
                        nsum(*pp)
                    for cq in range(nseg):
                        P.op("dve", lambda e, cq=cq: e.reciprocal(rn2[:, cq * SW:(cq + 1) * SW], self.psb(nbank[cq], n=SW)),
                             reads=[self.pk(nbank[cq])], writes=[rn2.k(cq)])
                        P.op("dve", lambda e, cq=cq: e.tensor_scalar(rn2[:, cq * SW:(cq + 1) * SW], rn2[:, cq * SW:(cq + 1) * SW],
                                                                    2.0 / (2 * L), None, ALU.mult),
                             reads=[rn2.k(cq)], writes=[rn2.k(cq)])
                self.cp("F")
                SW = min(512, C)
                nseg = FC // SW
                rnk = [rn2.k(cq) for cq in range(nseg)]
                ftk = lambda tb: [filtT.k(tb, q) for q in range(nseg)]
                with self.scope():
                    if g.samp:
                        Ft = [A.alloc("Ft%d" % q, [16, 128], BF16) for q in range(4)]
                    Hr = [A.alloc("Hr%d" % q, [SBW]) for q in range(2)]
                    Hb = [A.alloc("Hbt%d" % q, [SBW]) for q in range(2)]
                    Zs = [[A.alloc("Zs%d_%d" % (q, s), [SBW]) for s in range(nseq)] for q in range(2)]
                    tt = [A.alloc("yt%d" % q, [SBW]) for q in range(4)]
                    fx = A.alloc("fx", [SBW])
                    nf = 0
                    for sb in range(g.nsb):
                        c0 = sb * SBW
                        for bp in range(HP):
                            for half in range(2):
                                fb = bp + half * HP
                                if g.samp:
                                    F_ = Ft[nf % 4]
                                    nf += 1
                                    self.load("sp", F_[:], dr["Fs"][fb], [F_.k()], "ld_F%d" % ((nf - 1) % 4))
                                    Fap = lambda tb, F_=F_: F_[:, tb, :]
                                    Fk = F_.k()
                                else:
                                    Fap = lambda tb, fb=fb: self.Fp[:, tb, fb * 128:(fb + 1) * 128]
                                    Fk = self.Fp.k()
                                bHf, bHb = self.bank(), self.bank()
                                bZ = [self.bank() for _ in range(nseq)]
                                for tb in range(NB):
                                    st_, sp_ = tb == 0, tb == NB - 1
                                    self.mm(self.psb(bHf, n=SBW), Fap(tb), filtT[:, tb, c0:c0 + SBW], st_, sp_,
                                            [Fk] + ftk(tb), [self.pk(bHf)])
                                    self.mm(self.psb(bHb, n=SBW), Fap(tb), filtT[:, tb, C + c0:C + c0 + SBW], st_, sp_,
                                            [Fk] + ftk(tb), [self.pk(bHb)])
                                    for s in range(nseq):
                                        self.mm(self.psb(bZ[s], n=SBW), Fap(tb), zT[:, s * NB + tb, c0:c0 + SBW], st_, sp_,
                                                [Fk] + zTk, [self.pk(bZ[s])])
                                H_, T_ = Hr[half], Hb[half]
                                P.op("dve", lambda e, T_=T_, bHb=bHb: e.tensor_tensor(T_[:], self.psb(bHb, n=SBW), rn2[:, C + c0:C + c0 + SBW], ALU.mult),
                                     reads=[self.pk(bHb)] + rnk, writes=[T_.k()])
                                P.op("dve", lambda e, H_=H_, bHf=bHf: e.tensor_tensor(H_[:], self.psb(bHf, n=SBW), rn2[:, c0:c0 + SBW], ALU.mult),
                                     reads=[self.pk(bHf)] + rnk, writes=[H_.k()])
                                fix = bp == 0 and half == 1
                                if fix:
                                    P.op("dve", lambda e, H_=H_, T_=T_: e.tensor_tensor(fx[0:1, :], H_[0:1, :], T_[0:1, :], ALU.add),
                                         reads=[H_.k(), T_.k()], writes=[fx.k()])
                                P.op("dve", lambda e, H_=H_, T_=T_, half=half: e.tensor_tensor(
                                    H_[:], H_[:], T_[:], ALU.add if half == 0 else ALU.subtract),
                                    reads=[H_.k(), T_.k()], writes=[H_.k()])
                                if fix:
                                    P.op("dve", lambda e, H_=H_: e.tensor_copy(H_[0:1, :], fx[0:1, :]),
                                         reads=[fx.k(), H_.k()], writes=[H_.k()])
                                if half == 0:
                                    P.op("dve", lambda e, H_=H_: e.tensor_tensor(H_[:], H_[:], bias2[:, c0:c0 + SBW], ALU.add),
                                         reads=[H_.k(), bias2.k()], writes=[H_.k()])
                                elif bp == 0:
                                    P.op("dve", lambda e, H_=H_: e.tensor_tensor(H_[0:1, :], H_[0:1, :], bias2[0:1, c0:c0 + SBW], ALU.add),
                                         reads=[H_.k(), bias2.k()], writes=[H_.k()])
                                if bp == 0:
                                    P.op("dve", lambda e, H_=H_: e.tensor_scalar(H_[0:1, :], H_[0:1, :], 0.5, None, ALU.mult),
                                         reads=[H_.k()], writes=[H_.k()])
                                for s in range(nseq):
                                    Z_ = Zs[half][s]
                                    P.op("act", lambda e, Z_=Z_, s=s, bZ=bZ: e.activation(Z_[:], self.psb(bZ[s], n=SBW), AF.Copy),
                                         reads=[self.pk(bZ[s])], writes=[Z_.k()])
                            for s in range(nseq):
                                Zr, Zi = Zs[0][s], Zs[1][s]
                                Yre = Y[:, bp, s, c0:c0 + SBW]
                                Yim = Y[:, bp + HP, s, c0:c0 + SBW]
                                rk_ = [Zr.k(), Zi.k(), Hr[0].k(), Hr[1].k()]
                                P.op("dve", lambda e, Zr=Zr: e.tensor_tensor(tt[0][:], Zr[:], Hr[0][:], ALU.mult), reads=rk_, writes=[tt[0].k()])
                                P.op("dve", lambda e, Zi=Zi: e.tensor_tensor(tt[1][:], Zi[:], Hr[1][:], ALU.mult), reads=rk_, writes=[tt[1].k()])
                                P.op("dve", lambda e, Yre=Yre: e.tensor_tensor(Yre, tt[0][:], tt[1][:], ALU.subtract),
                                     reads=[tt[0].k(), tt[1].k()], writes=[Y.k(bp, s, sb)])
                                P.op("pool", lambda e, Zr=Zr: e.tensor_tensor(tt[2][:], Zr[:], Hr[1][:], ALU.mult), reads=rk_, writes=[tt[2].k()])
                                P.op("pool", lambda e, Zi=Zi: e.tensor_tensor(tt[3][:], Zi[:], Hr[0][:], ALU.mult), reads=rk_, writes=[tt[3].k()])
                                P.op("pool", lambda e, Yim=Yim: e.tensor_tensor(Yim, tt[2][:], tt[3][:], ALU.add),
                                     reads=[tt[2].k(), tt[3].k()], writes=[Y.k(bp + HP, s, sb)])
                                if bp == 0:
                                    P.op("dve", lambda e, Zr=Zr, s=s: e.tensor_tensor(Y[0:1, 0, s, c0:c0 + SBW], Zr[0:1, :], Hr[0][0:1, :], ALU.mult),
                                         reads=rk_ + [Y.k(bp, s, sb)], writes=[Y.k(bp, s, sb)])
                                    P.op("pool", lambda e, Zi=Zi, s=s: e.tensor_tensor(Y[0:1, HP, s, c0:c0 + SBW], Zi[0:1, :], Hr[1][0:1, :], ALU.mult),
                                         reads=rk_ + [Y.k(bp + HP, s, sb)], writes=[Y.k(bp + HP, s, sb)])
                self.cp("D")
                with self.scope():
                    if g.samp:
                        Gt = [A.alloc("Gt%d" % q, [8, 512], BF16) for q in range(3)]
                        ng = 0
                        for tq in range(4):
                            bo = [self.bank() for _ in range(g.ncb)]
                            for fg in range(4):
                                G_ = Gt[ng % 3]
                                ng += 1
                                self.load("sp", G_[:], dr["Gs"][tq, fg], [G_.k()], "ld_G%d" % ((ng - 1) % 3))
                                for q in range(8):
                                    fb = fg * 8 + q
                                    for cb in range(g.ncb):
                                        self.mm(self.psb(bo[cb]), Y[:, fb, 0, cb * 128:(cb + 1) * 128], G_[:, q, :],
                                                fb == 0, fb == NFB - 1,
                                                [G_.k(), Y.k(fb, 0, 0)], [self.pk(bo[cb])])
                            for cb in range(g.ncb):
                                P.op("dve", lambda e, cb=cb, tq=tq, bo=bo: e.tensor_tensor(
                                    yg[:, cb, tq * 512:(tq + 1) * 512], self.psb(bo[cb]), xg[:, cb, tq * 512:(tq + 1) * 512], ALU.mult),
                                    reads=[self.pk(bo[cb]), xg.k(cb)], writes=[yg.k(cb, tq)])
                    else:
                        for cb in range(g.ncb):
                            sb = (cb * 128) // SBW
                            for s in range(nseq):
                                b = self.bank()
                                for fb in range(NFB):
                                    self.mm(self.psb(b, n=L), Y[:, fb, s, cb * 128:(cb + 1) * 128], self.Gp[:, fb, :],
                                            fb == 0, fb == NFB - 1, [self.Gp.k(), Y.k(fb, s, sb)], [self.pk(b)])
                                P.op("dve", lambda e, cb=cb, s=s, b=b: e.tensor_tensor(
                                    yg[:, cb, s * L:(s + 1) * L], self.psb(b, n=L), xg[:, cb, s * L:(s + 1) * L], ALU.mult),
                                    reads=[self.pk(b), xg.k(cb)], writes=[yg.k(cb, s)])
                self.cp("E")
                if g.samp:
                    self.gather_and_project(g, i, j, yg, "hy_wout", 1)
                    self.cp("W")

    def mla(self, g, i, j, part):
        P, A, dr = self.P, self.A, self.dr
        n = g.name
        if part == 2:
            self.gather_and_project(g, i, j, None, "mla_wo", 2)
            return
        NT, TB, nh, nseq, L = g.NT, g.TB, g.nh, g.nseq, g.L
        NK = NT + (PAST if g.samp else 0)
        KTB = NK // 512
        NKB = NK // 128
        HG = nh * 64
        ncg = HG // 128
        scale = 1.0 / math.sqrt(QK_NOPE + QK_ROPE)
        R64, R96a, R96 = slice(0, 64), slice(0, 96), slice(64, 96)
        if part == "post":
            og = self._carry
            with self.scope():
                w = self._wout
                ogk = _KeyAll(og, [(c, 0, par) for c in range(ncg) for par in range(2)])
                self.resid_update(g, i, ogk, w, 1)
            if self.use_barriers:
                self.P.barrier()
            self.A.pop()
            return
        if not g.samp:
            self.A.push()
            self._carry = A.alloc("og", [ncg, NT], BF16)
        with self.scope():
            sg = A.alloc("sg", [ncg, NT], BF16)
            qn = A.alloc("qn", [3, NT], BF16)
            ckvT = A.alloc("ckvT", [2, NK], BF16)
            kpeT = A.alloc("kpeT", [NK], BF16)
            with self.scope():
                hb = [A.alloc("h%d" % q, [8, 512], BF16) for q in range(2)]
                sc = self.norm_scratch()
                Wa = A.alloc("wa", [8, 832], BF16)
                Wg = A.alloc("wg", [8, HG], BF16)
                self.load("pool", Wa[:], dr["mla_wa"][j], [Wa.k()], "ld_wa")
                self.load("pool", Wg[:], dr["mla_wg_" + n][j], [Wg.k()], "ld_wg")
                sqb = A.alloc("sqb", [3, 512], BF16)
                rq = A.alloc("rq", [512])
                if g.samp:
                    rope = [A.alloc("rope%d" % q, [2, 512]) for q in range(2)]
                    kt1 = A.alloc("kt1", [512])
                    kt2 = A.alloc("kt2", [512])
                    self.load("pool", ckvT[:, :, NT:NK], dr["cckv"][j], [ckvT.k(0, TB), ckvT.k(1, TB)], "ld_cckv")
                    self.load("pool", kpeT[64:96, NT:NK], dr["ckpe"][j, 64:96], [kpeT.k(TB)], "ld_ckpe")
                else:
                    cko = A.alloc("cko", [2, 512])
                    kpo = A.alloc("kpo", [512])
                for tb in range(TB):
                    ts = slice(tb * 512, (tb + 1) * 512)
                    h = hb[tb % 2]
                    self.norm_tb(g, i, tb, lambda k: h[:, k, :], lambda k: [h.k(k)], sc)
                    if g.samp:
                        rp = rope[tb % 2]
                        self.load("sp", rp[:], dr["rope"][:, :, ts], [rp.k()], "ld_rope%d" % (tb % 2))

                    def proj(c0, M):
                        b = self.bank()
                        for k in range(8):
                            self.mm(self.psb(b, slice(0, M)), Wa[:, k, c0:c0 + M], h[:, k, :], k == 0, k == 7,
                                    [Wa.k(), h.k(k)], [self.pk(b)])
                        return b

                    def rms(banks, nch, inv_n, wv, dst, dkey, extra=None):
                        for m, b in enumerate(banks):
                            P.op("act", lambda e, m=m, b=b: e.activation(sqb[:, m, :], self.psb(b), AF.Square),
                                 reads=[self.pk(b)], writes=[sqb.k(m)])
                        bs = self.bank()
                        for m in range(nch):
                            self.mm(self.psb(bs), self.ones[:], sqb[:, m, :], m == 0, m == nch - 1,
                                    [self.ones.k(), sqb.k(m)], [self.pk(bs)])
                        P.op("act", lambda e: e.activation(rq[:], self.psb(bs), AF.Ln, bias=self.cst[:, 0:1], scale=inv_n),
                             reads=[self.pk(bs), self.cst.k()], writes=[rq.k()])
                        P.op("act", lambda e: e.activation(rq[:], rq[:], AF.Exp, scale=-0.5), reads=[rq.k()], writes=[rq.k()])
                        for m, b in enumerate(banks):
                            P.op("dve", lambda e, m=m, b=b: e.scalar_tensor_tensor(
                                dst[:, m, ts], self.psb(b), wv[:, j, m:m + 1], rq[:], ALU.mult, ALU.mult),
                                reads=[self.pk(b), rq.k(), wv.k()], writes=[dkey(m)])
                            if extra is not None:
                                extra(m, b)

                    bq = [proj(128 * m, 128) for m in range(3)]
                    bkv = [proj(384 + 128 * m, 128) for m in range(2)]
                    rms(bq, 3, 1.0 / Q_LORA, self.qn_w, qn, lambda m: qn.k(m, tb))
                    bk = proj(640, 96)
                    if g.samp:
                        bk2 = proj(736, 96)
                    if g.samp:
                        rms(bkv, 2, 1.0 / KV_LORA, self.kvn_w, ckvT, lambda m: ckvT.k(m, tb))
                    else:
                        def extra(m, b):
                            P.op("dve", lambda e: e.scalar_tensor_tensor(
                                cko[:, m, :], self.psb(b), self.kvn_w[:, j, m:m + 1], rq[:], ALU.mult, ALU.mult),
                                reads=[self.pk(b), rq.k(), self.kvn_w.k()], writes=[cko.k(m)])
                        rms(bkv, 2, 1.0 / KV_LORA, self.kvn_w, ckvT, lambda m: ckvT.k(m, tb), extra)
                        self.load("sp", dr["ckv_out"][j], cko[:], ["ckv_out%d" % j], "st_ckv%d" % j, reads=[cko.k(0), cko.k(1)])
                        self.outs.append("st_ckv%d" % j)
                    if g.samp:
                        P.op("dve", lambda e: e.tensor_tensor(kt1[R96, :], self.psb(bk, R96), rp[R96, 0, :], ALU.mult),
                             reads=[self.pk(bk), rp.k()], writes=[kt1.k()])
                        P.op("dve", lambda e: e.tensor_tensor(kt2[R96, :], self.psb(bk2, R96), rp[R96, 1, :], ALU.mult),
                             reads=[self.pk(bk2), rp.k()], writes=[kt2.k()])
                        P.op("dve", lambda e: e.tensor_tensor(kpeT[R96, ts], kt1[R96, :], kt2[R96, :], ALU.add),
                             reads=[kt1.k(), kt2.k()], writes=[kpeT.k(tb)])
                    else:
                        P.op("act", lambda e: e.activation(kpeT[R96, ts], self.psb(bk, R96), AF.Copy),
                             reads=[self.pk(bk)], writes=[kpeT.k(tb)])
                        P.op("act", lambda e: e.activation(kpo[R96, :], self.psb(bk, R96), AF.Copy),
                             reads=[self.pk(bk)], writes=[kpo.k()])
                        self.load("sp", dr["kpe_out"][j], kpo[R96, :], ["kpe_out%d" % j], "st_kpe%d" % j, reads=[kpo.k()])
                        self.outs.append("st_kpe%d" % j)
                    for m in range(ncg):
                        b = self.bank()
                        for k in range(8):
                            self.mm(self.psb(b), Wg[:, k, 128 * m:128 * m + 128], h[:, k, :], k == 0, k == 7,
                                    [Wg.k(), h.k(k)], [self.pk(b)])
                        P.op("act", lambda e, b=b, m=m: e.activation(sg[:, m, ts], self.psb(b), AF.Silu),
                             reads=[self.pk(b)], writes=[sg.k(m, tb)])
            self.cp("P1")
            QT = A.alloc("QT", [nh, NT], BF16)
            KT = A.alloc("KT", [nh, NK], BF16)
            Va = A.alloc("Vaug", [NKB, nh, 128], BF16)
            with self.scope():
                if g.samp:
                    Wq = A.alloc("wq", [3, nh, 2, 96], BF16)
                    rope = [A.alloc("rope%d" % q, [2, 512]) for q in range(2)]
                    qt1 = A.alloc("qt1", [512])
                    qt2 = A.alloc("qt2", [512])
                else:
                    Wq = A.alloc("wq", [3, nh, 96], BF16)
                Wk = A.alloc("wk", [2, nh, 64], BF16)
                Wv = A.alloc("wv", [2, HG], BF16)
                self.load("pool", Wq[:], dr["mla_wq_" + n][j], [Wq.k()], "ld_wq")
                self.load("pool", Wk[:], dr["mla_wk_" + n][j], [Wk.k()], "ld_wk")
                self.load("pool", Wv[:], dr["mla_wv_" + n][j], [Wv.k()], "ld_wv")
                P.op("pool", lambda e: e.memset(Va[:], 1.0), writes=[Va.k()])
                for tb in range(TB):
                    ts = slice(tb * 512, (tb + 1) * 512)
                    if g.samp:
                        rp = rope[tb % 2]
                        self.load("sp", rp[:], dr["rope"][:, :, ts], [rp.k()], "ld_rope%d" % (tb % 2))
                    for hh in range(nh):
                        b = self.bank()
                        for k in range(3):
                            lw = Wq[:, k, hh, 0, :] if g.samp else Wq[:, k, hh, :]
                            self.mm(self.psb(b, R96a), lw, qn[:, k, ts], k == 0, k == 2,
                                    [Wq.k(), qn.k(k, tb)], [self.pk(b)])
                        if g.samp:
                            b2 = self.bank()
                            for k in range(3):
                                self.mm(self.psb(b2, R96a), Wq[:, k, hh, 1, :], qn[:, k, ts], k == 0, k == 2,
                                        [Wq.k(), qn.k(k, tb)], [self.pk(b2)])
                            P.op("dve", lambda e, b=b, hh=hh, ts=ts: e.tensor_copy(QT[R64, hh, ts], self.psb(b, R64)),
                                 reads=[self.pk(b)], writes=[QT.k(hh, tb, 0)])
                            P.op("dve", lambda e, b=b, rp=rp: e.tensor_tensor(qt1[R96, :], self.psb(b, R96), rp[R96, 0, :], ALU.mult),
                                 reads=[self.pk(b), rp.k()], writes=[qt1.k()])
                            P.op("dve", lambda e, b2=b2, rp=rp: e.tensor_tensor(qt2[R96, :], self.psb(b2, R96), rp[R96, 1, :], ALU.mult),
                                 reads=[self.pk(b2), rp.k()], writes=[qt2.k()])
                            P.op("dve", lambda e, hh=hh, ts=ts: e.tensor_tensor(QT[R96, hh, ts], qt1[R96, :], qt2[R96, :], ALU.add),
                                 reads=[qt1.k(), qt2.k()], writes=[QT.k(hh, tb, 1)])
                        else:
                            P.op("act", lambda e, b=b, hh=hh, ts=ts: e.activation(QT[R96a, hh, ts], self.psb(b, R96a), AF.Copy),
                                 reads=[self.pk(b)], writes=[QT.k(hh, tb, 0), QT.k(hh, tb, 1)])
                import os
                if os.environ.get("K_DBG_P2STOP") == "q":
                    P.muted = True
                for tb in range(KTB):
                    ts = slice(tb * 512, (tb + 1) * 512)
                    for hh in range(nh):
                        b = self.bank()
                        for k in range(2):
                            self.mm(self.psb(b, R64), Wk[:, k, hh, :], ckvT[:, k, ts], k == 0, k == 1,
                                    [Wk.k(), ckvT.k(k, tb)], [self.pk(b)])
                        P.op("dve", lambda e, b=b, hh=hh, ts=ts: e.tensor_copy(KT[R64, hh, ts], self.psb(b, R64)),
                             reads=[self.pk(b)], writes=[KT.k("n", tb, hh)])
                        P.op("pool", lambda e, hh=hh, ts=ts: e.tensor_copy(KT[R96, hh, ts], kpeT[R96, ts]),
                             reads=[kpeT.k(tb)], writes=[KT.k("pe", tb, hh)])
                if os.environ.get("K_DBG_P2STOP") == "k":
                    P.muted = True
                for kb in range(min(NKB, int(os.environ.get("K_DBG_NKB", NKB)))):
                    for c0 in range(0, HG, 512):
                        w_ = min(512, HG - c0)
                        b = self.bank()
                        for k in range(2):
                            self.mm(self.psb(b, n=w_), ckvT[:, k, kb * 128:(kb + 1) * 128], Wv[:, k, c0:c0 + w_],
                                    k == 0, k == 1, [Wv.k(), ckvT.k(k, kb // 4)], [self.pk(b)])
                        nhh = w_ // 64
                        h0 = c0 // 64
                        for q in range(nhh):
                            hh = h0 + q
                            par = hh % 2
                            src = self.psb(b, n=64, off=q * 64)
                            dst = Va[:, kb, hh, par * 64:(par + 1) * 64]
                            if kb % 2 == 0 or os.environ.get("K_DBG_VACT"):
                                P.op("act", lambda e, src=src, dst=dst: e.activation(dst, src, AF.Copy),
                                     reads=[self.pk(b), Va.k()], writes=[Va.k(kb, hh)])
                            else:
                                P.op("dve", lambda e, src=src, dst=dst: e.tensor_copy(dst, src),
                                     reads=[self.pk(b), Va.k()], writes=[Va.k(kb, hh)])
            self.cp("P2")
            with self.scope():
                og = A.alloc("og", [ncg, NT], BF16) if g.samp else self._carry
                NPT = 3
                PT = [A.alloc("PT%d" % q, [1024], BF16) for q in range(NPT)]
                NIF = 2 if g.samp else 4
                rdb = [A.alloc("rdb%d" % q, [512], BF16) for q in range(NIF)]
                if g.samp:
                    ot = [A.alloc("ot", [512])] * NIF
                    rdf = [A.alloc("rdf", [512])] * NIF
                else:
                    ot = [A.alloc("ot%d" % q, [512]) for q in range(NIF)]
                    rdf = [A.alloc("rdf%d" % q, [512]) for q in range(NIF)]
                for q in range(NIF):
                    P.op("dve", lambda e, q=q: e.memset(rdb[q][:], 0.0), writes=[rdb[q].k()])
                QB = min(512, L)
                npt = 0
                nat = 0
                nsp = 0
                tails = []
                DEFER = 1 if g.samp else 2

                def emit_tail(tl):
                    bO, rd, o_, rf, Ro, Rd, hh, qs, qtb, par = tl
                    if g.samp:
                        P.op("dve", lambda e: e.reciprocal(rf[Rd, 0:QB], self.psb(bO, Rd, QB)),
                             reads=[self.pk(bO)], writes=[rf.k()])
                        P.op("act", lambda e: e.activation(rd[Rd, 0:QB], rf[Rd, 0:QB], AF.Copy),
                             reads=[rf.k()], writes=[rd.k()])
                    else:
                        P.op("act", lambda e: e.activation(rf[Rd, 0:QB], self.psb(bO, Rd, QB), AF.Ln),
                             reads=[self.pk(bO)], writes=[rf.k()])
                        P.op("act", lambda e: e.activation(rd[Rd, 0:QB], rf[Rd, 0:QB], AF.Exp, scale=-1.0),
                             reads=[rf.k()], writes=[rd.k()])
                    nonlocal nsp
                    if g.samp:
                        bR = 2 * (nsp % 3)
                        nsp += 1
                    else:
                        bR = 7
                    self.mm(self.psb(bR, n=QB), self.swap[:], rd[:, 0:QB], True, True,
                            [self.swap.k(), rd.k()], [self.pk(bR)])
                    P.op("dve", lambda e: e.tensor_tensor(o_[Ro, 0:QB], self.psb(bO, Ro, QB), sg[Ro, hh // 2, qs], ALU.mult),
                         reads=[self.pk(bO), sg.k(hh // 2, qtb), rf.k()], writes=[o_.k()])
                    P.op("dve", lambda e: e.tensor_tensor(og[Ro, hh // 2, qs], o_[Ro, 0:QB], self.psb(bR, Ro, QB), ALU.mult),
                         reads=[self.pk(bR), o_.k()], writes=[og.k(hh // 2, qtb, par)])
                for s in range(nseq):
                    if g.samp:
                        kbs = list(range(NKB))
                    else:
                        kbs = [s * (L // 128) + q for q in range(L // 128)]
                    npair = len(kbs) // 2
                    for hh in range(nh):
                        par = hh % 2
                        Ro = slice(par * 64, par * 64 + 64)
                        Rd = slice((1 - par) * 64, (1 - par) * 64 + 64)
                        for qb in range(L // QB):
                            q0 = s * L + qb * QB
                            qs = slice(q0, q0 + QB)
                            qtb = q0 // 512
                            bO = (6 + (nat % 2)) if g.samp else (3 + (nat % 4))
                            pend = []
                            for ip in range(npair):
                                if g.samp:
                                    b0 = 2 * (nsp % 3)
                                    sb_ = [(b0, 0), (b0 + 1, 0)]
                                    Sv = self.ps[:, b0 * 512:b0 * 512 + 1024].rearrange("p (k n) -> p k n", k=2)[:, :, 0:QB]
                                    spk = [self.pk(b0), self.pk(b0 + 1)]
                                else:
                                    b0 = nsp % 3
                                    sb_ = [(b0, 0), (b0, QB)]
                                    Sv = self.ps[:, b0 * 512:b0 * 512 + 2 * QB].rearrange("p (k n) -> p k n", k=2)
                                    spk = [self.pk(b0)]
                                nsp += 1
                                for u in range(2):
                                    kb = kbs[2 * ip + u]
                                    ktb = kb // 4
                                    self.mm(self.psb(sb_[u][0], n=QB, off=sb_[u][1]), KT[0:96, hh, kb * 128:(kb + 1) * 128],
                                            QT[0:96, hh, qs], True, True,
                                            [KT.k("n", ktb, hh), KT.k("pe", ktb, hh), QT.k(hh, qtb, 0), QT.k(hh, qtb, 1)],
                                            [self.pk(sb_[u][0])])
                                pt = PT[npt % NPT]
                                npt += 1
                                Pv = pt[:, 0:2 * QB].rearrange("p (k n) -> p k n", k=2)
                                P.op("act", lambda e, Sv=Sv, Pv=Pv: e.activation(Pv, Sv, AF.Exp, scale=scale),
                                     reads=spk, writes=[pt.k()])
                                pend.append((ip, kbs[2 * ip], kbs[2 * ip + 1], hh, pt))
                                if len(pend) > 2:
                                    self._pv2(bO, Va, pend.pop(0), QB, npair)
                            for pp in pend:
                                self._pv2(bO, Va, pp, QB, npair)
                            tails.append((bO, rdb[nat % NIF], ot[nat % NIF], rdf[nat % NIF], Ro, Rd, hh, qs, qtb, par))
                            nat += 1
                            if len(tails) > DEFER:
                                emit_tail(tails.pop(0))
                for tl in tails:
                    emit_tail(tl)
                self.cp("ATT")
                if g.samp:
                    P.op("sp", lambda e: e.dma_start(out=dr["cc_in%d" % i].rearrange("(c p) t -> p c t", p=128), in_=og[:]),
                         reads=[og.k(c, tb, par) for c in range(ncg) for tb in range(TB) for par in range(2)],
                         writes=["cc_in%d" % i], dma_sem="st_cc%d" % i)
                    P.op("pool", lambda e: e.collective_compute("AllGather", ALU.bypass, replica_groups=GROUP4,
                                                                ins=[dr["cc_in%d" % i]], outs=[dr["cc_out%d" % i]]),
                         reads=["cc_in%d" % i], writes=["cc_out%d" % i], dma_sem="cc%d" % i, inc=1, persist=True)

    def _pv2(self, bO, Va, prev, QB, npair):
        ip, kb0, kb1, hh, pt = prev
        for u, kb in enumerate((kb0, kb1)):
            self.mm(self.psb(bO, n=QB), Va[:, kb, hh, :], pt[:, u * QB:(u + 1) * QB], ip == 0 and u == 0,
                    ip == npair - 1 and u == 1, [Va.k(kb, hh), Va.k(), pt.k()], [self.pk(bO)])

    def _pv(self, bO, Va, prev, QB, nk):
        ix, kb, hh, pt = prev
        self.mm(self.psb(bO, n=QB), Va[:, kb, hh, :], pt[:, 0:QB], ix == 0, ix == nk - 1,
                [Va.k(kb, hh), Va.k(), pt.k()], [self.pk(bO)])


class _KeyAll:
    def __init__(self, t, subs):
        self.t = t
        self.subs = subs
        self.ap = t.ap

    def __getitem__(self, idx):
        return self.t.ap[idx]

    def k(self, *sub):
        if not sub:
            return self.t.k()
        kk = sub[0]
        return _MultiKey([self.t.k(*s) for s in self.subs if s[0] == kk])


class _MultiKey(list):
    pass


_orig_op = Prog.op


def _op_flat(self, eng, fn, reads=(), writes=(), **kw):
    def flat(ks):
        out = []
        for k in ks:
            if isinstance(k, _MultiKey):
                out.extend(k)
            else:
                out.append(k)
        return out
    return _orig_op(self, eng, fn, flat(reads), flat(writes), **kw)


Prog.op = _op_flat


def _final(self, g, oname):
    P, A = self.P, self.A
    x = self.x[g.name]
    dst = self.dr[oname]
    with self.scope():
        yo = [A.alloc("yo%d" % q, [8, 512]) for q in range(2)]
        sq = [A.alloc("fsq%d" % q, [8, 512], BF16) for q in range(2)]
        rs = [A.alloc("frs%d" % q, [512]) for q in range(2)]
        for tb in range(g.NT // 512):
            y_, s_, r_ = yo[tb % 2], sq[tb % 2], rs[tb % 2]
            ts = slice(tb * 512, (tb + 1) * 512)
            P.op("act", lambda e, s_=s_, ts=ts: e.activation(s_[:], x[:, :, ts], AF.Square),
                 reads=[x.k(k, tb) for k in range(8)], writes=[s_.k()])
            b = self.bank()
            for k in range(8):
                self.mm(self.psb(b), self.ones[:], s_[:, k, :], k == 0, k == 7, [self.ones.k(), s_.k()], [self.pk(b)])
            P.op("act", lambda e, b=b, r_=r_: e.activation(r_[:], self.psb(b), AF.Ln, bias=self.cst[:, 0:1], scale=1.0 / D),
                 reads=[self.pk(b), self.cst.k()], writes=[r_.k()])
            P.op("act", lambda e, r_=r_: e.activation(r_[:], r_[:], AF.Exp, scale=-0.5), reads=[r_.k()], writes=[r_.k()])
            for k in range(8):
                P.op("dve", lambda e, k=k, ts=ts, y_=y_, r_=r_: e.scalar_tensor_tensor(
                    y_[:, k, :], x[:, k, ts], self.fnorm[:, k:k + 1], r_[:], ALU.mult, ALU.mult),
                    reads=[x.k(k, tb), r_.k(), self.fnorm.k(), y_.k()], writes=[y_.k()])
            self.load("sp", dst[:, :, ts], y_[:], [(oname, tb)], "st_%s%d" % (oname, tb % 2), reads=[y_.k()])
            if ("st_%s%d" % (oname, tb % 2)) not in self.outs:
                self.outs.append("st_%s%d" % (oname, tb % 2))


Builder.final = _final


def _prep(inp):
    C = _consts()
    f = _f32
    common = {}
    common["cond"] = f(_chunk(np.stack([inp["c_ctx"], inp["c"][0], inp["c"][1]], axis=1)))
    common["normw"] = f(_chunk(inp["norm_w"].T))
    common["normw"] = f(common["normw"].transpose(0, 2, 1))
    common["fnorm"] = f(_chunk(inp["final_norm"][:, None])[:, :, 0])
    for k in ("ident", "ones", "swap", "Fs", "Gs", "Fp", "Gp", "zemb_s", "zemb_p", "tv_s", "tv_p", "rope"):
        common[k] = C[k]
    common["hy_fw1"] = f(inp["hy_f_w1"])
    common["hy_fw2"] = f(inp["hy_f_w2"])
    fv = np.zeros((2, 64, 4), np.float32)
    fv[:, :, 0] = inp["hy_f_b1"]
    fv[:, :, 1] = inp["hy_f_freq"]
    fv[:, :, 2] = inp["hy_f_b2"]
    common["hy_fvec"] = fv
    common["hy_wout"] = f(np.stack([_chunk(inp["hy_w_out"][j]) for j in range(2)]))
    common["mla_wo"] = f(np.stack([_chunk(inp["mla_w_o"][j]) for j in range(2)]))
    common["mla_qn"] = f(_chunk(inp["mla_q_norm"].T))
    common["mla_qn"] = f(common["mla_qn"].transpose(0, 2, 1))
    common["mla_kvn"] = f(_chunk(inp["mla_kv_norm"].T).transpose(0, 2, 1))
    sw = C["ropesw"]
    wa = []
    for j in range(2):
        w = inp["mla_w_in"][j]
        kpe = w[:, 640:672]
        z64 = np.zeros((D, 64), np.float32)
        wa.append(_chunk(np.concatenate([w[:, :640], z64, kpe, z64, kpe[:, sw]], axis=1)))
    common["mla_wa"] = f(np.stack(wa))

    def hy_group(chs, tag, out):
        cbs = [chs[q * 128:(q + 1) * 128] for q in range(len(chs) // 128)]
        win = np.zeros((2, len(cbs), 128, 8, 512), np.float32)
        cv = np.zeros((2, 128, len(cbs), 3, 4), np.float32)
        for j in range(2):
            w = inp["hy_w_in"][j]
            for q, cb in enumerate(cbs):
                cols = np.concatenate([gi * D + cb for gi in range(4)])
                win[j, q] = _chunk(w[:, cols])
                for gi in range(3):
                    cv[j, :, q, gi, 0:3] = inp["hy_conv_w"][j][:, gi * D + cb].T
                    cv[j, :, q, gi, 3] = inp["hy_conv_b"][j][gi * D + cb]
        out["hy_win_" + tag] = win
        out["hy_conv_" + tag] = cv
        out["hy_fw3_" + tag] = f(np.stack([inp["hy_f_w3"][j][:, np.concatenate([chs, D + chs])] for j in range(2)]))
        out["hy_fbias_" + tag] = f(np.stack([np.broadcast_to(inp["hy_f_bias"][j][chs], (128, len(chs))) for j in range(2)]))
        out["ndelta_" + tag] = f(np.broadcast_to(C["ndelta"][chs], (128, len(chs))))

    def mla_group(heads, tag, out):
        nh = len(heads)
        gcols = np.concatenate([672 + h * 64 + np.arange(64) for h in heads])
        out["mla_wg_" + tag] = f(np.stack([_chunk(inp["mla_w_in"][j][:, gcols]) for j in range(2)]))
        wk = np.zeros((2, 128, 2, nh, 64), np.float32)
        wv = np.zeros((2, 128, 2, nh * 64), np.float32)
        for j in range(2):
            w = inp["mla_w_kvb"][j].reshape(KV_LORA, NH, 128)
            wk[j] = _chunk(w[:, heads, :64])
            wv[j] = _chunk(w[:, heads, 64:].reshape(KV_LORA, nh * 64))
        out["mla_wk_" + tag] = wk
        out["mla_wv_" + tag] = wv
        wq = np.stack([inp["mla_w_qb"][j].reshape(Q_LORA, NH, 96)[:, heads] for j in range(2)])
        if tag == "p":
            out["mla_wq_p"] = f(np.stack([_chunk(wq[j]) for j in range(2)]))
        else:
            wsw = np.concatenate([wq[..., :64], wq[..., 64:][..., sw]], axis=-1)
            both = np.stack([wq, wsw], axis=3)
            out["mla_wq_s"] = f(np.stack([_chunk(both[j]) for j in range(2)]))

    hy_group(np.arange(D), "p", common)
    mla_group(list(range(NH)), "p", common)
    maps = []
    for c in range(NCORES):
        g, r = c // 4, c % 4
        m = dict(common)
        xp = inp["x_prompt"][2 * c:2 * c + 2].reshape(512, D)
        m["xp"] = f(_chunk(xp.T))
        m["xs"] = f(_chunk(inp["x_sample"][g].T))
        m["ada_w"] = f(np.stack([_chunk(inp["ada_w"][i][:, 768 * r:768 * (r + 1)]) for i in range(4)]))
        m["ada_b"] = f(inp["ada_b"][:, 768 * r:768 * (r + 1)].reshape(4, 6, 128).transpose(2, 0, 1))
        sel = np.zeros((128, 2), np.float32)
        sel[:, g] = 1.0
        m["sel"] = sel
        hy_group(np.arange(256 * r, 256 * (r + 1)), "s", m)
        mla_group(list(range(4 * r, 4 * r + 4)), "s", m)
        m["cckv"] = f(np.stack([_chunk(inp["cache_ckv"][g, j].T) for j in range(2)]))
        ck = np.zeros((2, 128, 512), np.float32)
        for j in range(2):
            ck[j, 64:96] = inp["cache_kpe"][g, j].T
        m["ckpe"] = ck
        maps.append(m)
    return maps


_NC_CACHE = {}


def _get_nc(depth=DEPTH):
    if depth not in _NC_CACHE:
        _NC_CACHE[depth] = Builder(depth).build()
    return _NC_CACHE[depth]


def kernel(**inputs):
    inp = {k: np.asarray(v) for k, v in inputs.items()}
    maps = _prep(inp)
    nc = _get_nc()
    res = run_bass_kernel_spmd(nc, maps, core_ids=list(range(NCORES)))
    R = res.results
    y_prompt = np.zeros((BATCH, SEQ, D), np.float32)
    state_ckv = np.zeros((BATCH, 2, SEQ, KV_LORA), np.float32)
    state_kpe = np.zeros((BATCH, 2, SEQ, QK_ROPE), np.float32)
    y_sample = np.zeros((DEC_BATCH, DEC_SEQ, D), np.float32)
    for c in range(NCORES):
        yp = np.asarray(R[c]["yp"]).transpose(2, 1, 0).reshape(512, D)
        y_prompt[2 * c:2 * c + 2] = yp.reshape(2, SEQ, D)
        ck = np.asarray(R[c]["ckv_out"])
        kp = np.asarray(R[c]["kpe_out"])
        for j in range(2):
            t = ck[j].transpose(2, 1, 0).reshape(512, KV_LORA)
            state_ckv[2 * c:2 * c + 2, j] = t.reshape(2, SEQ, KV_LORA)
            state_kpe[2 * c:2 * c + 2, j] = kp[j].T.reshape(2, SEQ, QK_ROPE)
    for g in range(DEC_BATCH):
        ys = np.asarray(R[4 * g]["ys"]).transpose(2, 1, 0).reshape(DEC_SEQ, D)
        y_sample[g] = ys
    return (y_prompt, y_sample, state_ckv, state_kpe)
```

```python
import math
from contextlib import ExitStack

import numpy as np
import ml_dtypes

import concourse.bass as bass
import concourse.mybir as mybir
from concourse.bass_utils import run_bass_kernel_spmd

F32 = mybir.dt.float32
BF16 = mybir.dt.bfloat16
I32 = mybir.dt.int32
AF = mybir.ActivationFunctionType
ALU = mybir.AluOpType

NCORES = 8
D = 1024
DEPTH = 4
BATCH, SEQ = 16, 256
DEC_BATCH, DEC_SEQ = 2, 2048
PAST = 512
EPS = 1e-6
NH = 16
Q_LORA, KV_LORA, QK_NOPE, QK_ROPE, V_HEAD = 384, 256, 64, 32, 64
GRID_W = 64
ROPE_THETA = 10000.0
ENGS = ("pe", "act", "dve", "pool", "sp")
GROUP4 = [[0, 1, 2, 3], [4, 5, 6, 7]]
GROUP8 = [[0, 1, 2, 3, 4, 5, 6, 7]]


class _Op:
    __slots__ = ("eng", "fn", "deps", "signals", "val", "dma_sem", "inc", "persist", "idx")

    def __init__(self, eng, fn, deps, dma_sem=None):
        self.eng = eng
        self.fn = fn
        self.deps = deps
        self.signals = False
        self.val = None
        self.dma_sem = dma_sem
        self.inc = 16
        self.persist = False


class _Rec:
    def __init__(self):
        self.call = None

    def __getattr__(self, name):
        def f(*a, **kw):
            self.call = (name, a, kw)
            return self
        return f


class _Reg:
    __slots__ = ("w", "r")

    def __init__(self):
        self.w = None
        self.r = []


class Prog:
    def __init__(self, nc):
        self.nc = nc
        self.ops = {e: [] for e in ENGS}
        self.regs = {}
        self.dma_counts = {}
        self.last = {e: None for e in ENGS}
        self.dma_since_bar = {}
        self.tile_keys = {}
        self.hazards = {}

    def _reg(self, k):
        r = self.regs.get(k)
        if r is None:
            r = self.regs[k] = _Reg()
            tk = k if isinstance(k, str) else k[0]
            self.tile_keys.setdefault(tk, set()).add(k)
        return r

    def tile_ops(self, tkey):
        out = []
        for k in self.tile_keys.get(tkey, ()):
            rg = self.regs[k]
            if rg.w is not None:
                out.append(rg.w)
            out.extend(rg.r)
        return out

    muted = False

    def op(self, eng, fn, reads=(), writes=(), dma_sem=None, inc=16, persist=False):
        if self.muted:
            return None
        deps = []
        seen = set()

        def add(d):
            if d is None or id(d) in seen:
                return
            if eng == "pe" and d.eng == "pe" and d.dma_sem is None:
                return
            seen.add(id(d))
            deps.append(d)

        if self.hazards:
            for k in list(reads) + list(writes):
                hz = self.hazards.get(k if isinstance(k, str) else k[0])
                if hz:
                    for d in hz:
                        add(d)
        for k in reads:
            add(self._reg(k).w)
        for k in writes:
            rg = self._reg(k)
            w = rg.w
            if w is not None and dma_sem is not None and w.dma_sem == dma_sem:
                for d in w.deps:
                    add(d)
            else:
                add(w)
            for d in rg.r:
                add(d)
        rec = _Rec()
        fn(rec)
        call = rec.call
        import sys as _sys
        fr = _sys._getframe(2)
        where = "%s:%d" % (fr.f_code.co_name, fr.f_lineno)

        def _do(e, c=call, where=where):
            try:
                return getattr(e, c[0])(*c[1], **c[2])
            except Exception as ex:
                raise RuntimeError("emit failed for op recorded at %s: %s %s" % (where, c[0], ex)) from ex
        o = _Op(eng, _do, deps, dma_sem)
        self.nops = getattr(self, "nops", 0) + 1
        o.idx = self.nops
        if dma_sem is not None:
            v = self.dma_counts.get(dma_sem, 0) + inc
            self.dma_counts[dma_sem] = v
            o.val = v
            o.inc = inc
            o.persist = persist
            if not persist:
                self.dma_since_bar[dma_sem] = o
        for k in reads:
            self._reg(k).r.append(o)
        for k in writes:
            rg = self._reg(k)
            rg.w = o
            rg.r = []
        self.ops[eng].append(o)
        if dma_sem is None:
            self.last[eng] = o
        return o

    def barrier(self):
        lasts = [o for o in self.last.values() if o is not None]
        dmas = list(self.dma_since_bar.values())
        self.dma_since_bar = {}
        self.hazards = {}
        new = {}
        for e in ENGS:
            deps = [o for o in lasts if o.eng != e or e != "pe"] + dmas
            if e == "pe":
                deps = [o for o in deps if not (o.eng == "pe" and o.dma_sem is None)]
            b = _Op(e, None, deps)
            self.ops[e].append(b)
            new[e] = b

    def emit(self, final_dma_sems=()):
        nc = self.nc
        for e in ENGS:
            for o in self.ops[e]:
                for d in o.deps:
                    d.signals = True
        with ExitStack() as st:
            esem = {e: st.enter_context(nc.semaphore("s_" + e)) for e in ENGS}
            dsem = {k: st.enter_context(nc.semaphore("d_%d" % i))
                    for i, k in enumerate(self.dma_counts)}
            for e in ENGS:
                c = 0
                for o in self.ops[e]:
                    if o.dma_sem is None and o.signals:
                        c += 1
                        o.val = c
            block = st.enter_context(nc.Block())
            engobj = {"pe": "tensor", "act": "scalar", "dve": "vector", "pool": "gpsimd", "sp": "sync"}

            def run(e, eng):
                waited = {}
                for o in self.ops[e]:
                    need = {}
                    for d in o.deps:
                        if d.dma_sem is not None:
                            key = ("d", d.dma_sem)
                            sem = dsem[d.dma_sem]
                        else:
                            key = ("e", d.eng)
                            sem = esem[d.eng]
                        if waited.get(key, 0) >= d.val:
                            continue
                        if key not in need or need[key][1] < d.val:
                            need[key] = (sem, d.val)
                    for key, (sem, v) in need.items():
                        eng.wait_ge(sem, v)
                        waited[key] = v
                    if o.fn is None:
                        continue
                    ins = o.fn(eng)
                    if o.dma_sem is not None:
                        ins.then_inc(dsem[o.dma_sem], o.inc)
                    elif o.signals:
                        ins.then_inc(esem[e], 1)
                if e == "sp":
                    for k in final_dma_sems:
                        if k in dsem:
                            eng.wait_ge(dsem[k], self.dma_counts[k])

            for e in ENGS:
                def mk(e):
                    def f(eng):
                        run(e, eng)
                    return f
                getattr(block, engobj[e])(mk(e))


class T:
    _n = 0

    def __init__(self, ap, name):
        T._n += 1
        self.ap = ap
        self.key = "%s#%d" % (name, T._n)

    def __getitem__(self, idx):
        return self.ap[idx]

    def k(self, *sub):
        return (self.key,) + tuple(sub) if sub else self.key


class Arena:
    def __init__(self, nc, st, words):
        self.t = st.enter_context(nc.sbuf_tensor("arena", [128, words], F32))
        self.words = words
        self.top = 0
        self.stack = []
        self.peak = 0
        self.live = []
        self.dead = []
        self.prog = None

    def alloc(self, name, free_shape, dt=F32):
        n = int(np.prod(free_shape))
        w = n if dt in (F32, I32) else (n + 1) // 2
        w = (w + 7) // 8 * 8
        off = self.top
        self.top += w
        self.peak = max(self.peak, self.top)
        assert self.top <= self.words, "SBUF arena overflow: %s needs %d words, top %d" % (name, w, self.top)
        ap = self.t[:, off:off + w]
        if dt != F32:
            ap = ap.bitcast(dt)
        ap = ap[:, 0:n]
        if len(free_shape) == 2:
            ap = ap.rearrange("p (a b) -> p a b", a=free_shape[0])
        elif len(free_shape) == 3:
            ap = ap.rearrange("p (a b c) -> p a b c", a=free_shape[0], b=free_shape[1])
        elif len(free_shape) == 4:
            ap = ap.rearrange("p (a b c d) -> p a b c d", a=free_shape[0], b=free_shape[1], c=free_shape[2])
        t = T(ap, name)
        if self.prog is not None:
            hz = []
            seen = set()
            for (s0, e0, old) in self.dead:
                if s0 < off + w and off < e0:
                    for o in self.prog.tile_ops(old.key):
                        if id(o) not in seen:
                            seen.add(id(o))
                            hz.append(o)
            if hz:
                best = {}
                for o in hz:
                    kk = ("d", o.dma_sem) if o.dma_sem is not None else ("e", o.eng)
                    if kk not in best or best[kk].idx < o.idx:
                        best[kk] = o
                self.prog.hazards[t.key] = list(best.values())
        self.live.append((off, off + w, t))
        return t

    def push(self):
        self.stack.append(self.top)

    def pop(self):
        self.top = self.stack.pop()
        keep = []
        for it in self.live:
            if it[0] >= self.top:
                self.dead.append(it)
            else:
                keep.append(it)
        self.live = keep

    def clear_dead(self):
        self.dead = []


def _bf(a):
    return np.ascontiguousarray(np.asarray(a, np.float32).astype(ml_dtypes.bfloat16))


def _f32(a):
    return np.ascontiguousarray(np.asarray(a, np.float32))


def _dft(L):
    n = 2 * L
    t = np.arange(L, dtype=np.float64)[:, None]
    kre = np.arange(0, L + 1, dtype=np.float64)[None, :]
    kim = np.arange(1, L, dtype=np.float64)[None, :]
    return np.concatenate([np.cos(2 * np.pi * kre * t / n), -np.sin(2 * np.pi * kim * t / n)], axis=1)


def _zemb(L):
    f32 = np.float32
    t = np.linspace(0.0, 1.0, L, dtype=f32)[:, None]
    w = (f32(2.0 * math.pi / L) * np.arange(L, dtype=f32))[:, None]
    bands = np.linspace(1e-4, 15, 16, dtype=f32)[None, :]
    z = np.concatenate([t, np.cos(bands * w), -np.sin(bands * w)], axis=-1)
    return z.T


def _chunk(w):
    K = w.shape[0]
    return w.reshape(K // 128, 128, *w.shape[1:]).swapaxes(0, 1)


_CONST = {}


def _consts():
    if _CONST:
        return _CONST
    Ls, Lp = DEC_SEQ, SEQ
    F = _dft(Ls)
    _CONST["Fs"] = _bf(F.reshape(16, 128, 32, 128).transpose(2, 1, 0, 3))
    G = F.T
    _CONST["Gs"] = _bf(G.reshape(4, 8, 128, 4, 512).transpose(3, 0, 2, 1, 4))
    Fq = _dft(Lp)
    _CONST["Fp"] = _bf(Fq.reshape(2, 128, 512).transpose(1, 0, 2))
    _CONST["Gp"] = _bf(Fq.T.reshape(4, 128, 256).transpose(1, 0, 2))
    _CONST["zemb_s"] = _bf(_zemb(Ls))
    _CONST["zemb_p"] = _bf(_zemb(Lp))
    _CONST["tv_s"] = _f32(np.linspace(0.0, 1.0, Ls, dtype=np.float32).reshape(16, 128).T)
    _CONST["tv_p"] = _f32(np.linspace(0.0, 1.0, Lp, dtype=np.float32).reshape(2, 128).T)
    maxd = math.log(1e-2) / 0.3
    mind = math.log(1e-2) / 1.5
    _CONST["ndelta"] = -np.abs(np.linspace(mind, maxd, D, dtype=np.float32))
    ident = np.eye(128, dtype=np.float32)
    _CONST["ident"] = _bf(ident)
    _CONST["ones"] = _bf(np.ones((128, 128), np.float32))
    _CONST["swap"] = _bf(np.roll(ident, 64, axis=1))
    rows = Ls // GRID_W
    axis_dim = QK_ROPE // 2
    inv = (ROPE_THETA ** (-np.arange(0, axis_dim, 2, dtype=np.float32) / axis_dim)).astype(np.float32)
    row = np.repeat(np.arange(rows, dtype=np.float32), GRID_W)
    col = np.tile(np.arange(GRID_W, dtype=np.float32), rows)
    ar = (row[:, None] * inv).astype(np.float32)
    ac = (col[:, None] * inv).astype(np.float32)
    cs = np.zeros((128, 2, Ls), np.float32)
    cs[64:72, 0] = np.cos(ar).T
    cs[72:80, 0] = np.cos(ar).T
    cs[80:88, 0] = np.cos(ac).T
    cs[88:96, 0] = np.cos(ac).T
    cs[64:72, 1] = -np.sin(ar).T
    cs[72:80, 1] = np.sin(ar).T
    cs[80:88, 1] = -np.sin(ac).T
    cs[88:96, 1] = np.sin(ac).T
    _CONST["rope"] = cs
    _CONST["ropesw"] = np.concatenate([np.arange(8, 16), np.arange(0, 8), np.arange(24, 32), np.arange(16, 24)])
    return _CONST


class Grp:
    def __init__(self, name, NT, nseq, ncb, nh):
        self.name = name
        self.NT = NT
        self.nseq = nseq
        self.L = NT // nseq
        self.TB = max(1, NT // 512)
        self.NB = self.L // 128
        self.ncb = ncb
        self.C = ncb * 128
        self.nh = nh
        self.SBW = min(512, self.C)
        self.nsb = self.C // self.SBW
        self.NFB = 2 * self.L // 128
        self.samp = name == "s"


GP = Grp("p", 512, 2, 8, 16)
GS = Grp("s", 2048, 1, 2, 4)


class Builder:
    def __init__(self, depth=DEPTH):
        self.depth = depth
        self.nc = bass.Bass("TRN2", target_bir_lowering=False)
        self.P = Prog(self.nc)
        self.dr = {}
        self._bank = 0
        self._b4 = 0
        self.outs = []

    def din(self, name, shape, dt=F32):
        self.dr[name] = self.nc.dram_tensor(name, list(shape), dt, kind="ExternalInput").ap()
        return self.dr[name]

    def dout(self, name, shape, dt=F32):
        self.dr[name] = self.nc.dram_tensor(name, list(shape), dt, kind="ExternalOutput").ap()
        return self.dr[name]

    def dint(self, name, shape, dt=F32):
        self.dr[name] = self.nc.dram_tensor(name, list(shape), dt).ap()
        return self.dr[name]

    def bank(self):
        b = self._bank
        self._bank = (self._bank + 1) % 8
        return b

    def bank4(self):
        b = self._b4 * 4
        self._b4 ^= 1
        return b

    def psb(self, b, rows=slice(0, 128), n=512, off=0):
        return self.ps[rows, b * 512 + off:b * 512 + off + n]

    def pk(self, b):
        return ("ps", b)

    def load(self, q, dst, src, writes, sem, reads=(), persist=False):
        return self.P.op(q, lambda e: e.dma_start(out=dst, in_=src), reads=reads, writes=writes,
                         dma_sem=sem, persist=persist)

    def mm(self, out, lhsT, rhs, start, stop, reads, writes):
        self.P.op("pe", lambda e: e.matmul(out, lhsT, rhs, start=start, stop=stop), reads=reads, writes=writes)

    stop_at = None
    _cp = 0
    use_barriers = False

    def cp(self, name=""):
        import os
        self._cp += 1
        if self.stop_at is not None and self._cp >= self.stop_at:
            self.P.muted = True
        mr = os.environ.get("K_DBG_MUTE")
        if mr:
            a, b = [int(v) for v in mr.split(",")]
            self.P.muted = a <= self._cp < b

    def scope(self):
        b = self

        class _S:
            def __enter__(s):
                b.A.push()

            def __exit__(s, *a):
                if b.use_barriers:
                    b.P.barrier()
                    b.A.pop()
                    b.A.clear_dead()
                else:
                    b.A.pop()
        return _S()

    def declare(self):
        d = self.din
        d("xp", [128, 8, 512]); d("xs", [128, 8, 2048])
        d("cond", [128, 8, 3]); d("ada_w", [4, 128, 8, 768]); d("ada_b", [128, 4, 6])
        d("sel", [128, 2]); d("normw", [128, 4, 8]); d("fnorm", [128, 8])
        d("ident", [128, 128], BF16); d("ones", [128, 128], BF16); d("swap", [128, 128], BF16)
        for g in (GP, GS):
            n = g.name
            d("hy_win_" + n, [2, g.ncb, 128, 8, 512])
            d("hy_conv_" + n, [2, 128, g.ncb, 3, 4])
            d("hy_fw3_" + n, [2, 64, 2 * g.C])
            d("hy_fbias_" + n, [2, 128, g.C])
            d("ndelta_" + n, [128, g.C])
            d("zemb_" + n, [33, g.L], BF16)
            d("tv_" + n, [128, g.NB])
            d("mla_wg_" + n, [2, 128, 8, g.nh * 64])
            d("mla_wk_" + n, [2, 128, 2, g.nh, 64])
            d("mla_wv_" + n, [2, 128, 2, g.nh * 64])
        d("mla_wq_p", [2, 128, 3, 16, 96]); d("mla_wq_s", [2, 128, 3, 4, 2, 96])
        d("hy_fw1", [2, 33, 64]); d("hy_fvec", [2, 64, 4]); d("hy_fw2", [2, 64, 64])
        d("hy_wout", [2, 128, 8, 1024])
        d("Fs", [32, 128, 16, 128], BF16); d("Gs", [4, 4, 128, 8, 512], BF16)
        d("Fp", [128, 2, 512], BF16); d("Gp", [128, 4, 256], BF16)
        d("rope", [128, 2, 2048])
        d("mla_wa", [2, 128, 8, 832]); d("mla_qn", [128, 2, 3]); d("mla_kvn", [128, 2, 2])
        d("mla_wo", [2, 128, 8, 1024])
        d("cckv", [2, 128, 2, 512]); d("ckpe", [2, 128, 512])
        self.dout("yp", [128, 8, 512]); self.dout("ys", [128, 8, 2048])
        self.dout("ckv_out", [2, 128, 2, 512]); self.dout("kpe_out", [2, 32, 512])
        self.dint("cc_ada_in", [128, 72]); self.dint("cc_ada_out", [512, 72])
        for i in range(DEPTH):
            self.dint("cc_in%d" % i, [256, 2048], BF16)
            self.dint("cc_out%d" % i, [1024, 2048], BF16)

    def build(self):
        nc, P = self.nc, self.P
        self.declare()
        dr = self.dr
        with ExitStack() as st:
            self.A = A = Arena(nc, st, 51200)
            A.prog = P
            self.ps = st.enter_context(nc.psum_tensor("ps", [128, 4096], F32))
            self.xg_ = {}
            self.x = {"p": A.alloc("xp", [8, 512]), "s": A.alloc("xs", [8, 2048])}
            self.ident = A.alloc("ident", [128], BF16)
            self.ones = A.alloc("ones", [128], BF16)
            self.swap = A.alloc("swap", [128], BF16)
            self.modt = A.alloc("modt", [24, 4, 3])
            self.mods = A.alloc("mods", [24, 4])
            self.modA = A.alloc("modA", [2, 4, 8])
            self.normw = A.alloc("normw", [4, 8])
            self.fnorm = A.alloc("fnorm", [8])
            self.cst = A.alloc("cst", [8])
            self.sel = A.alloc("sel", [2])
            self.Fp = A.alloc("Fp", [2, 512], BF16)
            self.Gp = A.alloc("Gp", [4, 256], BF16)
            self.qn_w = A.alloc("qn_w", [2, 3])
            self.kvn_w = A.alloc("kvn_w", [2, 2])
            self.tv = {"p": A.alloc("tv_p", [2]), "s": A.alloc("tv_s", [16])}
            ld = self.load
            ld("sp", self.x["p"][:], dr["xp"], [self.x["p"].k(k, 0) for k in range(8)], "ld_xp")
            ld("sp", self.x["s"][:], dr["xs"], [self.x["s"].k(k, tb) for k in range(8) for tb in range(4)], "ld_xs")
            for t, n in ((self.ident, "ident"), (self.ones, "ones"), (self.swap, "swap"), (self.normw, "normw"),
                         (self.fnorm, "fnorm"), (self.sel, "sel"), (self.Fp, "Fp"), (self.Gp, "Gp"),
                         (self.qn_w, "mla_qn"), (self.kvn_w, "mla_kvn"), (self.tv["p"], "tv_p"), (self.tv["s"], "tv_s")):
                ld("sp", t[:], dr[n], [t.k()], "ld_" + n)
            P.op("dve", lambda e: e.memset(self.cst[:, 0:1], EPS), writes=[self.cst.k()])
            P.op("dve", lambda e: e.memset(self.cst[:, 1:2], 0.0), reads=[self.cst.k()], writes=[self.cst.k()])
            self.adaln()
            for i in range(self.depth):
                j = i // 2
                fn = self.hyena if i % 2 == 0 else self.mla
                fn(GS, i, j, part=1)
                self.A.push()
                self._wout = A.alloc("wout", [8, 1024], BF16)
                self.load("pool", self._wout[:], dr["hy_wout" if i % 2 == 0 else "mla_wo"][j], [self._wout.k()], "ld_wout")
                fn(GP, i, j, part="pre")
                fn(GS, i, j, part=2)
                fn(GP, i, j, part="post")
                self.A.pop()
            self.P.muted = False
            self.final(GP, "yp")
            self.final(GS, "ys")
            P.emit(final_dma_sems=self.outs)
        return nc

    def adaln(self):
        P, A, dr = self.P, self.A, self.dr
        with self.scope():
            cnd = A.alloc("cond", [8, 3])
            cb = A.alloc("condb", [8, 3], BF16)
            adb = A.alloc("adb", [4, 6])
            res = A.alloc("adares", [6, 4, 3])
            W = [A.alloc("adaW%d" % i, [8, 768], BF16) for i in range(2)]
            self.load("sp", cnd[:], dr["cond"], [cnd.k()], "ld_cond")
            self.load("sp", adb[:], dr["ada_b"], [adb.k()], "ld_adb")
            P.op("act", lambda e: e.activation(cb[:], cnd[:], AF.Silu), reads=[cnd.k()], writes=[cb.k()])
            b = self.bank()
            for i in range(4):
                Wt = W[i % 2]
                self.load("pool", Wt[:], dr["ada_w"][i], [Wt.k()], "ld_adaW%d" % (i % 2))
                for m in range(6):
                    c0 = (i * 6 + m) * 3
                    for k in range(8):
                        self.mm(self.psb(b, n=3, off=c0), Wt[:, k, 128 * m:128 * m + 128], cb[:, k, :],
                                k == 0, k == 7, [Wt.k(), cb.k()], [self.pk(b)])
            for i in range(4):
                for m in range(6):
                    c0 = (i * 6 + m) * 3
                    P.op("dve", lambda e, i=i, m=m, c0=c0: e.tensor_scalar(
                        res[:, m, i, :], self.psb(b, n=3, off=c0), adb[:, i, m:m + 1], None, ALU.add),
                        reads=[self.pk(b), adb.k()], writes=[res.k(i, m)])
            rk = [res.k(i, m) for i in range(4) for m in range(6)]
            self.load("sp", dr["cc_ada_in"], res[:].rearrange("p a b c -> p (a b c)"), ["cc_ada_in"], "st_ada", reads=rk)
            P.op("pool", lambda e: e.collective_compute("AllGather", ALU.bypass, replica_groups=GROUP4,
                                                        ins=[dr["cc_ada_in"]], outs=[dr["cc_ada_out"]]),
                 reads=["cc_ada_in"], writes=["cc_ada_out"], dma_sem="cc_ada", inc=1)
            self.load("sp", self.modt[:].rearrange("p (j m) i c -> p j (m i c)", j=4),
                      dr["cc_ada_out"].rearrange("(j p) f -> p j f", p=128), [self.modt.k()], "ld_modt",
                      reads=["cc_ada_out"])
            mt, ms = self.modt, self.mods
            P.op("dve", lambda e: e.tensor_scalar(ms[:], mt[:, :, :, 1], self.sel[:, 0:1], None, ALU.mult),
                 reads=[mt.k(), self.sel.k()], writes=[ms.k()])
            P.op("dve", lambda e: e.scalar_tensor_tensor(ms[:], mt[:, :, :, 2], self.sel[:, 1:2], ms[:], ALU.mult, ALU.add),
                 reads=[mt.k(), self.sel.k(), ms.k()], writes=[ms.k()])
            mA = self.modA
            for gi in range(2):
                for i in range(4):
                    src = mt[:, 8:16, i, 0] if gi == 0 else ms[:, 8:16, i]
                    P.op("dve", lambda e, gi=gi, i=i, src=src: e.scalar_tensor_tensor(
                        mA[:, gi, i, :], src, 1.0, self.normw[:, i, :], ALU.add, ALU.mult),
                        reads=[mt.k(), ms.k(), self.normw.k(), mA.k()], writes=[mA.k()])

    def mod(self, g, i, part, k):
        c = part * 8 + k
        if g.samp:
            return self.mods[:, c, i:i + 1]
        return self.modt[:, c, i, 0:1]

    def modkeys(self):
        return [self.modt.k(), self.mods.k(), self.modA.k()]

    def norm_scratch(self):
        A = self.A
        return {"sq": A.alloc("sq", [8, 512], BF16), "rs": [A.alloc("rstd%d" % q, [512]) for q in range(2)],
                "tmp": [A.alloc("nt%d" % q, [512]) for q in range(2)], "n": 0}

    def norm_tb(self, g, i, tb, out_fn, wk, sc):
        P = self.P
        x = self.x[g.name]
        gi = 1 if g.samp else 0
        ts = slice(tb * 512, (tb + 1) * 512)
        s_, r_ = sc["sq"], sc["rs"][tb % 2]
        P.op("act", lambda e: e.activation(s_[:, 0:4, :], x[:, 0:4, ts], AF.Square),
             reads=[x.k(k, tb) for k in range(4)], writes=[s_.k(0)])
        P.op("dve", lambda e: e.tensor_tensor(s_[:, 4:8, :], x[:, 4:8, ts], x[:, 4:8, ts], ALU.mult),
             reads=[x.k(k, tb) for k in range(4, 8)], writes=[s_.k(1)])
        b = self.bank()
        for k in range(8):
            self.mm(self.psb(b), self.ones[:], s_[:, k, :], k == 0, k == 7, [self.ones.k(), s_.k(k // 4)], [self.pk(b)])
        P.op("act", lambda e: e.activation(r_[:], self.psb(b), AF.Ln, bias=self.cst[:, 0:1], scale=1.0 / D),
             reads=[self.pk(b), self.cst.k()], writes=[r_.k()])
        P.op("act", lambda e: e.activation(r_[:], r_[:], AF.Exp, scale=-0.5), reads=[r_.k()], writes=[r_.k()])
        for k in range(8):
            t_ = sc["tmp"][sc["n"] % 2]
            sc["n"] += 1
            P.op("dve", lambda e, k=k, t_=t_: e.tensor_tensor(t_[:], x[:, k, ts], r_[:], ALU.mult),
                 reads=[x.k(k, tb), r_.k()], writes=[t_.k()])
            P.op("act", lambda e, k=k, t_=t_: e.activation(out_fn(k), t_[:], AF.Identity,
                                                          bias=self.mod(g, i, 0, k), scale=self.modA[:, gi, i, k:k + 1]),
                 reads=[t_.k()] + self.modkeys(), writes=wk(k))

    def norm_mod(self, g, i, h):
        def body():
            sc = self.norm_scratch()
            for tb in range(g.TB):
                self.norm_tb(g, i, tb, lambda k, tb=tb: h[:, k, tb * 512:(tb + 1) * 512],
                             lambda k, tb=tb: [h.k(k, tb)], sc)
        if g.samp:
            with self.scope():
                body()
        else:
            body()

    def resid_update(self, g, i, yga, w, TBs):
        P = self.P
        x = self.x[g.name]
        for tb in range(TBs):
            for m in range(8):
                b = self.bank()
                for k in range(8):
                    self.mm(self.psb(b), w[:, k, 128 * m:128 * m + 128], yga[:, k, tb * 512:(tb + 1) * 512],
                            k == 0, k == 7, [w.k(), yga.k(k, tb)], [self.pk(b)])
                P.op("dve", lambda e, b=b, m=m, tb=tb: e.scalar_tensor_tensor(
                    x[:, m, tb * 512:(tb + 1) * 512], self.psb(b), self.mod(g, i, 2, m),
                    x[:, m, tb * 512:(tb + 1) * 512], ALU.mult, ALU.add),
                    reads=[self.pk(b), x.k(m, tb)] + self.modkeys(), writes=[x.k(m, tb)])

    def gather_and_project(self, g, i, j, yg, wname, part):
        P, A, dr = self.P, self.A, self.dr
        if part == 1:
            self.load("sp", dr["cc_in%d" % i].rearrange("(c p) t -> p c t", p=128), yg[:],
                      ["cc_in%d" % i], "st_cc%d" % i, reads=[yg.k(c, tb) for c in range(2) for tb in range(4)])
            P.op("pool", lambda e: e.collective_compute("AllGather", ALU.bypass, replica_groups=GROUP4,
                                                        ins=[dr["cc_in%d" % i]], outs=[dr["cc_out%d" % i]]),
                 reads=["cc_in%d" % i], writes=["cc_out%d" % i], dma_sem="cc%d" % i, inc=1, persist=True)
            return
        with self.scope():
            yga = A.alloc("yga", [8, 2048], BF16)
            w = self._wout
            src = dr["cc_out%d" % i].rearrange("(k p) t -> p k t", p=128)
            for tb in range(4):
                self.load("sp", yga[:, :, tb * 512:(tb + 1) * 512], src[:, :, tb * 512:(tb + 1) * 512],
                          [yga.k(k, tb) for k in range(8)], "ld_yga%d" % tb, reads=["cc_out%d" % i])
            import os
            if os.environ.get("K_DBG_P2") == "loads":
                return
            self.resid_update(g, i, yga, w, 4)

    def sin_layer(self, ps_ap, rows, n, bvec, fvec, out_ap, reads, writes, scr):
        P = self.P
        t1, ki, kf = scr
        P.op("dve", lambda e: e.tensor_scalar(t1[rows, 0:n], ps_ap, bvec, fvec, ALU.add, ALU.mult),
             reads=reads, writes=[t1.k()])
        P.op("dve", lambda e: e.tensor_copy(ki[rows, 0:n], t1[rows, 0:n]), reads=[t1.k()], writes=[ki.k()])
        P.op("dve", lambda e: e.tensor_copy(kf[rows, 0:n], ki[rows, 0:n]), reads=[ki.k()], writes=[kf.k()])
        P.op("dve", lambda e: e.tensor_tensor(t1[rows, 0:n], t1[rows, 0:n], kf[rows, 0:n], ALU.subtract),
             reads=[t1.k(), kf.k()], writes=[t1.k()])
        P.op("act", lambda e: e.activation(out_ap, t1[rows, 0:n], AF.Sin, scale=2 * math.pi * (1 - 1e-6)),
             reads=[t1.k()], writes=writes)

    def hyena(self, g, i, j, part):
        P, A, dr = self.P, self.A, self.dr
        n = g.name
        if part == 2:
            self.gather_and_project(g, i, j, None, "hy_wout", 2)
            return
        NT, L, NB, TB, C, SBW, NFB = g.NT, g.L, g.NB, g.TB, g.C, g.SBW, g.NFB
        nseq = g.nseq
        HP = NFB // 2
        if part == "post":
            yg = self._carry
            with self.scope():
                w = self._wout
                ygk = _KeyAll(yg, [(cb, s) for cb in range(g.ncb) for s in range(nseq)])
                self.resid_update(g, i, ygk, w, 1)
            self.cp("Wp")
            if self.use_barriers:
                self.P.barrier()
            self.A.pop()
            return
        if not g.samp:
            self.A.push()
            self._carry = A.alloc("yg", [g.ncb, NT], BF16)
        with self.scope():
            xg = A.alloc("xg", [g.ncb, NT])
            zT = A.alloc("zT", [nseq * NB, C], BF16)
            with self.scope():
                h = A.alloc("h", [8, NT], BF16)
                Wt = [A.alloc("win%d" % q, [8, 512], BF16) for q in range(2)]
                cw = A.alloc("convw", [g.ncb, 3, 4])
                U = [A.alloc("u%d" % q, [NT]) for q in range(2)]
                zc = A.alloc("zc", [NT], BF16)
                self.load("sp", cw[:], dr["hy_conv_" + n][j], [cw.k()], "ld_convw")
                for cb in range(min(2, g.ncb)):
                    self.load("pool", Wt[cb][:], dr["hy_win_" + n][j, cb], [Wt[cb].k()], "ld_win%d" % cb)
                self.norm_mod(g, i, h)
                for cb in range(g.ncb):
                    W = Wt[cb % 2]
                    if cb >= 2:
                        self.load("pool", W[:], dr["hy_win_" + n][j, cb], [W.k()], "ld_win%d" % (cb % 2))

                    def proj(gi):
                        b0 = self.bank4() if TB == 4 else self.bank()
                        for tb in range(TB):
                            for k in range(8):
                                self.mm(self.psb(b0 + tb), W[:, k, 128 * gi:128 * gi + 128],
                                        h[:, k, tb * 512:(tb + 1) * 512], k == 0, k == 7,
                                        [W.k(), h.k(k, tb)], [self.pk(b0 + tb)])
                        return b0

                    def conv(gi, b0, Ut):
                        pk = [self.pk(b0 + tb) for tb in range(TB)]
                        Pf = self.ps[:, b0 * 512:b0 * 512 + NT]
                        c_ = lambda q: cw[:, cb, gi, q:q + 1]
                        P.op("act", lambda e: e.activation(Ut[:], Pf, AF.Identity, bias=c_(3), scale=c_(1)),
                             reads=pk + [cw.k()], writes=[Ut.k()])
                        Pv = Pf.rearrange("p (s l) -> p s l", s=nseq)
                        Uv = Ut[:].rearrange("p (s l) -> p s l", s=nseq)
                        P.op("dve", lambda e: e.scalar_tensor_tensor(Uv[:, :, 1:L], Pv[:, :, 0:L - 1], c_(0), Uv[:, :, 1:L],
                                                                     ALU.mult, ALU.add),
                             reads=pk + [cw.k(), Ut.k()], writes=[Ut.k()])
                        P.op("dve", lambda e: e.scalar_tensor_tensor(Uv[:, :, 0:L - 1], Pv[:, :, 1:L], c_(2), Uv[:, :, 0:L - 1],
                                                                     ALU.mult, ALU.add),
                             reads=pk + [cw.k(), Ut.k()], writes=[Ut.k()])

                    bv = proj(2)
                    bx1 = proj(1)
                    conv(2, bv, U[0])
                    conv(1, bx1, U[1])
                    bx0 = proj(0)
                    P.op("dve", lambda e: e.tensor_tensor(zc[:], U[0][:], U[1][:], ALU.mult),
                         reads=[U[0].k(), U[1].k()], writes=[zc.k()])
                    bg = proj(3)
                    conv(0, bx0, U[0])
                    P.op("act", lambda e, bg=bg: e.activation(U[1][:], self.ps[:, bg * 512:bg * 512 + NT], AF.Silu),
                         reads=[self.pk(bg + tb) for tb in range(TB)], writes=[U[1].k()])
                    P.op("dve", lambda e, cb=cb: e.tensor_tensor(xg[:, cb, :], U[0][:], U[1][:], ALU.mult),
                         reads=[U[0].k(), U[1].k()], writes=[xg.k(cb)])
                    nblk = NT // 128
                    for b8 in range(0, nblk, 8):
                        nb_ = min(8, nblk - b8)
                        bt = self.bank()
                        pt = self.psb(bt).bitcast(BF16)
                        for q in range(nb_):
                            P.op("pe", lambda e, q=q, b8=b8, pt=pt: e.transpose(
                                pt[:, q * 128:(q + 1) * 128], zc[:, (b8 + q) * 128:(b8 + q + 1) * 128], self.ident[:]),
                                reads=[zc.k(), self.ident.k()], writes=[self.pk(bt)])
                        P.op("act", lambda e, b8=b8, nb_=nb_, pt=pt, cb=cb: e.activation(
                            zT[:, b8:b8 + nb_, cb * 128:(cb + 1) * 128],
                            pt[:, 0:nb_ * 128].rearrange("p (a b) -> p a b", a=nb_), AF.Copy),
                            reads=[self.pk(bt)], writes=[zT.k(cb, b8)])
            zTk = [zT.k(cb, b8) for cb in range(g.ncb) for b8 in range(0, NT // 128, 8)]
            self.cp("A")
            with self.scope():
                FC = 2 * C
                filtT = A.alloc("filtT", [NB, FC], BF16)
                rn2 = A.alloc("rn2", [FC])
                bias2 = A.alloc("bias2", [C])
                Y = A.alloc("Y", [NFB, nseq, C], BF16)
                yg = A.alloc("yg", [g.ncb, NT], BF16) if g.samp else self._carry
                with self.scope():
                    zemb = A.alloc("zemb", [L], BF16)
                    w1 = A.alloc("fw1", [64], BF16)
                    w2 = A.alloc("fw2", [64], BF16)
                    w3 = A.alloc("fw3", [FC], BF16)
                    fv = A.alloc("fvec", [4])
                    nd = A.alloc("ndelta", [C])
                    hd1 = A.alloc("hd1", [L], BF16)
                    hd2 = A.alloc("hd2", [L], BF16)
                    scr = (A.alloc("sn_t", [512]), A.alloc("sn_i", [512], I32), A.alloc("sn_f", [512]))
                    dec = [A.alloc("dec%d" % q, [C]) for q in range(2)]
                    absb = [A.alloc("absb%d" % q, [512], BF16) for q in range(3)]
                    self.load("sp", zemb[0:33], dr["zemb_" + n], [zemb.k()], "ld_zemb")
                    self.load("pool", w1[0:33], dr["hy_fw1"][j], [w1.k()], "ld_fw1")
                    self.load("pool", w2[0:64], dr["hy_fw2"][j], [w2.k()], "ld_fw2")
                    self.load("pool", w3[0:64], dr["hy_fw3_" + n][j], [w3.k()], "ld_fw3")
                    self.load("sp", fv[0:64], dr["hy_fvec"][j], [fv.k()], "ld_fvec")
                    self.load("sp", nd[:], dr["ndelta_" + n], [nd.k()], "ld_nd")
                    self.load("sp", bias2[:], dr["hy_fbias_" + n][j], [bias2.k()], "ld_fbias")
                    P.op("dve", lambda e: e.tensor_scalar(fv[0:64, 3:4], fv[0:64, 1:2], 1.0 / (2 * math.pi), None, ALU.mult),
                         reads=[fv.k()], writes=[fv.k()])
                    P.op("dve", lambda e: e.tensor_scalar(bias2[:], bias2[:], 2.0 / (2 * L), None, ALU.mult),
                         reads=[bias2.k()], writes=[bias2.k()])
                    R64 = slice(0, 64)
                    for tq in range(0, L, 512):
                        nn = min(512, L - tq)
                        b = self.bank()
                        self.mm(self.psb(b, R64, nn), w1[0:33, :], zemb[0:33, tq:tq + nn], True, True,
                                [w1.k(), zemb.k()], [self.pk(b)])
                        self.sin_layer(self.psb(b, R64, nn), R64, nn, fv[0:64, 0:1], fv[0:64, 3:4], hd1[0:64, tq:tq + nn],
                                       [self.pk(b), fv.k()], [hd1.k(tq)], scr)
                        b = self.bank()
                        self.mm(self.psb(b, R64, nn), w2[0:64, :], hd1[0:64, tq:tq + nn], True, True,
                                [w2.k(), hd1.k(tq)], [self.pk(b)])
                        self.sin_layer(self.psb(b, R64, nn), R64, nn, fv[0:64, 2:3], fv[0:64, 3:4], hd2[0:64, tq:tq + nn],
                                       [self.pk(b), fv.k()], [hd2.k(tq)], scr)
                    SW = min(512, C)
                    nseg = FC // SW
                    nbank = [self.bank() for _ in range(nseg)]
                    pend = []
                    na = 0

                    def nsum(cq_, blk_, ab_):
                        self.mm(self.psb(nbank[cq_], n=SW), self.ones[:], ab_[:, 0:SW], blk_ == 0, blk_ == NB - 1,
                                [self.ones.k(), ab_.k()], [self.pk(nbank[cq_])])
                    for blk in range(NB):
                        dc = dec[blk % 2]
                        P.op("act", lambda e, dc=dc, blk=blk: e.activation(dc[:], nd[:], AF.Exp, scale=self.tv[n][:, blk:blk + 1]),
                             reads=[nd.k(), self.tv[n].k()], writes=[dc.k()])
                        for cq in range(nseg):
                            b = self.bank()
                            while b in nbank:
                                b = self.bank()
                            self.mm(self.psb(b, n=SW), hd2[0:64, blk * 128:(blk + 1) * 128], w3[0:64, cq * SW:(cq + 1) * SW],
                                    True, True, [hd2.k((blk * 128) // 512 * 512), w3.k()], [self.pk(b)])
                            dcol = (cq * SW) % C
                            P.op("dve", lambda e, b=b, blk=blk, cq=cq, dc=dc, dcol=dcol: e.tensor_tensor(
                                filtT[:, blk, cq * SW:(cq + 1) * SW], self.psb(b, n=SW), dc[:, dcol:dcol + SW], ALU.mult),
                                reads=[self.pk(b), dc.k()], writes=[filtT.k(blk, cq)])
                            ab = absb[na % 3]
                            na += 1
                            P.op("act", lambda e, ab=ab, blk=blk, cq=cq: e.activation(
                                ab[:, 0:SW], filtT[:, blk, cq * SW:(cq + 1) * SW], AF.Abs),
                                reads=[filtT.k(blk, cq)], writes=[ab.k()])
                            pend.append((cq, blk, ab))
                            if len(pend) > 2:
                                nsum(*pend.pop(0))
                    for pp in pend:
                        nsum(*pp)
                    for cq in range(nseg):
                        P.op("act", lambda e, cq=cq: e.activation(rn2[:, cq * SW:(cq + 1) * SW], self.psb(nbank[cq], n=SW), AF.Ln),
                             reads=[self.pk(nbank[cq])], writes=[rn2.k(cq)])
                        P.op("act", lambda e, cq=cq: e.activation(rn2[:, cq * SW:(cq + 1) * SW], rn2[:, cq * SW:(cq + 1) * SW],
                                                                 AF.Exp, scale=-1.0),
                             reads=[rn2.k(cq)], writes=[rn2.k(cq)])
                        P.op("dve", lambda e, cq=cq: e.tensor_scalar(rn2[:, cq * SW:(cq + 1) * SW], rn2[:, cq * SW:(cq + 1) * SW],
                                                                    2.0 / (2 * L), None, ALU.mult),
                             reads=[rn2.k(cq)], writes=[rn2.k(cq)])
                self.cp("F")
                SW = min(512, C)
                nseg = FC // SW
                rnk = [rn2.k(cq) for cq in range(nseg)]
                ftk = lambda tb: [filtT.k(tb, q) for q in range(nseg)]
                with self.scope():
                    if g.samp:
                        Ft = [A.alloc("Ft%d" % q, [16, 128], BF16) for q in range(4)]
                    Hr = [A.alloc("Hr%d" % q, [SBW]) for q in range(2)]
                    Hb = [A.alloc("Hbt%d" % q, [SBW]) for q in range(2)]
                    Zs = [[A.alloc("Zs%d_%d" % (q, s), [SBW]) for s in range(nseq)] for q in range(2)]
                    tt = [A.alloc("yt%d" % q, [SBW]) for q in range(4)]
                    fx = A.alloc("fx", [SBW])
                    nf = 0
                    for sb in range(g.nsb):
                        c0 = sb * SBW
                        for bp in range(HP):
                            for half in range(2):
                                fb = bp + half * HP
                                if g.samp:
                                    F_ = Ft[nf % 4]
                                    nf += 1
                                    self.load("sp", F_[:], dr["Fs"][fb], [F_.k()], "ld_F%d" % ((nf - 1) % 4))
                                    Fap = lambda tb, F_=F_: F_[:, tb, :]
                                    Fk = F_.k()
                                else:
                                    Fap = lambda tb, fb=fb: self.Fp[:, tb, fb * 128:(fb + 1) * 128]
                                    Fk = self.Fp.k()
                                bHf, bHb = self.bank(), self.bank()
                                bZ = [self.bank() for _ in range(nseq)]
                                for tb in range(NB):
                                    st_, sp_ = tb == 0, tb == NB - 1
                                    self.mm(self.psb(bHf, n=SBW), Fap(tb), filtT[:, tb, c0:c0 + SBW], st_, sp_,
                                            [Fk] + ftk(tb), [self.pk(bHf)])
                                    self.mm(self.psb(bHb, n=SBW), Fap(tb), filtT[:, tb, C + c0:C + c0 + SBW], st_, sp_,
                                            [Fk] + ftk(tb), [self.pk(bHb)])
                                    for s in range(nseq):
                                        self.mm(self.psb(bZ[s], n=SBW), Fap(tb), zT[:, s * NB + tb, c0:c0 + SBW], st_, sp_,
                                                [Fk] + zTk, [self.pk(bZ[s])])
                                H_, T_ = Hr[half], Hb[half]
                                P.op("dve", lambda e, T_=T_, bHb=bHb: e.tensor_tensor(T_[:], self.psb(bHb, n=SBW), rn2[:, C + c0:C + c0 + SBW], ALU.mult),
                                     reads=[self.pk(bHb)] + rnk, writes=[T_.k()])
                                P.op("dve", lambda e, H_=H_, bHf=bHf: e.tensor_tensor(H_[:], self.psb(bHf, n=SBW), rn2[:, c0:c0 + SBW], ALU.mult),
                                     reads=[self.pk(bHf)] + rnk, writes=[H_.k()])
                                fix = bp == 0 and half == 1
                                if fix:
                                    P.op("dve", lambda e, H_=H_, T_=T_: e.tensor_tensor(fx[0:1, :], H_[0:1, :], T_[0:1, :], ALU.add),
                                         reads=[H_.k(), T_.k()], writes=[fx.k()])
                                P.op("dve", lambda e, H_=H_, T_=T_, half=half: e.tensor_tensor(
                                    H_[:], H_[:], T_[:], ALU.add if half == 0 else ALU.subtract),
                                    reads=[H_.k(), T_.k()], writes=[H_.k()])
                                if fix:
                                    P.op("dve", lambda e, H_=H_: e.tensor_copy(H_[0:1, :], fx[0:1, :]),
                                         reads=[fx.k(), H_.k()], writes=[H_.k()])
                                if half == 0:
                                    P.op("dve", lambda e, H_=H_: e.tensor_tensor(H_[:], H_[:], bias2[:, c0:c0 + SBW], ALU.add),
                                         reads=[H_.k(), bias2.k()], writes=[H_.k()])
                                elif bp == 0:
                                    P.op("dve", lambda e, H_=H_: e.tensor_tensor(H_[0:1, :], H_[0:1, :], bias2[0:1, c0:c0 + SBW], ALU.add),
                                         reads=[H_.k(), bias2.k()], writes=[H_.k()])
                                if bp == 0:
                                    P.op("dve", lambda e, H_=H_: e.tensor_scalar(H_[0:1, :], H_[0:1, :], 0.5, None, ALU.mult),
                                         reads=[H_.k()], writes=[H_.k()])
                                for s in range(nseq):
                                    Z_ = Zs[half][s]
                                    P.op("act", lambda e, Z_=Z_, s=s, bZ=bZ: e.activation(Z_[:], self.psb(bZ[s], n=SBW), AF.Copy),
                                         reads=[self.pk(bZ[s])], writes=[Z_.k()])
                            for s in range(nseq):
                                Zr, Zi = Zs[0][s], Zs[1][s]
                                Yre = Y[:, bp, s, c0:c0 + SBW]
                                Yim = Y[:, bp + HP, s, c0:c0 + SBW]
                                rk_ = [Zr.k(), Zi.k(), Hr[0].k(), Hr[1].k()]
                                P.op("dve", lambda e, Zr=Zr: e.tensor_tensor(tt[0][:], Zr[:], Hr[0][:], ALU.mult), reads=rk_, writes=[tt[0].k()])
                                P.op("dve", lambda e, Zi=Zi: e.tensor_tensor(tt[1][:], Zi[:], Hr[1][:], ALU.mult), reads=rk_, writes=[tt[1].k()])
                                P.op("dve", lambda e, Yre=Yre: e.tensor_tensor(Yre, tt[0][:], tt[1][:], ALU.subtract),
                                     reads=[tt[0].k(), tt[1].k()], writes=[Y.k(bp, s, sb)])
                                P.op("pool", lambda e, Zr=Zr: e.tensor_tensor(tt[2][:], Zr[:], Hr[1][:], ALU.mult), reads=rk_, writes=[tt[2].k()])
                                P.op("pool", lambda e, Zi=Zi: e.tensor_tensor(tt[3][:], Zi[:], Hr[0][:], ALU.mult), reads=rk_, writes=[tt[3].k()])
                                P.op("pool", lambda e, Yim=Yim: e.tensor_tensor(Yim, tt[2][:], tt[3][:], ALU.add),
                                     reads=[tt[2].k(), tt[3].k()], writes=[Y.k(bp + HP, s, sb)])
                                if bp == 0:
                                    P.op("dve", lambda e, Zr=Zr, s=s: e.tensor_tensor(Y[0:1, 0, s, c0:c0 + SBW], Zr[0:1, :], Hr[0][0:1, :], ALU.mult),
                                         reads=rk_ + [Y.k(bp, s, sb)], writes=[Y.k(bp, s, sb)])
                                    P.op("pool", lambda e, Zi=Zi, s=s: e.tensor_tensor(Y[0:1, HP, s, c0:c0 + SBW], Zi[0:1, :], Hr[1][0:1, :], ALU.mult),
                                         reads=rk_ + [Y.k(bp + HP, s, sb)], writes=[Y.k(bp + HP, s, sb)])
                self.cp("D")
                with self.scope():
                    if g.samp:
                        Gt = [A.alloc("Gt%d" % q, [8, 512], BF16) for q in range(3)]
                        ng = 0
                        for tq in range(4):
                            bo = [self.bank() for _ in range(g.ncb)]
                            for fg in range(4):
                                G_ = Gt[ng % 3]
                                ng += 1
                                self.load("sp", G_[:], dr["Gs"][tq, fg], [G_.k()], "ld_G%d" % ((ng - 1) % 3))
                                for q in range(8):
                                    fb = fg * 8 + q
                                    for cb in range(g.ncb):
                                        self.mm(self.psb(bo[cb]), Y[:, fb, 0, cb * 128:(cb + 1) * 128], G_[:, q, :],
                                                fb == 0, fb == NFB - 1,
                                                [G_.k(), Y.k(fb, 0, 0)], [self.pk(bo[cb])])
                            for cb in range(g.ncb):
                                P.op("dve", lambda e, cb=cb, tq=tq, bo=bo: e.tensor_tensor(
                                    yg[:, cb, tq * 512:(tq + 1) * 512], self.psb(bo[cb]), xg[:, cb, tq * 512:(tq + 1) * 512], ALU.mult),
                                    reads=[self.pk(bo[cb]), xg.k(cb)], writes=[yg.k(cb, tq)])
                    else:
                        for cb in range(g.ncb):
                            sb = (cb * 128) // SBW
                            for s in range(nseq):
                                b = self.bank()
                                for fb in range(NFB):
                                    self.mm(self.psb(b, n=L), Y[:, fb, s, cb * 128:(cb + 1) * 128], self.Gp[:, fb, :],
                                            fb == 0, fb == NFB - 1, [self.Gp.k(), Y.k(fb, s, sb)], [self.pk(b)])
                                P.op("dve", lambda e, cb=cb, s=s, b=b: e.tensor_tensor(
                                    yg[:, cb, s * L:(s + 1) * L], self.psb(b, n=L), xg[:, cb, s * L:(s + 1) * L], ALU.mult),
                                    reads=[self.pk(b), xg.k(cb)], writes=[yg.k(cb, s)])
                self.cp("E")
                if g.samp:
                    self.gather_and_project(g, i, j, yg, "hy_wout", 1)
                    self.cp("W")

    def mla(self, g, i, j, part):
        P, A, dr = self.P, self.A, self.dr
        n = g.name
        if part == 2:
            self.gather_and_project(g, i, j, None, "mla_wo", 2)
            return
        NT, TB, nh, nseq, L = g.NT, g.TB, g.nh, g.nseq, g.L
        NK = NT + (PAST if g.samp else 0)
        KTB = NK // 512
        NKB = NK // 128
        HG = nh * 64
        ncg = HG // 128
        scale = 1.0 / math.sqrt(QK_NOPE + QK_ROPE)
        R64, R96a, R96 = slice(0, 64), slice(0, 96), slice(64, 96)
        if part == "post":
            og = self._carry
            with self.scope():
                w = self._wout
                ogk = _KeyAll(og, [(c, 0, par) for c in range(ncg) for par in range(2)])
                self.resid_update(g, i, ogk, w, 1)
            if self.use_barriers:
                self.P.barrier()
            self.A.pop()
            return
        if not g.samp:
            self.A.push()
            self._carry = A.alloc("og", [ncg, NT], BF16)
        with self.scope():
            sg = A.alloc("sg", [ncg, NT], BF16)
            qn = A.alloc("qn", [3, NT], BF16)
            ckvT = A.alloc("ckvT", [2, NK], BF16)
            kpeT = A.alloc("kpeT", [NK], BF16)
            with self.scope():
                hb = [A.alloc("h%d" % q, [8, 512], BF16) for q in range(2)]
                sc = self.norm_scratch()
                Wa = A.alloc("wa", [8, 832], BF16)
                Wg = A.alloc("wg", [8, HG], BF16)
                self.load("pool", Wa[:], dr["mla_wa"][j], [Wa.k()], "ld_wa")
                self.load("pool", Wg[:], dr["mla_wg_" + n][j], [Wg.k()], "ld_wg")
                sqb = A.alloc("sqb", [3, 512], BF16)
                rq = A.alloc("rq", [512])
                if g.samp:
                    rope = [A.alloc("rope%d" % q, [2, 512]) for q in range(2)]
                    kt1 = A.alloc("kt1", [512])
                    kt2 = A.alloc("kt2", [512])
                    self.load("pool", ckvT[:, :, NT:NK], dr["cckv"][j], [ckvT.k(0, TB), ckvT.k(1, TB)], "ld_cckv")
                    self.load("pool", kpeT[64:96, NT:NK], dr["ckpe"][j, 64:96], [kpeT.k(TB)], "ld_ckpe")
                else:
                    cko = A.alloc("cko", [2, 512])
                    kpo = A.alloc("kpo", [512])
                for tb in range(TB):
                    ts = slice(tb * 512, (tb + 1) * 512)
                    h = hb[tb % 2]
                    self.norm_tb(g, i, tb, lambda k: h[:, k, :], lambda k: [h.k(k)], sc)
                    if g.samp:
                        rp = rope[tb % 2]
                        self.load("sp", rp[:], dr["rope"][:, :, ts], [rp.k()], "ld_rope%d" % (tb % 2))

                    def proj(c0, M):
                        b = self.bank()
                        for k in range(8):
                            self.mm(self.psb(b, slice(0, M)), Wa[:, k, c0:c0 + M], h[:, k, :], k == 0, k == 7,
                                    [Wa.k(), h.k(k)], [self.pk(b)])
                        return b

                    def rms(banks, nch, inv_n, wv, dst, dkey, extra=None):
                        for m, b in enumerate(banks):
                            P.op("act", lambda e, m=m, b=b: e.activation(sqb[:, m, :], self.psb(b), AF.Square),
                                 reads=[self.pk(b)], writes=[sqb.k(m)])
                        bs = self.bank()
                        for m in range(nch):
                            self.mm(self.psb(bs), self.ones[:], sqb[:, m, :], m == 0, m == nch - 1,
                                    [self.ones.k(), sqb.k(m)], [self.pk(bs)])
                        P.op("act", lambda e: e.activation(rq[:], self.psb(bs), AF.Ln, bias=self.cst[:, 0:1], scale=inv_n),
                             reads=[self.pk(bs), self.cst.k()], writes=[rq.k()])
                        P.op("act", lambda e: e.activation(rq[:], rq[:], AF.Exp, scale=-0.5), reads=[rq.k()], writes=[rq.k()])
                        for m, b in enumerate(banks):
                            P.op("dve", lambda e, m=m, b=b: e.scalar_tensor_tensor(
                                dst[:, m, ts], self.psb(b), wv[:, j, m:m + 1], rq[:], ALU.mult, ALU.mult),
                                reads=[self.pk(b), rq.k(), wv.k()], writes=[dkey(m)])
                            if extra is not None:
                                extra(m, b)

                    bq = [proj(128 * m, 128) for m in range(3)]
                    bkv = [proj(384 + 128 * m, 128) for m in range(2)]
                    rms(bq, 3, 1.0 / Q_LORA, self.qn_w, qn, lambda m: qn.k(m, tb))
                    bk = proj(640, 96)
                    if g.samp:
                        bk2 = proj(736, 96)
                    if g.samp:
                        rms(bkv, 2, 1.0 / KV_LORA, self.kvn_w, ckvT, lambda m: ckvT.k(m, tb))
                    else:
                        def extra(m, b):
                            P.op("dve", lambda e: e.scalar_tensor_tensor(
                                cko[:, m, :], self.psb(b), self.kvn_w[:, j, m:m + 1], rq[:], ALU.mult, ALU.mult),
                                reads=[self.pk(b), rq.k(), self.kvn_w.k()], writes=[cko.k(m)])
                        rms(bkv, 2, 1.0 / KV_LORA, self.kvn_w, ckvT, lambda m: ckvT.k(m, tb), extra)
                        self.load("sp", dr["ckv_out"][j], cko[:], ["ckv_out%d" % j], "st_ckv%d" % j, reads=[cko.k(0), cko.k(1)])
                        self.outs.append("st_ckv%d" % j)
                    if g.samp:
                        P.op("dve", lambda e: e.tensor_tensor(kt1[R96, :], self.psb(bk, R96), rp[R96, 0, :], ALU.mult),
                             reads=[self.pk(bk), rp.k()], writes=[kt1.k()])
                        P.op("dve", lambda e: e.tensor_tensor(kt2[R96, :], self.psb(bk2, R96), rp[R96, 1, :], ALU.mult),
                             reads=[self.pk(bk2), rp.k()], writes=[kt2.k()])
                        P.op("dve", lambda e: e.tensor_tensor(kpeT[R96, ts], kt1[R96, :], kt2[R96, :], ALU.add),
                             reads=[kt1.k(), kt2.k()], writes=[kpeT.k(tb)])
                    else:
                        P.op("act", lambda e: e.activation(kpeT[R96, ts], self.psb(bk, R96), AF.Copy),
                             reads=[self.pk(bk)], writes=[kpeT.k(tb)])
                        P.op("act", lambda e: e.activation(kpo[R96, :], self.psb(bk, R96), AF.Copy),
                             reads=[self.pk(bk)], writes=[kpo.k()])
                        self.load("sp", dr["kpe_out"][j], kpo[R96, :], ["kpe_out%d" % j], "st_kpe%d" % j, reads=[kpo.k()])
                        self.outs.append("st_kpe%d" % j)
                    for m in range(ncg):
                        b = self.bank()
                        for k in range(8):
                            self.mm(self.psb(b), Wg[:, k, 128 * m:128 * m + 128], h[:, k, :], k == 0, k == 7,
                                    [Wg.k(), h.k(k)], [self.pk(b)])
                        P.op("act", lambda e, b=b, m=m: e.activation(sg[:, m, ts], self.psb(b), AF.Silu),
                             reads=[self.pk(b)], writes=[sg.k(m, tb)])
            self.cp("P1")
            QT = A.alloc("QT", [nh, NT], BF16)
            KT = A.alloc("KT", [nh, NK], BF16)
            Va = A.alloc("Vaug", [NKB, nh, 128], BF16)
            with self.scope():
                if g.samp:
                    Wq = A.alloc("wq", [3, nh, 2, 96], BF16)
                    rope = [A.alloc("rope%d" % q, [2, 512]) for q in range(2)]
                    qt1 = A.alloc("qt1", [512])
                    qt2 = A.alloc("qt2", [512])
                else:
                    Wq = A.alloc("wq", [3, nh, 96], BF16)
                Wk = A.alloc("wk", [2, nh, 64], BF16)
                Wv = A.alloc("wv", [2, HG], BF16)
                self.load("pool", Wq[:], dr["mla_wq_" + n][j], [Wq.k()], "ld_wq")
                self.load("pool", Wk[:], dr["mla_wk_" + n][j], [Wk.k()], "ld_wk")
                self.load("pool", Wv[:], dr["mla_wv_" + n][j], [Wv.k()], "ld_wv")
                P.op("pool", lambda e: e.memset(Va[:], 1.0), writes=[Va.k()])
                for tb in range(TB):
                    ts = slice(tb * 512, (tb + 1) * 512)
                    if g.samp:
                        rp = rope[tb % 2]
                        self.load("sp", rp[:], dr["rope"][:, :, ts], [rp.k()], "ld_rope%d" % (tb % 2))
                    for hh in range(nh):
                        b = self.bank()
                        for k in range(3):
                            lw = Wq[:, k, hh, 0, :] if g.samp else Wq[:, k, hh, :]
                            self.mm(self.psb(b, R96a), lw, qn[:, k, ts], k == 0, k == 2,
                                    [Wq.k(), qn.k(k, tb)], [self.pk(b)])
                        if g.samp:
                            b2 = self.bank()
                            for k in range(3):
                                self.mm(self.psb(b2, R96a), Wq[:, k, hh, 1, :], qn[:, k, ts], k == 0, k == 2,
                                        [Wq.k(), qn.k(k, tb)], [self.pk(b2)])
                            P.op("dve", lambda e, b=b, hh=hh, ts=ts: e.tensor_copy(QT[R64, hh, ts], self.psb(b, R64)),
                                 reads=[self.pk(b)], writes=[QT.k(hh, tb, 0)])
                            P.op("dve", lambda e, b=b, rp=rp: e.tensor_tensor(qt1[R96, :], self.psb(b, R96), rp[R96, 0, :], ALU.mult),
                                 reads=[self.pk(b), rp.k()], writes=[qt1.k()])
                            P.op("dve", lambda e, b2=b2, rp=rp: e.tensor_tensor(qt2[R96, :], self.psb(b2, R96), rp[R96, 1, :], ALU.mult),
                                 reads=[self.pk(b2), rp.k()], writes=[qt2.k()])
                            P.op("dve", lambda e, hh=hh, ts=ts: e.tensor_tensor(QT[R96, hh, ts], qt1[R96, :], qt2[R96, :], ALU.add),
                                 reads=[qt1.k(), qt2.k()], writes=[QT.k(hh, tb, 1)])
                        else:
                            P.op("act", lambda e, b=b, hh=hh, ts=ts: e.activation(QT[R96a, hh, ts], self.psb(b, R96a), AF.Copy),
                                 reads=[self.pk(b)], writes=[QT.k(hh, tb, 0), QT.k(hh, tb, 1)])
                import os
                if os.environ.get("K_DBG_P2STOP") == "q":
                    P.muted = True
                for tb in range(KTB):
                    ts = slice(tb * 512, (tb + 1) * 512)
                    for hh in range(nh):
                        b = self.bank()
                        for k in range(2):
                            self.mm(self.psb(b, R64), Wk[:, k, hh, :], ckvT[:, k, ts], k == 0, k == 1,
                                    [Wk.k(), ckvT.k(k, tb)], [self.pk(b)])
                        P.op("dve", lambda e, b=b, hh=hh, ts=ts: e.tensor_copy(KT[R64, hh, ts], self.psb(b, R64)),
                             reads=[self.pk(b)], writes=[KT.k("n", tb, hh)])
                        P.op("pool", lambda e, hh=hh, ts=ts: e.tensor_copy(KT[R96, hh, ts], kpeT[R96, ts]),
                             reads=[kpeT.k(tb)], writes=[KT.k("pe", tb, hh)])
                if os.environ.get("K_DBG_P2STOP") == "k":
                    P.muted = True
                for kb in range(min(NKB, int(os.environ.get("K_DBG_NKB", NKB)))):
                    for c0 in range(0, HG, 512):
                        w_ = min(512, HG - c0)
                        b = self.bank()
                        for k in range(2):
                            self.mm(self.psb(b, n=w_), ckvT[:, k, kb * 128:(kb + 1) * 128], Wv[:, k, c0:c0 + w_],
                                    k == 0, k == 1, [Wv.k(), ckvT.k(k, kb // 4)], [self.pk(b)])
                        nhh = w_ // 64
                        h0 = c0 // 64
                        for q in range(nhh):
                            hh = h0 + q
                            par = hh % 2
                            src = self.psb(b, n=64, off=q * 64)
                            dst = Va[:, kb, hh, par * 64:(par + 1) * 64]
                            if kb % 2 == 0 or os.environ.get("K_DBG_VACT"):
                                P.op("act", lambda e, src=src, dst=dst: e.activation(dst, src, AF.Copy),
                                     reads=[self.pk(b), Va.k()], writes=[Va.k(kb, hh)])
                            else:
                                P.op("dve", lambda e, src=src, dst=dst: e.tensor_copy(dst, src),
                                     reads=[self.pk(b), Va.k()], writes=[Va.k(kb, hh)])
            self.cp("P2")
            with self.scope():
                og = A.alloc("og", [ncg, NT], BF16) if g.samp else self._carry
                NPT = 3
                PT = [A.alloc("PT%d" % q, [1024], BF16) for q in range(NPT)]
                NIF = 2 if g.samp else 4
                rdb = [A.alloc("rdb%d" % q, [512], BF16) for q in range(NIF)]
                if g.samp:
                    ot = [A.alloc("ot", [512])] * NIF
                    rdf = [A.alloc("rdf", [512])] * NIF
                else:
                    ot = [A.alloc("ot%d" % q, [512]) for q in range(NIF)]
                    rdf = [A.alloc("rdf%d" % q, [512]) for q in range(NIF)]
                for q in range(NIF):
                    P.op("dve", lambda e, q=q: e.memset(rdb[q][:], 0.0), writes=[rdb[q].k()])
                QB = min(512, L)
                npt = 0
                nat = 0
                nsp = 0
                tails = []
                DEFER = 1 if g.samp else 2

                def emit_tail(tl):
                    bO, rd, o_, rf, Ro, Rd, hh, qs, qtb, par = tl
                    if g.samp:
                        P.op("dve", lambda e: e.reciprocal(rf[Rd, 0:QB], self.psb(bO, Rd, QB)),
                             reads=[self.pk(bO)], writes=[rf.k()])
                        P.op("dve", lambda e: e.tensor_copy(rd[Rd, 0:QB], rf[Rd, 0:QB]),
                             reads=[rf.k()], writes=[rd.k()])
                    else:
                        P.op("act", lambda e: e.activation(rf[Rd, 0:QB], self.psb(bO, Rd, QB), AF.Ln),
                             reads=[self.pk(bO)], writes=[rf.k()])
                        P.op("act", lambda e: e.activation(rd[Rd, 0:QB], rf[Rd, 0:QB], AF.Exp, scale=-1.0),
                             reads=[rf.k()], writes=[rd.k()])
                    nonlocal nsp
                    if g.samp:
                        bR = 2 * (nsp % 3)
                        nsp += 1
                    else:
                        bR = 7
                    self.mm(self.psb(bR, n=QB), self.swap[:], rd[:, 0:QB], True, True,
                            [self.swap.k(), rd.k()], [self.pk(bR)])
                    P.op("dve", lambda e: e.tensor_tensor(o_[Ro, 0:QB], self.psb(bO, Ro, QB), sg[Ro, hh // 2, qs], ALU.mult),
                         reads=[self.pk(bO), sg.k(hh // 2, qtb), rf.k()], writes=[o_.k()])
                    P.op("dve", lambda e: e.tensor_tensor(og[Ro, hh // 2, qs], o_[Ro, 0:QB], self.psb(bR, Ro, QB), ALU.mult),
                         reads=[self.pk(bR), o_.k()], writes=[og.k(hh // 2, qtb, par)])
                for s in range(nseq):
                    if g.samp:
                        kbs = list(range(NKB))
                    else:
                        kbs = [s * (L // 128) + q for q in range(L // 128)]
                    npair = len(kbs) // 2
                    for hh in range(nh):
                        par = hh % 2
                        Ro = slice(par * 64, par * 64 + 64)
                        Rd = slice((1 - par) * 64, (1 - par) * 64 + 64)
                        for qb in range(L // QB):
                            q0 = s * L + qb * QB
                            qs = slice(q0, q0 + QB)
                            qtb = q0 // 512
                            bO = (6 + (nat % 2)) if g.samp else (3 + (nat % 4))
                            pend = []
                            for ip in range(npair):
                                if g.samp:
                                    b0 = 2 * (nsp % 3)
                                    sb_ = [(b0, 0), (b0 + 1, 0)]
                                    Sv = self.ps[:, b0 * 512:b0 * 512 + 1024].rearrange("p (k n) -> p k n", k=2)[:, :, 0:QB]
                                    spk = [self.pk(b0), self.pk(b0 + 1)]
                                else:
                                    b0 = nsp % 3
                                    sb_ = [(b0, 0), (b0, QB)]
                                    Sv = self.ps[:, b0 * 512:b0 * 512 + 2 * QB].rearrange("p (k n) -> p k n", k=2)
                                    spk = [self.pk(b0)]
                                nsp += 1
                                for u in range(2):
                                    kb = kbs[2 * ip + u]
                                    ktb = kb // 4
                                    self.mm(self.psb(sb_[u][0], n=QB, off=sb_[u][1]), KT[0:96, hh, kb * 128:(kb + 1) * 128],
                                            QT[0:96, hh, qs], True, True,
                                            [KT.k("n", ktb, hh), KT.k("pe", ktb, hh), QT.k(hh, qtb, 0), QT.k(hh, qtb, 1)],
                                            [self.pk(sb_[u][0])])
                                pt = PT[npt % NPT]
                                npt += 1
                                Pv = pt[:, 0:2 * QB].rearrange("p (k n) -> p k n", k=2)
                                P.op("act", lambda e, Sv=Sv, Pv=Pv: e.activation(Pv, Sv, AF.Exp, scale=scale),
                                     reads=spk, writes=[pt.k()])
                                pend.append((ip, kbs[2 * ip], kbs[2 * ip + 1], hh, pt))
                                if len(pend) > 2:
                                    self._pv2(bO, Va, pend.pop(0), QB, npair)
                            for pp in pend:
                                self._pv2(bO, Va, pp, QB, npair)
                            tails.append((bO, rdb[nat % NIF], ot[nat % NIF], rdf[nat % NIF], Ro, Rd, hh, qs, qtb, par))
                            nat += 1
                            if len(tails) > DEFER:
                                emit_tail(tails.pop(0))
                for tl in tails:
                    emit_tail(tl)
                self.cp("ATT")
                if g.samp:
                    P.op("sp", lambda e: e.dma_start(out=dr["cc_in%d" % i].rearrange("(c p) t -> p c t", p=128), in_=og[:]),
                         reads=[og.k(c, tb, par) for c in range(ncg) for tb in range(TB) for par in range(2)],
                         writes=["cc_in%d" % i], dma_sem="st_cc%d" % i)
                    P.op("pool", lambda e: e.collective_compute("AllGather", ALU.bypass, replica_groups=GROUP4,
                                                                ins=[dr["cc_in%d" % i]], outs=[dr["cc_out%d" % i]]),
                         reads=["cc_in%d" % i], writes=["cc_out%d" % i], dma_sem="cc%d" % i, inc=1, persist=True)

    def _pv2(self, bO, Va, prev, QB, npair):
        ip, kb0, kb1, hh, pt = prev
        for u, kb in enumerate((kb0, kb1)):
            self.mm(self.psb(bO, n=QB), Va[:, kb, hh, :], pt[:, u * QB:(u + 1) * QB], ip == 0 and u == 0,
                    ip == npair - 1 and u == 1, [Va.k(kb, hh), Va.k(), pt.k()], [self.pk(bO)])

    def _pv(self, bO, Va, prev, QB, nk):
        ix, kb, hh, pt = prev
        self.mm(self.psb(bO, n=QB), Va[:, kb, hh, :], pt[:, 0:QB], ix == 0, ix == nk - 1,
                [Va.k(kb, hh), Va.k(), pt.k()], [self.pk(bO)])


class _KeyAll:
    def __init__(self, t, subs):
        self.t = t
        self.subs = subs
        self.ap = t.ap

    def __getitem__(self, idx):
        return self.t.ap[idx]

    def k(self, *sub):
        if not sub:
            return self.t.k()
        kk = sub[0]
        return _MultiKey([self.t.k(*s) for s in self.subs if s[0] == kk])


class _MultiKey(list):
    pass


_orig_op = Prog.op


def _op_flat(self, eng, fn, reads=(), writes=(), **kw):
    def flat(ks):
        out = []
        for k in ks:
            if isinstance(k, _MultiKey):
                out.extend(k)
            else:
                out.append(k)
        return out
    return _orig_op(self, eng, fn, flat(reads), flat(writes), **kw)


Prog.op = _op_flat


def _final(self, g, oname):
    P, A = self.P, self.A
    x = self.x[g.name]
    dst = self.dr[oname]
    with self.scope():
        yo = [A.alloc("yo%d" % q, [8, 512]) for q in range(2)]
        sq = [A.alloc("fsq%d" % q, [8, 512], BF16) for q in range(2)]
        rs = [A.alloc("frs%d" % q, [512]) for q in range(2)]
        for tb in range(g.NT // 512):
            y_, s_, r_ = yo[tb % 2], sq[tb % 2], rs[tb % 2]
            ts = slice(tb * 512, (tb + 1) * 512)
            P.op("act", lambda e, s_=s_, ts=ts: e.activation(s_[:], x[:, :, ts], AF.Square),
                 reads=[x.k(k, tb) for k in range(8)], writes=[s_.k()])
            b = self.bank()
            for k in range(8):
                self.mm(self.psb(b), self.ones[:], s_[:, k, :], k == 0, k == 7, [self.ones.k(), s_.k()], [self.pk(b)])
            P.op("act", lambda e, b=b, r_=r_: e.activation(r_[:], self.psb(b), AF.Ln, bias=self.cst[:, 0:1], scale=1.0 / D),
                 reads=[self.pk(b), self.cst.k()], writes=[r_.k()])
            P.op("act", lambda e, r_=r_: e.activation(r_[:], r_[:], AF.Exp, scale=-0.5), reads=[r_.k()], writes=[r_.k()])
            for k in range(8):
                P.op("dve", lambda e, k=k, ts=ts, y_=y_, r_=r_: e.scalar_tensor_tensor(
                    y_[:, k, :], x[:, k, ts], self.fnorm[:, k:k + 1], r_[:], ALU.mult, ALU.mult),
                    reads=[x.k(k, tb), r_.k(), self.fnorm.k(), y_.k()], writes=[y_.k()])
            self.load("sp", dst[:, :, ts], y_[:], [(oname, tb)], "st_%s%d" % (oname, tb % 2), reads=[y_.k()])
            if ("st_%s%d" % (oname, tb % 2)) not in self.outs:
                self.outs.append("st_%s%d" % (oname, tb % 2))


Builder.final = _final


def _prep(inp):
    C = _consts()
    f = _f32
    common = {}
    common["cond"] = f(_chunk(np.stack([inp["c_ctx"], inp["c"][0], inp["c"][1]], axis=1)))
    common["normw"] = f(_chunk(inp["norm_w"].T))
    common["normw"] = f(common["normw"].transpose(0, 2, 1))
    common["fnorm"] = f(_chunk(inp["final_norm"][:, None])[:, :, 0])
    for k in ("ident", "ones", "swap", "Fs", "Gs", "Fp", "Gp", "zemb_s", "zemb_p", "tv_s", "tv_p", "rope"):
        common[k] = C[k]
    common["hy_fw1"] = f(inp["hy_f_w1"])
    common["hy_fw2"] = f(inp["hy_f_w2"])
    fv = np.zeros((2, 64, 4), np.float32)
    fv[:, :, 0] = inp["hy_f_b1"]
    fv[:, :, 1] = inp["hy_f_freq"]
    fv[:, :, 2] = inp["hy_f_b2"]
    common["hy_fvec"] = fv
    common["hy_wout"] = f(np.stack([_chunk(inp["hy_w_out"][j]) for j in range(2)]))
    common["mla_wo"] = f(np.stack([_chunk(inp["mla_w_o"][j]) for j in range(2)]))
    common["mla_qn"] = f(_chunk(inp["mla_q_norm"].T))
    common["mla_qn"] = f(common["mla_qn"].transpose(0, 2, 1))
    common["mla_kvn"] = f(_chunk(inp["mla_kv_norm"].T).transpose(0, 2, 1))
    sw = C["ropesw"]
    wa = []
    for j in range(2):
        w = inp["mla_w_in"][j]
        kpe = w[:, 640:672]
        z64 = np.zeros((D, 64), np.float32)
        wa.append(_chunk(np.concatenate([w[:, :640], z64, kpe, z64, kpe[:, sw]], axis=1)))
    common["mla_wa"] = f(np.stack(wa))

    def hy_group(chs, tag, out):
        cbs = [chs[q * 128:(q + 1) * 128] for q in range(len(chs) // 128)]
        win = np.zeros((2, len(cbs), 128, 8, 512), np.float32)
        cv = np.zeros((2, 128, len(cbs), 3, 4), np.float32)
        for j in range(2):
            w = inp["hy_w_in"][j]
            for q, cb in enumerate(cbs):
                cols = np.concatenate([gi * D + cb for gi in range(4)])
                win[j, q] = _chunk(w[:, cols])
                for gi in range(3):
                    cv[j, :, q, gi, 0:3] = inp["hy_conv_w"][j][:, gi * D + cb].T
                    cv[j, :, q, gi, 3] = inp["hy_conv_b"][j][gi * D + cb]
        out["hy_win_" + tag] = win
        out["hy_conv_" + tag] = cv
        out["hy_fw3_" + tag] = f(np.stack([inp["hy_f_w3"][j][:, np.concatenate([chs, D + chs])] for j in range(2)]))
        out["hy_fbias_" + tag] = f(np.stack([np.broadcast_to(inp["hy_f_bias"][j][chs], (128, len(chs))) for j in range(2)]))
        out["ndelta_" + tag] = f(np.broadcast_to(C["ndelta"][chs], (128, len(chs))))

    def mla_group(heads, tag, out):
        nh = len(heads)
        gcols = np.concatenate([672 + h * 64 + np.arange(64) for h in heads])
        out["mla_wg_" + tag] = f(np.stack([_chunk(inp["mla_w_in"][j][:, gcols]) for j in range(2)]))
        wk = np.zeros((2, 128, 2, nh, 64), np.float32)
        wv = np.zeros((2, 128, 2, nh * 64), np.float32)
        for j in range(2):
            w = inp["mla_w_kvb"][j].reshape(KV_LORA, NH, 128)
            wk[j] = _chunk(w[:, heads, :64])
            wv[j] = _chunk(w[:, heads, 64:].reshape(KV_LORA, nh * 64))
        out["mla_wk_" + tag] = wk
        out["mla_wv_" + tag] = wv
        wq = np.stack([inp["mla_w_qb"][j].reshape(Q_LORA, NH, 96)[:, heads] for j in range(2)])
        if tag == "p":
            out["mla_wq_p"] = f(np.stack([_chunk(wq[j]) for j in range(2)]))
        else:
            wsw = np.concatenate([wq[..., :64], wq[..., 64:][..., sw]], axis=-1)
            both = np.stack([wq, wsw], axis=3)
            out["mla_wq_s"] = f(np.stack([_chunk(both[j]) for j in range(2)]))

    hy_group(np.arange(D), "p", common)
    mla_group(list(range(NH)), "p", common)
    maps = []
    for c in range(NCORES):
        g, r = c // 4, c % 4
        m = dict(common)
        xp = inp["x_prompt"][2 * c:2 * c + 2].reshape(512, D)
        m["xp"] = f(_chunk(xp.T))
        m["xs"] = f(_chunk(inp["x_sample"][g].T))
        m["ada_w"] = f(np.stack([_chunk(inp["ada_w"][i][:, 768 * r:768 * (r + 1)]) for i in range(4)]))
        m["ada_b"] = f(inp["ada_b"][:, 768 * r:768 * (r + 1)].reshape(4, 6, 128).transpose(2, 0, 1))
        sel = np.zeros((128, 2), np.float32)
        sel[:, g] = 1.0
        m["sel"] = sel
        hy_group(np.arange(256 * r, 256 * (r + 1)), "s", m)
        mla_group(list(range(4 * r, 4 * r + 4)), "s", m)
        m["cckv"] = f(np.stack([_chunk(inp["cache_ckv"][g, j].T) for j in range(2)]))
        ck = np.zeros((2, 128, 512), np.float32)
        for j in range(2):
            ck[j, 64:96] = inp["cache_kpe"][g, j].T
        m["ckpe"] = ck
        maps.append(m)
    return maps


_NC_CACHE = {}


def _get_nc(depth=DEPTH):
    if depth not in _NC_CACHE:
        _NC_CACHE[depth] = Builder(depth).build()
    return _NC_CACHE[depth]


def kernel(**inputs):
    inp = {k: np.asarray(v) for k, v in inputs.items()}
    maps = _prep(inp)
    nc = _get_nc()
    res = run_bass_kernel_spmd(nc, maps, core_ids=list(range(NCORES)))
    R = res.results
    y_prompt = np.zeros((BATCH, SEQ, D), np.float32)
    state_ckv = np.zeros((BATCH, 2, SEQ, KV_LORA), np.float32)
    state_kpe = np.zeros((BATCH, 2, SEQ, QK_ROPE), np.float32)
    y_sample = np.zeros((DEC_BATCH, DEC_SEQ, D), np.float32)
    for c in range(NCORES):
        yp = np.asarray(R[c]["yp"]).transpose(2, 1, 0).reshape(512, D)
        y_prompt[2 * c:2 * c + 2] = yp.reshape(2, SEQ, D)
        ck = np.asarray(R[c]["ckv_out"])
        kp = np.asarray(R[c]["kpe_out"])
        for j in range(2):
            t = ck[j].transpose(2, 1, 0).reshape(512, KV_LORA)
            state_ckv[2 * c:2 * c + 2, j] = t.reshape(2, SEQ, KV_LORA)
            state_kpe[2 * c:2 * c + 2, j] = kp[j].T.reshape(2, SEQ, QK_ROPE)
    for g in range(DEC_BATCH):
        ys = np.asarray(R[4 * g]["ys"]).transpose(2, 1, 0).reshape(DEC_SEQ, D)
        y_sample[g] = ys
    return (y_prompt, y_sample, state_ckv, state_kpe)
```

```python
import math
from contextlib import ExitStack

import numpy as np
import ml_dtypes

import concourse.bass as bass
import concourse.mybir as mybir
from concourse.bass_utils import run_bass_kernel_spmd

F32 = mybir.dt.float32
BF16 = mybir.dt.bfloat16
I32 = mybir.dt.int32
AF = mybir.ActivationFunctionType
ALU = mybir.AluOpType

NCORES = 8
D = 1024
DEPTH = 4
BATCH, SEQ = 16, 256
DEC_BATCH, DEC_SEQ = 2, 2048
PAST = 512
EPS = 1e-6
NH = 16
Q_LORA, KV_LORA, QK_NOPE, QK_ROPE, V_HEAD = 384, 256, 64, 32, 64
GRID_W = 64
ROPE_THETA = 10000.0
ENGS = ("pe", "act", "dve", "pool", "sp")
GROUP4 = [[0, 1, 2, 3], [4, 5, 6, 7]]
GROUP8 = [[0, 1, 2, 3, 4, 5, 6, 7]]


class _Op:
    __slots__ = ("eng", "fn", "deps", "signals", "val", "dma_sem", "inc", "persist", "idx")

    def __init__(self, eng, fn, deps, dma_sem=None):
        self.eng = eng
        self.fn = fn
        self.deps = deps
        self.signals = False
        self.val = None
        self.dma_sem = dma_sem
        self.inc = 16
        self.persist = False


class _Rec:
    def __init__(self):
        self.call = None

    def __getattr__(self, name):
        def f(*a, **kw):
            self.call = (name, a, kw)
            return self
        return f


class _Reg:
    __slots__ = ("w", "r")

    def __init__(self):
        self.w = None
        self.r = []


class Prog:
    def __init__(self, nc):
        self.nc = nc
        self.ops = {e: [] for e in ENGS}
        self.regs = {}
        self.dma_counts = {}
        self.last = {e: None for e in ENGS}
        self.dma_since_bar = {}
        self.tile_keys = {}
        self.hazards = {}

    def _reg(self, k):
        r = self.regs.get(k)
        if r is None:
            r = self.regs[k] = _Reg()
            tk = k if isinstance(k, str) else k[0]
            self.tile_keys.setdefault(tk, set()).add(k)
        return r

    def tile_ops(self, tkey):
        out = []
        for k in self.tile_keys.get(tkey, ()):
            rg = self.regs[k]
            if rg.w is not None:
                out.append(rg.w)
            out.extend(rg.r)
        return out

    muted = False

    def op(self, eng, fn, reads=(), writes=(), dma_sem=None, inc=16, persist=False):
        if self.muted:
            return None
        deps = []
        seen = set()

        def add(d):
            if d is None or id(d) in seen:
                return
            if eng == "pe" and d.eng == "pe" and d.dma_sem is None:
                return
            seen.add(id(d))
            deps.append(d)

        if self.hazards:
            for k in list(reads) + list(writes):
                hz = self.hazards.get(k if isinstance(k, str) else k[0])
                if hz:
                    for d in hz:
                        add(d)
        for k in reads:
            add(self._reg(k).w)
        for k in writes:
            rg = self._reg(k)
            w = rg.w
            if w is not None and dma_sem is not None and w.dma_sem == dma_sem:
                for d in w.deps:
                    add(d)
            else:
                add(w)
            for d in rg.r:
                add(d)
        rec = _Rec()
        fn(rec)
        call = rec.call
        import sys as _sys
        fr = _sys._getframe(2)
        where = "%s:%d" % (fr.f_code.co_name, fr.f_lineno)

        def _do(e, c=call, where=where):
            try:
                return getattr(e, c[0])(*c[1], **c[2])
            except Exception as ex:
                raise RuntimeError("emit failed for op recorded at %s: %s %s" % (where, c[0], ex)) from ex
        o = _Op(eng, _do, deps, dma_sem)
        self.nops = getattr(self, "nops", 0) + 1
        o.idx = self.nops
        if dma_sem is not None:
            v = self.dma_counts.get(dma_sem, 0) + inc
            self.dma_counts[dma_sem] = v
            o.val = v
            o.inc = inc
            o.persist = persist
            if not persist:
                self.dma_since_bar[dma_sem] = o
        for k in reads:
            self._reg(k).r.append(o)
        for k in writes:
            rg = self._reg(k)
            rg.w = o
            rg.r = []
        self.ops[eng].append(o)
        if dma_sem is None:
            self.last[eng] = o
        return o

    def barrier(self):
        lasts = [o for o in self.last.values() if o is not None]
        dmas = list(self.dma_since_bar.values())
        self.dma_since_bar = {}
        self.hazards = {}
        new = {}
        for e in ENGS:
            deps = [o for o in lasts if o.eng != e or e != "pe"] + dmas
            if e == "pe":
                deps = [o for o in deps if not (o.eng == "pe" and o.dma_sem is None)]
            b = _Op(e, None, deps)
            self.ops[e].append(b)
            new[e] = b

    def emit(self, final_dma_sems=()):
        nc = self.nc
        for e in ENGS:
            for o in self.ops[e]:
                for d in o.deps:
                    d.signals = True
        with ExitStack() as st:
            esem = {e: st.enter_context(nc.semaphore("s_" + e)) for e in ENGS}
            dsem = {k: st.enter_context(nc.semaphore("d_%d" % i))
                    for i, k in enumerate(self.dma_counts)}
            for e in ENGS:
                c = 0
                for o in self.ops[e]:
                    if o.dma_sem is None and o.signals:
                        c += 1
                        o.val = c
            block = st.enter_context(nc.Block())
            engobj = {"pe": "tensor", "act": "scalar", "dve": "vector", "pool": "gpsimd", "sp": "sync"}

            def run(e, eng):
                waited = {}
                for o in self.ops[e]:
                    need = {}
                    for d in o.deps:
                        if d.dma_sem is not None:
                            key = ("d", d.dma_sem)
                            sem = dsem[d.dma_sem]
                        else:
                            key = ("e", d.eng)
                            sem = esem[d.eng]
                        if waited.get(key, 0) >= d.val:
                            continue
                        if key not in need or need[key][1] < d.val:
                            need[key] = (sem, d.val)
                    for key, (sem, v) in need.items():
                        eng.wait_ge(sem, v)
                        waited[key] = v
                    if o.fn is None:
                        continue
                    ins = o.fn(eng)
                    if o.dma_sem is not None:
                        ins.then_inc(dsem[o.dma_sem], o.inc)
                    elif o.signals:
                        ins.then_inc(esem[e], 1)
                if e == "sp":
                    for k in final_dma_sems:
                        if k in dsem:
                            eng.wait_ge(dsem[k], self.dma_counts[k])

            for e in ENGS:
                def mk(e):
                    def f(eng):
                        run(e, eng)
                    return f
                getattr(block, engobj[e])(mk(e))


class T:
    _n = 0

    def __init__(self, ap, name):
        T._n += 1
        self.ap = ap
        self.key = "%s#%d" % (name, T._n)

    def __getitem__(self, idx):
        return self.ap[idx]

    def k(self, *sub):
        return (self.key,) + tuple(sub) if sub else self.key


class Arena:
    def __init__(self, nc, st, words):
        self.t = st.enter_context(nc.sbuf_tensor("arena", [128, words], F32))
        self.words = words
        self.top = 0
        self.stack = []
        self.peak = 0
        self.live = []
        self.dead = []
        self.prog = None

    def alloc(self, name, free_shape, dt=F32):
        n = int(np.prod(free_shape))
        w = n if dt in (F32, I32) else (n + 1) // 2
        w = (w + 7) // 8 * 8
        off = self.top
        self.top += w
        self.peak = max(self.peak, self.top)
        assert self.top <= self.words, "SBUF arena overflow: %s needs %d words, top %d" % (name, w, self.top)
        ap = self.t[:, off:off + w]
        if dt != F32:
            ap = ap.bitcast(dt)
        ap = ap[:, 0:n]
        if len(free_shape) == 2:
            ap = ap.rearrange("p (a b) -> p a b", a=free_shape[0])
        elif len(free_shape) == 3:
            ap = ap.rearrange("p (a b c) -> p a b c", a=free_shape[0], b=free_shape[1])
        elif len(free_shape) == 4:
            ap = ap.rearrange("p (a b c d) -> p a b c d", a=free_shape[0], b=free_shape[1], c=free_shape[2])
        t = T(ap, name)
        if self.prog is not None:
            hz = []
            seen = set()
            for (s0, e0, old) in self.dead:
                if s0 < off + w and off < e0:
                    for o in self.prog.tile_ops(old.key):
                        if id(o) not in seen:
                            seen.add(id(o))
                            hz.append(o)
            if hz:
                best = {}
                for o in hz:
                    kk = ("d", o.dma_sem) if o.dma_sem is not None else ("e", o.eng)
                    if kk not in best or best[kk].idx < o.idx:
                        best[kk] = o
                self.prog.hazards[t.key] = list(best.values())
        self.live.append((off, off + w, t))
        return t

    def push(self):
        self.stack.append(self.top)

    def pop(self):
        self.top = self.stack.pop()
        keep = []
        for it in self.live:
            if it[0] >= self.top:
                self.dead.append(it)
            else:
                keep.append(it)
        self.live = keep

    def clear_dead(self):
        self.dead = []


def _bf(a):
    return np.ascontiguousarray(np.asarray(a, np.float32).astype(ml_dtypes.bfloat16))


def _f32(a):
    return np.ascontiguousarray(np.asarray(a, np.float32))


def _dft(L):
    n = 2 * L
    t = np.arange(L, dtype=np.float64)[:, None]
    kre = np.arange(0, L + 1, dtype=np.float64)[None, :]
    kim = np.arange(1, L, dtype=np.float64)[None, :]
    return np.concatenate([np.cos(2 * np.pi * kre * t / n), -np.sin(2 * np.pi * kim * t / n)], axis=1)


def _zemb(L):
    f32 = np.float32
    t = np.linspace(0.0, 1.0, L, dtype=f32)[:, None]
    w = (f32(2.0 * math.pi / L) * np.arange(L, dtype=f32))[:, None]
    bands = np.linspace(1e-4, 15, 16, dtype=f32)[None, :]
    z = np.concatenate([t, np.cos(bands * w), -np.sin(bands * w)], axis=-1)
    return z.T


def _chunk(w):
    K = w.shape[0]
    return w.reshape(K // 128, 128, *w.shape[1:]).swapaxes(0, 1)


_CONST = {}


def _consts():
    if _CONST:
        return _CONST
    Ls, Lp = DEC_SEQ, SEQ
    F = _dft(Ls)
    _CONST["Fs"] = _bf(F.reshape(16, 128, 32, 128).transpose(2, 1, 0, 3))
    G = F.T
    _CONST["Gs"] = _bf(G.reshape(4, 8, 128, 4, 512).transpose(3, 0, 2, 1, 4))
    Fq = _dft(Lp)
    _CONST["Fp"] = _bf(Fq.reshape(2, 128, 512).transpose(1, 0, 2))
    _CONST["Gp"] = _bf(Fq.T.reshape(4, 128, 256).transpose(1, 0, 2))
    _CONST["zemb_s"] = _bf(_zemb(Ls))
    _CONST["zemb_p"] = _bf(_zemb(Lp))
    _CONST["tv_s"] = _f32(np.linspace(0.0, 1.0, Ls, dtype=np.float32).reshape(16, 128).T)
    _CONST["tv_p"] = _f32(np.linspace(0.0, 1.0, Lp, dtype=np.float32).reshape(2, 128).T)
    maxd = math.log(1e-2) / 0.3
    mind = math.log(1e-2) / 1.5
    _CONST["ndelta"] = -np.abs(np.linspace(mind, maxd, D, dtype=np.float32))
    ident = np.eye(128, dtype=np.float32)
    _CONST["ident"] = _bf(ident)
    _CONST["ones"] = _bf(np.ones((128, 128), np.float32))
    _CONST["swap"] = _bf(np.roll(ident, 64, axis=1))
    rows = Ls // GRID_W
    axis_dim = QK_ROPE // 2
    inv = (ROPE_THETA ** (-np.arange(0, axis_dim, 2, dtype=np.float32) / axis_dim)).astype(np.float32)
    row = np.repeat(np.arange(rows, dtype=np.float32), GRID_W)
    col = np.tile(np.arange(GRID_W, dtype=np.float32), rows)
    ar = (row[:, None] * inv).astype(np.float32)
    ac = (col[:, None] * inv).astype(np.float32)
    cs = np.zeros((128, 2, Ls), np.float32)
    cs[64:72, 0] = np.cos(ar).T
    cs[72:80, 0] = np.cos(ar).T
    cs[80:88, 0] = np.cos(ac).T
    cs[88:96, 0] = np.cos(ac).T
    cs[64:72, 1] = -np.sin(ar).T
    cs[72:80, 1] = np.sin(ar).T
    cs[80:88, 1] = -np.sin(ac).T
    cs[88:96, 1] = np.sin(ac).T
    _CONST["rope"] = cs
    _CONST["ropesw"] = np.concatenate([np.arange(8, 16), np.arange(0, 8), np.arange(24, 32), np.arange(16, 24)])
    return _CONST


class Grp:
    def __init__(self, name, NT, nseq, ncb, nh):
        self.name = name
        self.NT = NT
        self.nseq = nseq
        self.L = NT // nseq
        self.TB = max(1, NT // 512)
        self.NB = self.L // 128
        self.ncb = ncb
        self.C = ncb * 128
        self.nh = nh
        self.SBW = min(512, self.C)
        self.nsb = self.C // self.SBW
        self.NFB = 2 * self.L // 128
        self.samp = name == "s"


GP = Grp("p", 512, 2, 8, 16)
GS = Grp("s", 2048, 1, 2, 4)


class Builder:
    def __init__(self, depth=DEPTH):
        self.depth = depth
        self.nc = bass.Bass("TRN2", target_bir_lowering=False)
        self.P = Prog(self.nc)
        self.dr = {}
        self._bank = 0
        self._b4 = 0
        self.outs = []

    def din(self, name, shape, dt=F32):
        self.dr[name] = self.nc.dram_tensor(name, list(shape), dt, kind="ExternalInput").ap()
        return self.dr[name]

    def dout(self, name, shape, dt=F32):
        self.dr[name] = self.nc.dram_tensor(name, list(shape), dt, kind="ExternalOutput").ap()
        return self.dr[name]

    def dint(self, name, shape, dt=F32):
        self.dr[name] = self.nc.dram_tensor(name, list(shape), dt).ap()
        return self.dr[name]

    def bank(self):
        b = self._bank
        self._bank = (self._bank + 1) % 8
        return b

    def bank4(self):
        b = self._b4 * 4
        self._b4 ^= 1
        return b

    def psb(self, b, rows=slice(0, 128), n=512, off=0):
        return self.ps[rows, b * 512 + off:b * 512 + off + n]

    def pk(self, b):
        return ("ps", b)

    def load(self, q, dst, src, writes, sem, reads=(), persist=False):
        return self.P.op(q, lambda e: e.dma_start(out=dst, in_=src), reads=reads, writes=writes,
                         dma_sem=sem, persist=persist)

    def mm(self, out, lhsT, rhs, start, stop, reads, writes):
        self.P.op("pe", lambda e: e.matmul(out, lhsT, rhs, start=start, stop=stop), reads=reads, writes=writes)

    stop_at = None
    _cp = 0
    use_barriers = False

    def cp(self, name=""):
        import os
        self._cp += 1
        if self.stop_at is not None and self._cp >= self.stop_at:
            self.P.muted = True
        mr = os.environ.get("K_DBG_MUTE")
        if mr:
            a, b = [int(v) for v in mr.split(",")]
            self.P.muted = a <= self._cp < b

    def scope(self):
        b = self

        class _S:
            def __enter__(s):
                b.A.push()

            def __exit__(s, *a):
                if b.use_barriers:
                    b.P.barrier()
                    b.A.pop()
                    b.A.clear_dead()
                else:
                    b.A.pop()
        return _S()

    def declare(self):
        d = self.din
        d("xp", [128, 8, 512]); d("xs", [128, 8, 2048])
        d("cond", [128, 8, 3]); d("ada_w", [4, 128, 8, 768]); d("ada_b", [128, 4, 6])
        d("sel", [128, 2]); d("normw", [128, 4, 8]); d("fnorm", [128, 8])
        d("ident", [128, 128], BF16); d("ones", [128, 128], BF16); d("swap", [128, 128], BF16)
        for g in (GP, GS):
            n = g.name
            d("hy_win_" + n, [2, g.ncb, 128, 8, 512])
            d("hy_conv_" + n, [2, 128, g.ncb, 3, 4])
            d("hy_fw3_" + n, [2, 64, 2 * g.C])
            d("hy_fbias_" + n, [2, 128, g.C])
            d("ndelta_" + n, [128, g.C])
            d("zemb_" + n, [33, g.L], BF16)
            d("tv_" + n, [128, g.NB])
            d("mla_wg_" + n, [2, 128, 8, g.nh * 64])
            d("mla_wk_" + n, [2, 128, 2, g.nh, 64])
            d("mla_wv_" + n, [2, 128, 2, g.nh * 64])
        d("mla_wq_p", [2, 128, 3, 16, 96]); d("mla_wq_s", [2, 128, 3, 4, 2, 96])
        d("hy_fw1", [2, 33, 64]); d("hy_fvec", [2, 64, 4]); d("hy_fw2", [2, 64, 64])
        d("hy_wout", [2, 128, 8, 1024])
        d("Fs", [32, 128, 16, 128], BF16); d("Gs", [4, 4, 128, 8, 512], BF16)
        d("Fp", [128, 2, 512], BF16); d("Gp", [128, 4, 256], BF16)
        d("rope", [128, 2, 2048])
        d("mla_wa", [2, 128, 8, 832]); d("mla_qn", [128, 2, 3]); d("mla_kvn", [128, 2, 2])
        d("mla_wo", [2, 128, 8, 1024])
        d("cckv", [2, 128, 2, 512]); d("ckpe", [2, 128, 512])
        self.dout("yp", [128, 8, 512]); self.dout("ys", [128, 8, 2048])
        self.dout("ckv_out", [2, 128, 2, 512]); self.dout("kpe_out", [2, 32, 512])
        self.dint("cc_ada_in", [128, 72]); self.dint("cc_ada_out", [512, 72])
        for i in range(DEPTH):
            self.dint("cc_in%d" % i, [256, 2048], BF16)
            self.dint("cc_out%d" % i, [1024, 2048], BF16)

    def build(self):
        nc, P = self.nc, self.P
        self.declare()
        dr = self.dr
        with ExitStack() as st:
            self.A = A = Arena(nc, st, 51200)
            A.prog = P
            self.ps = st.enter_context(nc.psum_tensor("ps", [128, 4096], F32))
            self.xg_ = {}
            self.x = {"p": A.alloc("xp", [8, 512]), "s": A.alloc("xs", [8, 2048])}
            self.ident = A.alloc("ident", [128], BF16)
            self.ones = A.alloc("ones", [128], BF16)
            self.swap = A.alloc("swap", [128], BF16)
            self.modt = A.alloc("modt", [24, 4, 3])
            self.mods = A.alloc("mods", [24, 4])
            self.modA = A.alloc("modA", [2, 4, 8])
            self.normw = A.alloc("normw", [4, 8])
            self.fnorm = A.alloc("fnorm", [8])
            self.cst = A.alloc("cst", [8])
            self.sel = A.alloc("sel", [2])
            self.Fp = A.alloc("Fp", [2, 512], BF16)
            self.Gp = A.alloc("Gp", [4, 256], BF16)
            self.qn_w = A.alloc("qn_w", [2, 3])
            self.kvn_w = A.alloc("kvn_w", [2, 2])
            self.tv = {"p": A.alloc("tv_p", [2]), "s": A.alloc("tv_s", [16])}
            ld = self.load
            ld("sp", self.x["p"][:], dr["xp"], [self.x["p"].k(k, 0) for k in range(8)], "ld_xp")
            ld("sp", self.x["s"][:], dr["xs"], [self.x["s"].k(k, tb) for k in range(8) for tb in range(4)], "ld_xs")
            for t, n in ((self.ident, "ident"), (self.ones, "ones"), (self.swap, "swap"), (self.normw, "normw"),
                         (self.fnorm, "fnorm"), (self.sel, "sel"), (self.Fp, "Fp"), (self.Gp, "Gp"),
                         (self.qn_w, "mla_qn"), (self.kvn_w, "mla_kvn"), (self.tv["p"], "tv_p"), (self.tv["s"], "tv_s")):
                ld("sp", t[:], dr[n], [t.k()], "ld_" + n)
            P.op("dve", lambda e: e.memset(self.cst[:, 0:1], EPS), writes=[self.cst.k()])
            P.op("dve", lambda e: e.memset(self.cst[:, 1:2], 0.0), reads=[self.cst.k()], writes=[self.cst.k()])
            self.adaln()
            for i in range(self.depth):
                j = i // 2
                fn = self.hyena if i % 2 == 0 else self.mla
                fn(GS, i, j, part=1)
                self.A.push()
                self._wout = A.alloc("wout", [8, 1024], BF16)
                self.load("pool", self._wout[:], dr["hy_wout" if i % 2 == 0 else "mla_wo"][j], [self._wout.k()], "ld_wout")
                fn(GP, i, j, part="pre")
                fn(GS, i, j, part=2)
                fn(GP, i, j, part="post")
                self.A.pop()
            self.P.muted = False
            self.final(GP, "yp")
            self.final(GS, "ys")
            P.emit(final_dma_sems=self.outs)
        return nc

    def adaln(self):
        P, A, dr = self.P, self.A, self.dr
        with self.scope():
            cnd = A.alloc("cond", [8, 3])
            cb = A.alloc("condb", [8, 3], BF16)
            adb = A.alloc("adb", [4, 6])
            res = A.alloc("adares", [6, 4, 3])
            W = [A.alloc("adaW%d" % i, [8, 768], BF16) for i in range(2)]
            self.load("sp", cnd[:], dr["cond"], [cnd.k()], "ld_cond")
            self.load("sp", adb[:], dr["ada_b"], [adb.k()], "ld_adb")
            P.op("act", lambda e: e.activation(cb[:], cnd[:], AF.Silu), reads=[cnd.k()], writes=[cb.k()])
            b = self.bank()
            for i in range(4):
                Wt = W[i % 2]
                self.load("pool", Wt[:], dr["ada_w"][i], [Wt.k()], "ld_adaW%d" % (i % 2))
                for m in range(6):
                    c0 = (i * 6 + m) * 3
                    for k in range(8):
                        self.mm(self.psb(b, n=3, off=c0), Wt[:, k, 128 * m:128 * m + 128], cb[:, k, :],
                                k == 0, k == 7, [Wt.k(), cb.k()], [self.pk(b)])
            for i in range(4):
                for m in range(6):
                    c0 = (i * 6 + m) * 3
                    P.op("dve", lambda e, i=i, m=m, c0=c0: e.tensor_scalar(
                        res[:, m, i, :], self.psb(b, n=3, off=c0), adb[:, i, m:m + 1], None, ALU.add),
                        reads=[self.pk(b), adb.k()], writes=[res.k(i, m)])
            rk = [res.k(i, m) for i in range(4) for m in range(6)]
            self.load("sp", dr["cc_ada_in"], res[:].rearrange("p a b c -> p (a b c)"), ["cc_ada_in"], "st_ada", reads=rk)
            P.op("pool", lambda e: e.collective_compute("AllGather", ALU.bypass, replica_groups=GROUP4,
                                                        ins=[dr["cc_ada_in"]], outs=[dr["cc_ada_out"]]),
                 reads=["cc_ada_in"], writes=["cc_ada_out"], dma_sem="cc_ada", inc=1)
            self.load("sp", self.modt[:].rearrange("p (j m) i c -> p j (m i c)", j=4),
                      dr["cc_ada_out"].rearrange("(j p) f -> p j f", p=128), [self.modt.k()], "ld_modt",
                      reads=["cc_ada_out"])
            mt, ms = self.modt, self.mods
            P.op("dve", lambda e: e.tensor_scalar(ms[:], mt[:, :, :, 1], self.sel[:, 0:1], None, ALU.mult),
                 reads=[mt.k(), self.sel.k()], writes=[ms.k()])
            P.op("dve", lambda e: e.scalar_tensor_tensor(ms[:], mt[:, :, :, 2], self.sel[:, 1:2], ms[:], ALU.mult, ALU.add),
                 reads=[mt.k(), self.sel.k(), ms.k()], writes=[ms.k()])
            mA = self.modA
            for gi in range(2):
                for i in range(4):
                    src = mt[:, 8:16, i, 0] if gi == 0 else ms[:, 8:16, i]
                    P.op("dve", lambda e, gi=gi, i=i, src=src: e.scalar_tensor_tensor(
                        mA[:, gi, i, :], src, 1.0, self.normw[:, i, :], ALU.add, ALU.mult),
                        reads=[mt.k(), ms.k(), self.normw.k(), mA.k()], writes=[mA.k()])

    def mod(self, g, i, part, k):
        c = part * 8 + k
        if g.samp:
            return self.mods[:, c, i:i + 1]
        return self.modt[:, c, i, 0:1]

    def modkeys(self):
        return [self.modt.k(), self.mods.k(), self.modA.k()]

    def norm_scratch(self):
        A = self.A
        return {"sq": A.alloc("sq", [8, 512], BF16), "rs": [A.alloc("rstd%d" % q, [512]) for q in range(2)],
                "tmp": [A.alloc("nt%d" % q, [512]) for q in range(2)], "n": 0}

    def norm_tb(self, g, i, tb, out_fn, wk, sc):
        P = self.P
        x = self.x[g.name]
        gi = 1 if g.samp else 0
        ts = slice(tb * 512, (tb + 1) * 512)
        s_, r_ = sc["sq"], sc["rs"][tb % 2]
        P.op("act", lambda e: e.activation(s_[:, 0:4, :], x[:, 0:4, ts], AF.Square),
             reads=[x.k(k, tb) for k in range(4)], writes=[s_.k(0)])
        P.op("dve", lambda e: e.tensor_tensor(s_[:, 4:8, :], x[:, 4:8, ts], x[:, 4:8, ts], ALU.mult),
             reads=[x.k(k, tb) for k in range(4, 8)], writes=[s_.k(1)])
        b = self.bank()
        for k in range(8):
            self.mm(self.psb(b), self.ones[:], s_[:, k, :], k == 0, k == 7, [self.ones.k(), s_.k(k // 4)], [self.pk(b)])
        P.op("act", lambda e: e.activation(r_[:], self.psb(b), AF.Ln, bias=self.cst[:, 0:1], scale=1.0 / D),
             reads=[self.pk(b), self.cst.k()], writes=[r_.k()])
        P.op("act", lambda e: e.activation(r_[:], r_[:], AF.Exp, scale=-0.5), reads=[r_.k()], writes=[r_.k()])
        for k in range(8):
            t_ = sc["tmp"][sc["n"] % 2]
            sc["n"] += 1
            P.op("dve", lambda e, k=k, t_=t_: e.tensor_tensor(t_[:], x[:, k, ts], r_[:], ALU.mult),
                 reads=[x.k(k, tb), r_.k()], writes=[t_.k()])
            P.op("act", lambda e, k=k, t_=t_: e.activation(out_fn(k), t_[:], AF.Identity,
                                                          bias=self.mod(g, i, 0, k), scale=self.modA[:, gi, i, k:k + 1]),
                 reads=[t_.k()] + self.modkeys(), writes=wk(k))

    def norm_mod(self, g, i, h):
        def body():
            sc = self.norm_scratch()
            for tb in range(g.TB):
                self.norm_tb(g, i, tb, lambda k, tb=tb: h[:, k, tb * 512:(tb + 1) * 512],
                             lambda k, tb=tb: [h.k(k, tb)], sc)
        if g.samp:
            with self.scope():
                body()
        else:
            body()

    def resid_update(self, g, i, yga, w, TBs):
        P = self.P
        x = self.x[g.name]
        for tb in range(TBs):
            for m in range(8):
                b = self.bank()
                for k in range(8):
                    self.mm(self.psb(b), w[:, k, 128 * m:128 * m + 128], yga[:, k, tb * 512:(tb + 1) * 512],
                            k == 0, k == 7, [w.k(), yga.k(k, tb)], [self.pk(b)])
                P.op("dve", lambda e, b=b, m=m, tb=tb: e.scalar_tensor_tensor(
                    x[:, m, tb * 512:(tb + 1) * 512], self.psb(b), self.mod(g, i, 2, m),
                    x[:, m, tb * 512:(tb + 1) * 512], ALU.mult, ALU.add),
                    reads=[self.pk(b), x.k(m, tb)] + self.modkeys(), writes=[x.k(m, tb)])

    def gather_and_project(self, g, i, j, yg, wname, part):
        P, A, dr = self.P, self.A, self.dr
        if part == 1:
            self.load("sp", dr["cc_in%d" % i].rearrange("(c p) t -> p c t", p=128), yg[:],
                      ["cc_in%d" % i], "st_cc%d" % i, reads=[yg.k(c, tb) for c in range(2) for tb in range(4)])
            P.op("pool", lambda e: e.collective_compute("AllGather", ALU.bypass, replica_groups=GROUP4,
                                                        ins=[dr["cc_in%d" % i]], outs=[dr["cc_out%d" % i]]),
                 reads=["cc_in%d" % i], writes=["cc_out%d" % i], dma_sem="cc%d" % i, inc=1, persist=True)
            return
        with self.scope():
            yga = A.alloc("yga", [8, 2048], BF16)
            w = self._wout
            src = dr["cc_out%d" % i].rearrange("(k p) t -> p k t", p=128)
            for tb in range(4):
                self.load("sp", yga[:, :, tb * 512:(tb + 1) * 512], src[:, :, tb * 512:(tb + 1) * 512],
                          [yga.k(k, tb) for k in range(8)], "ld_yga%d" % tb, reads=["cc_out%d" % i])
            import os
            if os.environ.get("K_DBG_P2") == "loads":
                return
            self.resid_update(g, i, yga, w, 4)

    def sin_layer(self, ps_ap, rows, n, bvec, fvec, out_ap, reads, writes, scr):
        P = self.P
        t1, ki, kf = scr
        P.op("dve", lambda e: e.tensor_scalar(t1[rows, 0:n], ps_ap, bvec, fvec, ALU.add, ALU.mult),
             reads=reads, writes=[t1.k()])
        P.op("dve", lambda e: e.tensor_copy(ki[rows, 0:n], t1[rows, 0:n]), reads=[t1.k()], writes=[ki.k()])
        P.op("dve", lambda e: e.tensor_tensor(t1[rows, 0:n], t1[rows, 0:n], ki[rows, 0:n], ALU.subtract),
             reads=[t1.k(), ki.k()], writes=[t1.k()])
        P.op("act", lambda e: e.activation(out_ap, t1[rows, 0:n], AF.Sin, scale=2 * math.pi * (1 - 1e-6)),
             reads=[t1.k()], writes=writes)

    def hyena(self, g, i, j, part):
        P, A, dr = self.P, self.A, self.dr
        n = g.name
        if part == 2:
            self.gather_and_project(g, i, j, None, "hy_wout", 2)
            return
        NT, L, NB, TB, C, SBW, NFB = g.NT, g.L, g.NB, g.TB, g.C, g.SBW, g.NFB
        nseq = g.nseq
        HP = NFB // 2
        if part == "post":
            yg = self._carry
            with self.scope():
                w = self._wout
                ygk = _KeyAll(yg, [(cb, s) for cb in range(g.ncb) for s in range(nseq)])
                self.resid_update(g, i, ygk, w, 1)
            self.cp("Wp")
            if self.use_barriers:
                self.P.barrier()
            self.A.pop()
            return
        if not g.samp:
            self.A.push()
            self._carry = A.alloc("yg", [g.ncb, NT], BF16)
        with self.scope():
            xg = A.alloc("xg", [g.ncb, NT])
            zT = A.alloc("zT", [nseq * NB, C], BF16)
            with self.scope():
                h = A.alloc("h", [8, NT], BF16)
                Wt = [A.alloc("win%d" % q, [8, 512], BF16) for q in range(2)]
                cw = A.alloc("convw", [g.ncb, 3, 4])
                U = [A.alloc("u%d" % q, [NT]) for q in range(2)]
                zc = A.alloc("zc", [NT], BF16)
                self.load("sp", cw[:], dr["hy_conv_" + n][j], [cw.k()], "ld_convw")
                for cb in range(min(2, g.ncb)):
                    self.load("pool", Wt[cb][:], dr["hy_win_" + n][j, cb], [Wt[cb].k()], "ld_win%d" % cb)
                self.norm_mod(g, i, h)
                for cb in range(g.ncb):
                    W = Wt[cb % 2]
                    if cb >= 2:
                        self.load("pool", W[:], dr["hy_win_" + n][j, cb], [W.k()], "ld_win%d" % (cb % 2))

                    def proj(gi):
                        b0 = self.bank4() if TB == 4 else self.bank()
                        for tb in range(TB):
                            for k in range(8):
                                self.mm(self.psb(b0 + tb), W[:, k, 128 * gi:128 * gi + 128],
                                        h[:, k, tb * 512:(tb + 1) * 512], k == 0, k == 7,
                                        [W.k(), h.k(k, tb)], [self.pk(b0 + tb)])
                        return b0

                    def conv(gi, b0, Ut):
                        pk = [self.pk(b0 + tb) for tb in range(TB)]
                        Pf = self.ps[:, b0 * 512:b0 * 512 + NT]
                        c_ = lambda q: cw[:, cb, gi, q:q + 1]
                        P.op("act", lambda e: e.activation(Ut[:], Pf, AF.Identity, bias=c_(3), scale=c_(1)),
                             reads=pk + [cw.k()], writes=[Ut.k()])
                        Pv = Pf.rearrange("p (s l) -> p s l", s=nseq)
                        Uv = Ut[:].rearrange("p (s l) -> p s l", s=nseq)
                        P.op("dve", lambda e: e.scalar_tensor_tensor(Uv[:, :, 1:L], Pv[:, :, 0:L - 1], c_(0), Uv[:, :, 1:L],
                                                                     ALU.mult, ALU.add),
                             reads=pk + [cw.k(), Ut.k()], writes=[Ut.k()])
                        P.op("dve", lambda e: e.scalar_tensor_tensor(Uv[:, :, 0:L - 1], Pv[:, :, 1:L], c_(2), Uv[:, :, 0:L - 1],
                                                                     ALU.mult, ALU.add),
                             reads=pk + [cw.k(), Ut.k()], writes=[Ut.k()])

                    bv = proj(2)
                    bx1 = proj(1)
                    conv(2, bv, U[0])
                    conv(1, bx1, U[1])
                    bx0 = proj(0)
                    P.op("dve", lambda e: e.tensor_tensor(zc[:], U[0][:], U[1][:], ALU.mult),
                         reads=[U[0].k(), U[1].k()], writes=[zc.k()])
                    bg = proj(3)
                    conv(0, bx0, U[0])
                    P.op("act", lambda e, bg=bg: e.activation(U[1][:], self.ps[:, bg * 512:bg * 512 + NT], AF.Silu),
                         reads=[self.pk(bg + tb) for tb in range(TB)], writes=[U[1].k()])
                    P.op("dve", lambda e, cb=cb: e.tensor_tensor(xg[:, cb, :], U[0][:], U[1][:], ALU.mult),
                         reads=[U[0].k(), U[1].k()], writes=[xg.k(cb)])
                    nblk = NT // 128
                    for b8 in range(0, nblk, 8):
                        nb_ = min(8, nblk - b8)
                        bt = self.bank()
                        pt = self.psb(bt).bitcast(BF16)
                        for q in range(nb_):
                            P.op("pe", lambda e, q=q, b8=b8, pt=pt: e.transpose(
                                pt[:, q * 128:(q + 1) * 128], zc[:, (b8 + q) * 128:(b8 + q + 1) * 128], self.ident[:]),
                                reads=[zc.k(), self.ident.k()], writes=[self.pk(bt)])
                        P.op("act", lambda e, b8=b8, nb_=nb_, pt=pt, cb=cb: e.activation(
                            zT[:, b8:b8 + nb_, cb * 128:(cb + 1) * 128],
                            pt[:, 0:nb_ * 128].rearrange("p (a b) -> p a b", a=nb_), AF.Copy),
                            reads=[self.pk(bt)], writes=[zT.k(cb, b8)])
            zTk = [zT.k(cb, b8) for cb in range(g.ncb) for b8 in range(0, NT // 128, 8)]
            self.cp("A")
            with self.scope():
                FC = 2 * C
                filtT = A.alloc("filtT", [NB, FC], BF16)
                rn2 = A.alloc("rn2", [FC])
                bias2 = A.alloc("bias2", [C])
                Y = A.alloc("Y", [NFB, nseq, C], BF16)
                yg = A.alloc("yg", [g.ncb, NT], BF16) if g.samp else self._carry
                with self.scope():
                    zemb = A.alloc("zemb", [L], BF16)
                    w1 = A.alloc("fw1", [64], BF16)
                    w2 = A.alloc("fw2", [64], BF16)
                    w3 = A.alloc("fw3", [FC], BF16)
                    fv = A.alloc("fvec", [4])
                    nd = A.alloc("ndelta", [C])
                    hd1 = A.alloc("hd1", [L], BF16)
                    hd2 = A.alloc("hd2", [L], BF16)
                    scr = (A.alloc("sn_t", [512]), A.alloc("sn_i", [512], I32), A.alloc("sn_f", [512]))
                    dec = [A.alloc("dec%d" % q, [C]) for q in range(2)]
                    absb = [A.alloc("absb%d" % q, [512], BF16) for q in range(3)]
                    self.load("sp", zemb[0:33], dr["zemb_" + n], [zemb.k()], "ld_zemb")
                    self.load("pool", w1[0:33], dr["hy_fw1"][j], [w1.k()], "ld_fw1")
                    self.load("pool", w2[0:64], dr["hy_fw2"][j], [w2.k()], "ld_fw2")
                    self.load("pool", w3[0:64], dr["hy_fw3_" + n][j], [w3.k()], "ld_fw3")
                    self.load("sp", fv[0:64], dr["hy_fvec"][j], [fv.k()], "ld_fvec")
                    self.load("sp", nd[:], dr["ndelta_" + n], [nd.k()], "ld_nd")
                    self.load("sp", bias2[:], dr["hy_fbias_" + n][j], [bias2.k()], "ld_fbias")
                    P.op("dve", lambda e: e.tensor_scalar(fv[0:64, 3:4], fv[0:64, 1:2], 1.0 / (2 * math.pi), None, ALU.mult),
                         reads=[fv.k()], writes=[fv.k()])
                    P.op("dve", lambda e: e.tensor_scalar(bias2[:], bias2[:], 2.0 / (2 * L), None, ALU.mult),
                         reads=[bias2.k()], writes=[bias2.k()])
                    R64 = slice(0, 64)
                    for tq in range(0, L, 512):
                        nn = min(512, L - tq)
                        b = self.bank()
                        self.mm(self.psb(b, R64, nn), w1[0:33, :], zemb[0:33, tq:tq + nn], True, True,
                                [w1.k(), zemb.k()], [self.pk(b)])
                        self.sin_layer(self.psb(b, R64, nn), R64, nn, fv[0:64, 0:1], fv[0:64, 3:4], hd1[0:64, tq:tq + nn],
                                       [self.pk(b), fv.k()], [hd1.k(tq)], scr)
                        b = self.bank()
                        self.mm(self.psb(b, R64, nn), w2[0:64, :], hd1[0:64, tq:tq + nn], True, True,
                                [w2.k(), hd1.k(tq)], [self.pk(b)])
                        self.sin_layer(self.psb(b, R64, nn), R64, nn, fv[0:64, 2:3], fv[0:64, 3:4], hd2[0:64, tq:tq + nn],
                                       [self.pk(b), fv.k()], [hd2.k(tq)], scr)
                    SW = min(512, C)
                    nseg = FC // SW
                    nbank = [self.bank() for _ in range(nseg)]
                    pend = []
                    na = 0

                    def nsum(cq_, blk_, ab_):
                        self.mm(self.psb(nbank[cq_], n=SW), self.ones[:], ab_[:, 0:SW], blk_ == 0, blk_ == NB - 1,
                                [self.ones.k(), ab_.k()], [self.pk(nbank[cq_])])
                    for blk in range(NB):
                        dc = dec[blk % 2]
                        P.op("act", lambda e, dc=dc, blk=blk: e.activation(dc[:], nd[:], AF.Exp, scale=self.tv[n][:, blk:blk + 1]),
                             reads=[nd.k(), self.tv[n].k()], writes=[dc.k()])
                        for cq in range(nseg):
                            b = self.bank()
                            while b in nbank:
                                b = self.bank()
                            self.mm(self.psb(b, n=SW), hd2[0:64, blk * 128:(blk + 1) * 128], w3[0:64, cq * SW:(cq + 1) * SW],
                                    True, True, [hd2.k((blk * 128) // 512 * 512), w3.k()], [self.pk(b)])
                            dcol = (cq * SW) % C
                            P.op("dve", lambda e, b=b, blk=blk, cq=cq, dc=dc, dcol=dcol: e.tensor_tensor(
                                filtT[:, blk, cq * SW:(cq + 1) * SW], self.psb(b, n=SW), dc[:, dcol:dcol + SW], ALU.mult),
                                reads=[self.pk(b), dc.k()], writes=[filtT.k(blk, cq)])
                            ab = absb[na % 3]
                            na += 1
                            P.op("act", lambda e, ab=ab, blk=blk, cq=cq: e.activation(
                                ab[:, 0:SW], filtT[:, blk, cq * SW:(cq + 1) * SW], AF.Abs),
                                reads=[filtT.k(blk, cq)], writes=[ab.k()])
                            pend.append((cq, blk, ab))
                            if len(pend) > 2:
                                nsum(*pend.pop(0))
                    for pp in pend:
                        nsum(*pp)
                    for cq in range(nseg):
                        P.op("act", lambda e, cq=cq: e.activation(rn2[:, cq * SW:(cq + 1) * SW], self.psb(nbank[cq], n=SW), AF.Ln),
                             reads=[self.pk(nbank[cq])], writes=[rn2.k(cq)])
                        P.op("act", lambda e, cq=cq: e.activation(rn2[:, cq * SW:(cq + 1) * SW], rn2[:, cq * SW:(cq + 1) * SW],
                                                                 AF.Exp, scale=-1.0),
                             reads=[rn2.k(cq)], writes=[rn2.k(cq)])
                        P.op("dve", lambda e, cq=cq: e.tensor_scalar(rn2[:, cq * SW:(cq + 1) * SW], rn2[:, cq * SW:(cq + 1) * SW],
                                                                    2.0 / (2 * L), None, ALU.mult),
                             reads=[rn2.k(cq)], writes=[rn2.k(cq)])
                self.cp("F")
                SW = min(512, C)
                nseg = FC // SW
                rnk = [rn2.k(cq) for cq in range(nseg)]
                ftk = lambda tb: [filtT.k(tb, q) for q in range(nseg)]
                with self.scope():
                    if g.samp:
                        Ft = [A.alloc("Ft%d" % q, [16, 128], BF16) for q in range(4)]
                    Hr = [A.alloc("Hr%d" % q, [SBW]) for q in range(2)]
                    Hb = [A.alloc("Hbt%d" % q, [SBW]) for q in range(2)]
                    Zs = [[A.alloc("Zs%d_%d" % (q, s), [SBW]) for s in range(nseq)] for q in range(2)]
                    tt = [A.alloc("yt%d" % q, [SBW]) for q in range(4)]
                    fx = A.alloc("fx", [SBW])
                    nf = 0
                    for sb in range(g.nsb):
                        c0 = sb * SBW
                        for bp in range(HP):
                            for half in range(2):
                                fb = bp + half * HP
                                if g.samp:
                                    F_ = Ft[nf % 4]
                                    nf += 1
                                    self.load("sp", F_[:], dr["Fs"][fb], [F_.k()], "ld_F%d" % ((nf - 1) % 4))
                                    Fap = lambda tb, F_=F_: F_[:, tb, :]
                                    Fk = F_.k()
                                else:
                                    Fap = lambda tb, fb=fb: self.Fp[:, tb, fb * 128:(fb + 1) * 128]
                                    Fk = self.Fp.k()
                                bHf, bHb = self.bank(), self.bank()
                                bZ = [self.bank() for _ in range(nseq)]
                                for tb in range(NB):
                                    st_, sp_ = tb == 0, tb == NB - 1
                                    self.mm(self.psb(bHf, n=SBW), Fap(tb), filtT[:, tb, c0:c0 + SBW], st_, sp_,
                                            [Fk] + ftk(tb), [self.pk(bHf)])
                                    self.mm(self.psb(bHb, n=SBW), Fap(tb), filtT[:, tb, C + c0:C + c0 + SBW], st_, sp_,
                                            [Fk] + ftk(tb), [self.pk(bHb)])
                                    for s in range(nseq):
                                        self.mm(self.psb(bZ[s], n=SBW), Fap(tb), zT[:, s * NB + tb, c0:c0 + SBW], st_, sp_,
                                                [Fk] + zTk, [self.pk(bZ[s])])
                                H_, T_ = Hr[half], Hb[half]
                                P.op("dve", lambda e, T_=T_, bHb=bHb: e.tensor_tensor(T_[:], self.psb(bHb, n=SBW), rn2[:, C + c0:C + c0 + SBW], ALU.mult),
                                     reads=[self.pk(bHb)] + rnk, writes=[T_.k()])
                                P.op("dve", lambda e, H_=H_, bHf=bHf: e.tensor_tensor(H_[:], self.psb(bHf, n=SBW), rn2[:, c0:c0 + SBW], ALU.mult),
                                     reads=[self.pk(bHf)] + rnk, writes=[H_.k()])
                                fix = bp == 0 and half == 1
                                if fix:
                                    P.op("dve", lambda e, H_=H_, T_=T_: e.tensor_tensor(fx[0:1, :], H_[0:1, :], T_[0:1, :], ALU.add),
                                         reads=[H_.k(), T_.k()], writes=[fx.k()])
                                P.op("dve", lambda e, H_=H_, T_=T_, half=half: e.tensor_tensor(
                                    H_[:], H_[:], T_[:], ALU.add if half == 0 else ALU.subtract),
                                    reads=[H_.k(), T_.k()], writes=[H_.k()])
                                if fix:
                                    P.op("dve", lambda e, H_=H_: e.tensor_copy(H_[0:1, :], fx[0:1, :]),
                                         reads=[fx.k(), H_.k()], writes=[H_.k()])
                                if half == 0:
                                    P.op("dve", lambda e, H_=H_: e.tensor_tensor(H_[:], H_[:], bias2[:, c0:c0 + SBW], ALU.add),
                                         reads=[H_.k(), bias2.k()], writes=[H_.k()])
                                elif bp == 0:
                                    P.op("dve", lambda e, H_=H_: e.tensor_tensor(H_[0:1, :], H_[0:1, :], bias2[0:1, c0:c0 + SBW], ALU.add),
                                         reads=[H_.k(), bias2.k()], writes=[H_.k()])
                                if bp == 0:
                                    P.op("dve", lambda e, H_=H_: e.tensor_scalar(H_[0:1, :], H_[0:1, :], 0.5, None, ALU.mult),
                                         reads=[H_.k()], writes=[H_.k()])
                                for s in range(nseq):
                                    Z_ = Zs[half][s]
                                    P.op("act", lambda e, Z_=Z_, s=s, bZ=bZ: e.activation(Z_[:], self.psb(bZ[s], n=SBW), AF.Copy),
                                         reads=[self.pk(bZ[s])], writes=[Z_.k()])
                            for s in range(nseq):
                                Zr, Zi = Zs[0][s], Zs[1][s]
                                Yre = Y[:, bp, s, c0:c0 + SBW]
                                Yim = Y[:, bp + HP, s, c0:c0 + SBW]
                                rk_ = [Zr.k(), Zi.k(), Hr[0].k(), Hr[1].k()]
                                P.op("dve", lambda e, Zr=Zr: e.tensor_tensor(tt[0][:], Zr[:], Hr[0][:], ALU.mult), reads=rk_, writes=[tt[0].k()])
                                P.op("dve", lambda e, Zi=Zi: e.tensor_tensor(tt[1][:], Zi[:], Hr[1][:], ALU.mult), reads=rk_, writes=[tt[1].k()])
                                P.op("dve", lambda e, Yre=Yre: e.tensor_tensor(Yre, tt[0][:], tt[1][:], ALU.subtract),
                                     reads=[tt[0].k(), tt[1].k()], writes=[Y.k(bp, s, sb)])
                                P.op("pool", lambda e, Zr=Zr: e.tensor_tensor(tt[2][:], Zr[:], Hr[1][:], ALU.mult), reads=rk_, writes=[tt[2].k()])
                                P.op("pool", lambda e, Zi=Zi: e.tensor_tensor(tt[3][:], Zi[:], Hr[0][:], ALU.mult), reads=rk_, writes=[tt[3].k()])
                                P.op("pool", lambda e, Yim=Yim: e.tensor_tensor(Yim, tt[2][:], tt[3][:], ALU.add),
                                     reads=[tt[2].k(), tt[3].k()], writes=[Y.k(bp + HP, s, sb)])
                                if bp == 0:
                                    P.op("dve", lambda e, Zr=Zr, s=s: e.tensor_tensor(Y[0:1, 0, s, c0:c0 + SBW], Zr[0:1, :], Hr[0][0:1, :], ALU.mult),
                                         reads=rk_ + [Y.k(bp, s, sb)], writes=[Y.k(bp, s, sb)])
                                    P.op("pool", lambda e, Zi=Zi, s=s: e.tensor_tensor(Y[0:1, HP, s, c0:c0 + SBW], Zi[0:1, :], Hr[1][0:1, :], ALU.mult),
                                         reads=rk_ + [Y.k(bp + HP, s, sb)], writes=[Y.k(bp + HP, s, sb)])
                self.cp("D")
                with self.scope():
                    if g.samp:
                        Gt = [A.alloc("Gt%d" % q, [8, 512], BF16) for q in range(3)]
                        ng = 0
                        for tq in range(4):
                            bo = [self.bank() for _ in range(g.ncb)]
                            for fg in range(4):
                                G_ = Gt[ng % 3]
                                ng += 1
                                self.load("sp", G_[:], dr["Gs"][tq, fg], [G_.k()], "ld_G%d" % ((ng - 1) % 3))
                                for q in range(8):
                                    fb = fg * 8 + q
                                    for cb in range(g.ncb):
                                        self.mm(self.psb(bo[cb]), Y[:, fb, 0, cb * 128:(cb + 1) * 128], G_[:, q, :],
                                                fb == 0, fb == NFB - 1,
                                                [G_.k(), Y.k(fb, 0, 0)], [self.pk(bo[cb])])
                            for cb in range(g.ncb):
                                P.op("dve", lambda e, cb=cb, tq=tq, bo=bo: e.tensor_tensor(
                                    yg[:, cb, tq * 512:(tq + 1) * 512], self.psb(bo[cb]), xg[:, cb, tq * 512:(tq + 1) * 512], ALU.mult),
                                    reads=[self.pk(bo[cb]), xg.k(cb)], writes=[yg.k(cb, tq)])
                    else:
                        for cb in range(g.ncb):
                            sb = (cb * 128) // SBW
                            for s in range(nseq):
                                b = self.bank()
                                for fb in range(NFB):
                                    self.mm(self.psb(b, n=L), Y[:, fb, s, cb * 128:(cb + 1) * 128], self.Gp[:, fb, :],
                                            fb == 0, fb == NFB - 1, [self.Gp.k(), Y.k(fb, s, sb)], [self.pk(b)])
                                P.op("dve", lambda e, cb=cb, s=s, b=b: e.tensor_tensor(
                                    yg[:, cb, s * L:(s + 1) * L], self.psb(b, n=L), xg[:, cb, s * L:(s + 1) * L], ALU.mult),
                                    reads=[self.pk(b), xg.k(cb)], writes=[yg.k(cb, s)])
                self.cp("E")
                if g.samp:
                    self.gather_and_project(g, i, j, yg, "hy_wout", 1)
                    self.cp("W")

    def mla(self, g, i, j, part):
        P, A, dr = self.P, self.A, self.dr
        n = g.name
        if part == 2:
            self.gather_and_project(g, i, j, None, "mla_wo", 2)
            return
        NT, TB, nh, nseq, L = g.NT, g.TB, g.nh, g.nseq, g.L
        NK = NT + (PAST if g.samp else 0)
        KTB = NK // 512
        NKB = NK // 128
        HG = nh * 64
        ncg = HG // 128
        scale = 1.0 / math.sqrt(QK_NOPE + QK_ROPE)
        R64, R96a, R96 = slice(0, 64), slice(0, 96), slice(64, 96)
        if part == "post":
            og = self._carry
            with self.scope():
                w = self._wout
                ogk = _KeyAll(og, [(c, 0, par) for c in range(ncg) for par in range(2)])
                self.resid_update(g, i, ogk, w, 1)
            if self.use_barriers:
                self.P.barrier()
            self.A.pop()
            return
        if not g.samp:
            self.A.push()
            self._carry = A.alloc("og", [ncg, NT], BF16)
        with self.scope():
            sg = A.alloc("sg", [ncg, NT], BF16)
            qn = A.alloc("qn", [3, NT], BF16)
            ckvT = A.alloc("ckvT", [2, NK], BF16)
            kpeT = A.alloc("kpeT", [NK], BF16)
            with self.scope():
                hb = [A.alloc("h%d" % q, [8, 512], BF16) for q in range(2)]
                sc = self.norm_scratch()
                Wa = A.alloc("wa", [8, 832], BF16)
                Wg = A.alloc("wg", [8, HG], BF16)
                self.load("pool", Wa[:], dr["mla_wa"][j], [Wa.k()], "ld_wa")
                self.load("pool", Wg[:], dr["mla_wg_" + n][j], [Wg.k()], "ld_wg")
                sqb = A.alloc("sqb", [3, 512], BF16)
                rq = A.alloc("rq", [512])
                if g.samp:
                    rope = [A.alloc("rope%d" % q, [2, 512]) for q in range(2)]
                    kt1 = A.alloc("kt1", [512])
                    kt2 = A.alloc("kt2", [512])
                    self.load("pool", ckvT[:, :, NT:NK], dr["cckv"][j], [ckvT.k(0, TB), ckvT.k(1, TB)], "ld_cckv")
                    self.load("pool", kpeT[64:96, NT:NK], dr["ckpe"][j, 64:96], [kpeT.k(TB)], "ld_ckpe")
                else:
                    cko = A.alloc("cko", [2, 512])
                    kpo = A.alloc("kpo", [512])
                for tb in range(TB):
                    ts = slice(tb * 512, (tb + 1) * 512)
                    h = hb[tb % 2]
                    self.norm_tb(g, i, tb, lambda k: h[:, k, :], lambda k: [h.k(k)], sc)
                    if g.samp:
                        rp = rope[tb % 2]
                        self.load("sp", rp[:], dr["rope"][:, :, ts], [rp.k()], "ld_rope%d" % (tb % 2))

                    def proj(c0, M):
                        b = self.bank()
                        for k in range(8):
                            self.mm(self.psb(b, slice(0, M)), Wa[:, k, c0:c0 + M], h[:, k, :], k == 0, k == 7,
                                    [Wa.k(), h.k(k)], [self.pk(b)])
                        return b

                    def rms(banks, nch, inv_n, wv, dst, dkey, extra=None):
                        for m, b in enumerate(banks):
                            P.op("act", lambda e, m=m, b=b: e.activation(sqb[:, m, :], self.psb(b), AF.Square),
                                 reads=[self.pk(b)], writes=[sqb.k(m)])
                        bs = self.bank()
                        for m in range(nch):
                            self.mm(self.psb(bs), self.ones[:], sqb[:, m, :], m == 0, m == nch - 1,
                                    [self.ones.k(), sqb.k(m)], [self.pk(bs)])
                        P.op("act", lambda e: e.activation(rq[:], self.psb(bs), AF.Ln, bias=self.cst[:, 0:1], scale=inv_n),
                             reads=[self.pk(bs), self.cst.k()], writes=[rq.k()])
                        P.op("act", lambda e: e.activation(rq[:], rq[:], AF.Exp, scale=-0.5), reads=[rq.k()], writes=[rq.k()])
                        for m, b in enumerate(banks):
                            P.op("dve", lambda e, m=m, b=b: e.scalar_tensor_tensor(
                                dst[:, m, ts], self.psb(b), wv[:, j, m:m + 1], rq[:], ALU.mult, ALU.mult),
                                reads=[self.pk(b), rq.k(), wv.k()], writes=[dkey(m)])
                            if extra is not None:
                                extra(m, b)

                    bq = [proj(128 * m, 128) for m in range(3)]
                    bkv = [proj(384 + 128 * m, 128) for m in range(2)]
                    rms(bq, 3, 1.0 / Q_LORA, self.qn_w, qn, lambda m: qn.k(m, tb))
                    bk = proj(640, 96)
                    if g.samp:
                        bk2 = proj(736, 96)
                    if g.samp:
                        rms(bkv, 2, 1.0 / KV_LORA, self.kvn_w, ckvT, lambda m: ckvT.k(m, tb))
                    else:
                        def extra(m, b):
                            P.op("dve", lambda e: e.scalar_tensor_tensor(
                                cko[:, m, :], self.psb(b), self.kvn_w[:, j, m:m + 1], rq[:], ALU.mult, ALU.mult),
                                reads=[self.pk(b), rq.k(), self.kvn_w.k()], writes=[cko.k(m)])
                        rms(bkv, 2, 1.0 / KV_LORA, self.kvn_w, ckvT, lambda m: ckvT.k(m, tb), extra)
                        self.load("sp", dr["ckv_out"][j], cko[:], ["ckv_out%d" % j], "st_ckv%d" % j, reads=[cko.k(0), cko.k(1)])
                        self.outs.append("st_ckv%d" % j)
                    if g.samp:
                        P.op("dve", lambda e: e.tensor_tensor(kt1[R96, :], self.psb(bk, R96), rp[R96, 0, :], ALU.mult),
                             reads=[self.pk(bk), rp.k()], writes=[kt1.k()])
                        P.op("dve", lambda e: e.tensor_tensor(kt2[R96, :], self.psb(bk2, R96), rp[R96, 1, :], ALU.mult),
                             reads=[self.pk(bk2), rp.k()], writes=[kt2.k()])
                        P.op("dve", lambda e: e.tensor_tensor(kpeT[R96, ts], kt1[R96, :], kt2[R96, :], ALU.add),
                             reads=[kt1.k(), kt2.k()], writes=[kpeT.k(tb)])
                    else:
                        P.op("act", lambda e: e.activation(kpeT[R96, ts], self.psb(bk, R96), AF.Copy),
                             reads=[self.pk(bk)], writes=[kpeT.k(tb)])
                        P.op("act", lambda e: e.activation(kpo[R96, :], self.psb(bk, R96), AF.Copy),
                             reads=[self.pk(bk)], writes=[kpo.k()])
                        self.load("sp", dr["kpe_out"][j], kpo[R96, :], ["kpe_out%d" % j], "st_kpe%d" % j, reads=[kpo.k()])
                        self.outs.append("st_kpe%d" % j)
                    for m in range(ncg):
                        b = self.bank()
                        for k in range(8):
                            self.mm(self.psb(b), Wg[:, k, 128 * m:128 * m + 128], h[:, k, :], k == 0, k == 7,
                                    [Wg.k(), h.k(k)], [self.pk(b)])
                        P.op("act", lambda e, b=b, m=m: e.activation(sg[:, m, ts], self.psb(b), AF.Silu),
                             reads=[self.pk(b)], writes=[sg.k(m, tb)])
            self.cp("P1")
            QT = A.alloc("QT", [nh, NT], BF16)
            KT = A.alloc("KT", [nh, NK], BF16)
            Va = A.alloc("Vaug", [NKB, nh, 128], BF16)
            with self.scope():
                if g.samp:
                    Wq = A.alloc("wq", [3, nh, 2, 96], BF16)
                    rope = [A.alloc("rope%d" % q, [2, 512]) for q in range(2)]
                    qt1 = A.alloc("qt1", [512])
                    qt2 = A.alloc("qt2", [512])
                else:
                    Wq = A.alloc("wq", [3, nh, 96], BF16)
                Wk = A.alloc("wk", [2, nh, 64], BF16)
                Wv = A.alloc("wv", [2, HG], BF16)
                self.load("pool", Wq[:], dr["mla_wq_" + n][j], [Wq.k()], "ld_wq")
                self.load("pool", Wk[:], dr["mla_wk_" + n][j], [Wk.k()], "ld_wk")
                self.load("pool", Wv[:], dr["mla_wv_" + n][j], [Wv.k()], "ld_wv")
                P.op("pool", lambda e: e.memset(Va[:], 1.0), writes=[Va.k()])
                for tb in range(TB):
                    ts = slice(tb * 512, (tb + 1) * 512)
                    if g.samp:
                        rp = rope[tb % 2]
                        self.load("sp", rp[:], dr["rope"][:, :, ts], [rp.k()], "ld_rope%d" % (tb % 2))
                    for hh in range(nh):
                        b = self.bank()
                        for k in range(3):
                            lw = Wq[:, k, hh, 0, :] if g.samp else Wq[:, k, hh, :]
                            self.mm(self.psb(b, R96a), lw, qn[:, k, ts], k == 0, k == 2,
                                    [Wq.k(), qn.k(k, tb)], [self.pk(b)])
                        if g.samp:
                            b2 = self.bank()
                            for k in range(3):
                                self.mm(self.psb(b2, R96a), Wq[:, k, hh, 1, :], qn[:, k, ts], k == 0, k == 2,
                                        [Wq.k(), qn.k(k, tb)], [self.pk(b2)])
                            P.op("dve", lambda e, b=b, hh=hh, ts=ts: e.tensor_copy(QT[R64, hh, ts], self.psb(b, R64)),
                                 reads=[self.pk(b)], writes=[QT.k(hh, tb, 0)])
                            P.op("dve", lambda e, b=b, rp=rp: e.tensor_tensor(qt1[R96, :], self.psb(b, R96), rp[R96, 0, :], ALU.mult),
                                 reads=[self.pk(b), rp.k()], writes=[qt1.k()])
                            P.op("dve", lambda e, b2=b2, rp=rp: e.tensor_tensor(qt2[R96, :], self.psb(b2, R96), rp[R96, 1, :], ALU.mult),
                                 reads=[self.pk(b2), rp.k()], writes=[qt2.k()])
                            P.op("dve", lambda e, hh=hh, ts=ts: e.tensor_tensor(QT[R96, hh, ts], qt1[R96, :], qt2[R96, :], ALU.add),
                                 reads=[qt1.k(), qt2.k()], writes=[QT.k(hh, tb, 1)])
                        else:
                            P.op("act", lambda e, b=b, hh=hh, ts=ts: e.activation(QT[R96a, hh, ts], self.psb(b, R96a), AF.Copy),
                                 reads=[self.pk(b)], writes=[QT.k(hh, tb, 0), QT.k(hh, tb, 1)])
                import os
                if os.environ.get("K_DBG_P2STOP") == "q":
                    P.muted = True
                for tb in range(KTB):
                    ts = slice(tb * 512, (tb + 1) * 512)
                    for hh in range(nh):
                        b = self.bank()
                        for k in range(2):
                            self.mm(self.psb(b, R64), Wk[:, k, hh, :], ckvT[:, k, ts], k == 0, k == 1,
                                    [Wk.k(), ckvT.k(k, tb)], [self.pk(b)])
                        P.op("dve", lambda e, b=b, hh=hh, ts=ts: e.tensor_copy(KT[R64, hh, ts], self.psb(b, R64)),
                             reads=[self.pk(b)], writes=[KT.k("n", tb, hh)])
                        P.op("pool", lambda e, hh=hh, ts=ts: e.tensor_copy(KT[R96, hh, ts], kpeT[R96, ts]),
                             reads=[kpeT.k(tb)], writes=[KT.k("pe", tb, hh)])
                if os.environ.get("K_DBG_P2STOP") == "k":
                    P.muted = True
                for kb in range(min(NKB, int(os.environ.get("K_DBG_NKB", NKB)))):
                    for c0 in range(0, HG, 512):
                        w_ = min(512, HG - c0)
                        b = self.bank()
                        for k in range(2):
                            self.mm(self.psb(b, n=w_), ckvT[:, k, kb * 128:(kb + 1) * 128], Wv[:, k, c0:c0 + w_],
                                    k == 0, k == 1, [Wv.k(), ckvT.k(k, kb // 4)], [self.pk(b)])
                        nhh = w_ // 64
                        h0 = c0 // 64
                        for q in range(nhh):
                            hh = h0 + q
                            par = hh % 2
                            src = self.psb(b, n=64, off=q * 64)
                            dst = Va[:, kb, hh, par * 64:(par + 1) * 64]
                            if kb % 2 == 0 or os.environ.get("K_DBG_VACT"):
                                P.op("act", lambda e, src=src, dst=dst: e.activation(dst, src, AF.Copy),
                                     reads=[self.pk(b), Va.k()], writes=[Va.k(kb, hh)])
                            else:
                                P.op("dve", lambda e, src=src, dst=dst: e.tensor_copy(dst, src),
                                     reads=[self.pk(b), Va.k()], writes=[Va.k(kb, hh)])
            self.cp("P2")
            with self.scope():
                og = A.alloc("og", [ncg, NT], BF16) if g.samp else self._carry
                NPT = 3
                PT = [A.alloc("PT%d" % q, [1024], BF16) for q in range(NPT)]
                NIF = 2 if g.samp else 4
                rdb = [A.alloc("rdb%d" % q, [512], BF16) for q in range(NIF)]
                if g.samp:
                    ot = [A.alloc("ot", [512])] * NIF
                    rdf = [A.alloc("rdf", [512])] * NIF
                else:
                    ot = [A.alloc("ot%d" % q, [512]) for q in range(NIF)]
                    rdf = [A.alloc("rdf%d" % q, [512]) for q in range(NIF)]
                for q in range(NIF):
                    P.op("dve", lambda e, q=q: e.memset(rdb[q][:], 0.0), writes=[rdb[q].k()])
                QB = min(512, L)
                npt = 0
                nat = 0
                nsp = 0
                tails = []
                DEFER = 1 if g.samp else 2

                def emit_tail(tl):
                    bO, rd, o_, rf, Ro, Rd, hh, qs, qtb, par = tl
                    if g.samp:
                        P.op("dve", lambda e: e.reciprocal(rf[Rd, 0:QB], self.psb(bO, Rd, QB)),
                             reads=[self.pk(bO)], writes=[rf.k()])
                        P.op("dve", lambda e: e.tensor_copy(rd[Rd, 0:QB], rf[Rd, 0:QB]),
                             reads=[rf.k()], writes=[rd.k()])
                    else:
                        P.op("act", lambda e: e.activation(rf[Rd, 0:QB], self.psb(bO, Rd, QB), AF.Ln),
                             reads=[self.pk(bO)], writes=[rf.k()])
                        P.op("act", lambda e: e.activation(rd[Rd, 0:QB], rf[Rd, 0:QB], AF.Exp, scale=-1.0),
                             reads=[rf.k()], writes=[rd.k()])
                    nonlocal nsp
                    if g.samp:
                        bR = 2 * (nsp % 3)
                        nsp += 1
                    else:
                        bR = 7
                    self.mm(self.psb(bR, n=QB), self.swap[:], rd[:, 0:QB], True, True,
                            [self.swap.k(), rd.k()], [self.pk(bR)])
                    P.op("dve", lambda e: e.tensor_tensor(o_[Ro, 0:QB], self.psb(bO, Ro, QB), sg[Ro, hh // 2, qs], ALU.mult),
                         reads=[self.pk(bO), sg.k(hh // 2, qtb), rf.k()], writes=[o_.k()])
                    P.op("dve", lambda e: e.tensor_tensor(og[Ro, hh // 2, qs], o_[Ro, 0:QB], self.psb(bR, Ro, QB), ALU.mult),
                         reads=[self.pk(bR), o_.k()], writes=[og.k(hh // 2, qtb, par)])
                for s in range(nseq):
                    if g.samp:
                        kbs = list(range(NKB))
                    else:
                        kbs = [s * (L // 128) + q for q in range(L // 128)]
                    npair = len(kbs) // 2
                    for hh in range(nh):
                        par = hh % 2
                        Ro = slice(par * 64, par * 64 + 64)
                        Rd = slice((1 - par) * 64, (1 - par) * 64 + 64)
                        for qb in range(L // QB):
                            q0 = s * L + qb * QB
                            qs = slice(q0, q0 + QB)
                            qtb = q0 // 512
                            bO = (6 + (nat % 2)) if g.samp else (3 + (nat % 4))
                            pend = []
                            for ip in range(npair):
                                if g.samp:
                                    b0 = 2 * (nsp % 3)
                                    sb_ = [(b0, 0), (b0 + 1, 0)]
                                    Sv = self.ps[:, b0 * 512:b0 * 512 + 1024].rearrange("p (k n) -> p k n", k=2)[:, :, 0:QB]
                                    spk = [self.pk(b0), self.pk(b0 + 1)]
                                else:
                                    b0 = nsp % 3
                                    sb_ = [(b0, 0), (b0, QB)]
                                    Sv = self.ps[:, b0 * 512:b0 * 512 + 2 * QB].rearrange("p (k n) -> p k n", k=2)
                                    spk = [self.pk(b0)]
                                nsp += 1
                                for u in range(2):
                                    kb = kbs[2 * ip + u]
                                    ktb = kb // 4
                                    self.mm(self.psb(sb_[u][0], n=QB, off=sb_[u][1]), KT[0:96, hh, kb * 128:(kb + 1) * 128],
                                            QT[0:96, hh, qs], True, True,
                                            [KT.k("n", ktb, hh), KT.k("pe", ktb, hh), QT.k(hh, qtb, 0), QT.k(hh, qtb, 1)],
                                            [self.pk(sb_[u][0])])
                                pt = PT[npt % NPT]
                                npt += 1
                                Pv = pt[:, 0:2 * QB].rearrange("p (k n) -> p k n", k=2)
                                P.op("act", lambda e, Sv=Sv, Pv=Pv: e.activation(Pv, Sv, AF.Exp, scale=scale),
                                     reads=spk, writes=[pt.k()])
                                pend.append((ip, kbs[2 * ip], kbs[2 * ip + 1], hh, pt))
                                if len(pend) > 2:
                                    self._pv2(bO, Va, pend.pop(0), QB, npair)
                            for pp in pend:
                                self._pv2(bO, Va, pp, QB, npair)
                            tails.append((bO, rdb[nat % NIF], ot[nat % NIF], rdf[nat % NIF], Ro, Rd, hh, qs, qtb, par))
                            nat += 1
                            if len(tails) > DEFER:
                                emit_tail(tails.pop(0))
                for tl in tails:
                    emit_tail(tl)
                self.cp("ATT")
                if g.samp:
                    P.op("sp", lambda e: e.dma_start(out=dr["cc_in%d" % i].rearrange("(c p) t -> p c t", p=128), in_=og[:]),
                         reads=[og.k(c, tb, par) for c in range(ncg) for tb in range(TB) for par in range(2)],
                         writes=["cc_in%d" % i], dma_sem="st_cc%d" % i)
                    P.op("pool", lambda e: e.collective_compute("AllGather", ALU.bypass, replica_groups=GROUP4,
                                                                ins=[dr["cc_in%d" % i]], outs=[dr["cc_out%d" % i]]),
                         reads=["cc_in%d" % i], writes=["cc_out%d" % i], dma_sem="cc%d" % i, inc=1, persist=True)

    def _pv2(self, bO, Va, prev, QB, npair):
        ip, kb0, kb1, hh, pt = prev
        for u, kb in enumerate((kb0, kb1)):
            self.mm(self.psb(bO, n=QB), Va[:, kb, hh, :], pt[:, u * QB:(u + 1) * QB], ip == 0 and u == 0,
                    ip == npair - 1 and u == 1, [Va.k(kb, hh), Va.k(), pt.k()], [self.pk(bO)])

    def _pv(self, bO, Va, prev, QB, nk):
        ix, kb, hh, pt = prev
        self.mm(self.psb(bO, n=QB), Va[:, kb, hh, :], pt[:, 0:QB], ix == 0, ix == nk - 1,
                [Va.k(kb, hh), Va.k(), pt.k()], [self.pk(bO)])


class _KeyAll:
    def __init__(self, t, subs):
        self.t = t
        self.subs = subs
        self.ap = t.ap

    def __getitem__(self, idx):
        return self.t.ap[idx]

    def k(self, *sub):
        if not sub:
            return self.t.k()
        kk = sub[0]
        return _MultiKey([self.t.k(*s) for s in self.subs if s[0] == kk])


class _MultiKey(list):
    pass


_orig_op = Prog.op


def _op_flat(self, eng, fn, reads=(), writes=(), **kw):
    def flat(ks):
        out = []
        for k in ks:
            if isinstance(k, _MultiKey):
                out.extend(k)
            else:
                out.append(k)
        return out
    return _orig_op(self, eng, fn, flat(reads), flat(writes), **kw)


Prog.op = _op_flat


def _final(self, g, oname):
    P, A = self.P, self.A
    x = self.x[g.name]
    dst = self.dr[oname]
    with self.scope():
        yo = [A.alloc("yo%d" % q, [8, 512]) for q in range(2)]
        sq = [A.alloc("fsq%d" % q, [8, 512], BF16) for q in range(2)]
        rs = [A.alloc("frs%d" % q, [512]) for q in range(2)]
        for tb in range(g.NT // 512):
            y_, s_, r_ = yo[tb % 2], sq[tb % 2], rs[tb % 2]
            ts = slice(tb * 512, (tb + 1) * 512)
            P.op("act", lambda e, s_=s_, ts=ts: e.activation(s_[:], x[:, :, ts], AF.Square),
                 reads=[x.k(k, tb) for k in range(8)], writes=[s_.k()])
            b = self.bank()
            for k in range(8):
                self.mm(self.psb(b), self.ones[:], s_[:, k, :], k == 0, k == 7, [self.ones.k(), s_.k()], [self.pk(b)])
            P.op("act", lambda e, b=b, r_=r_: e.activation(r_[:], self.psb(b), AF.Ln, bias=self.cst[:, 0:1], scale=1.0 / D),
                 reads=[self.pk(b), self.cst.k()], writes=[r_.k()])
            P.op("act", lambda e, r_=r_: e.activation(r_[:], r_[:], AF.Exp, scale=-0.5), reads=[r_.k()], writes=[r_.k()])
            for k in range(8):
                P.op("dve", lambda e, k=k, ts=ts, y_=y_, r_=r_: e.scalar_tensor_tensor(
                    y_[:, k, :], x[:, k, ts], self.fnorm[:, k:k + 1], r_[:], ALU.mult, ALU.mult),
                    reads=[x.k(k, tb), r_.k(), self.fnorm.k(), y_.k()], writes=[y_.k()])
            self.load("sp", dst[:, :, ts], y_[:], [(oname, tb)], "st_%s%d" % (oname, tb % 2), reads=[y_.k()])
            if ("st_%s%d" % (oname, tb % 2)) not in self.outs:
                self.outs.append("st_%s%d" % (oname, tb % 2))


Builder.final = _final


def _prep(inp):
    C = _consts()
    f = _f32
    common = {}
    common["cond"] = f(_chunk(np.stack([inp["c_ctx"], inp["c"][0], inp["c"][1]], axis=1)))
    common["normw"] = f(_chunk(inp["norm_w"].T))
    common["normw"] = f(common["normw"].transpose(0, 2, 1))
    common["fnorm"] = f(_chunk(inp["final_norm"][:, None])[:, :, 0])
    for k in ("ident", "ones", "swap", "Fs", "Gs", "Fp", "Gp", "zemb_s", "zemb_p", "tv_s", "tv_p", "rope"):
        common[k] = C[k]
    common["hy_fw1"] = f(inp["hy_f_w1"])
    common["hy_fw2"] = f(inp["hy_f_w2"])
    fv = np.zeros((2, 64, 4), np.float32)
    fv[:, :, 0] = inp["hy_f_b1"]
    fv[:, :, 1] = inp["hy_f_freq"]
    fv[:, :, 2] = inp["hy_f_b2"]
    common["hy_fvec"] = fv
    common["hy_wout"] = f(np.stack([_chunk(inp["hy_w_out"][j]) for j in range(2)]))
    common["mla_wo"] = f(np.stack([_chunk(inp["mla_w_o"][j]) for j in range(2)]))
    common["mla_qn"] = f(_chunk(inp["mla_q_norm"].T))
    common["mla_qn"] = f(common["mla_qn"].transpose(0, 2, 1))
    common["mla_kvn"] = f(_chunk(inp["mla_kv_norm"].T).transpose(0, 2, 1))
    sw = C["ropesw"]
    wa = []
    for j in range(2):
        w = inp["mla_w_in"][j]
        kpe = w[:, 640:672]
        z64 = np.zeros((D, 64), np.float32)
        wa.append(_chunk(np.concatenate([w[:, :640], z64, kpe, z64, kpe[:, sw]], axis=1)))
    common["mla_wa"] = f(np.stack(wa))

    def hy_group(chs, tag, out):
        cbs = [chs[q * 128:(q + 1) * 128] for q in range(len(chs) // 128)]
        win = np.zeros((2, len(cbs), 128, 8, 512), np.float32)
        cv = np.zeros((2, 128, len(cbs), 3, 4), np.float32)
        for j in range(2):
            w = inp["hy_w_in"][j]
            for q, cb in enumerate(cbs):
                cols = np.concatenate([gi * D + cb for gi in range(4)])
                win[j, q] = _chunk(w[:, cols])
                for gi in range(3):
                    cv[j, :, q, gi, 0:3] = inp["hy_conv_w"][j][:, gi * D + cb].T
                    cv[j, :, q, gi, 3] = inp["hy_conv_b"][j][gi * D + cb]
        out["hy_win_" + tag] = win
        out["hy_conv_" + tag] = cv
        out["hy_fw3_" + tag] = f(np.stack([inp["hy_f_w3"][j][:, np.concatenate([chs, D + chs])] for j in range(2)]))
        out["hy_fbias_" + tag] = f(np.stack([np.broadcast_to(inp["hy_f_bias"][j][chs], (128, len(chs))) for j in range(2)]))
        out["ndelta_" + tag] = f(np.broadcast_to(C["ndelta"][chs], (128, len(chs))))

    def mla_group(heads, tag, out):
        nh = len(heads)
        gcols = np.concatenate([672 + h * 64 + np.arange(64) for h in heads])
        out["mla_wg_" + tag] = f(np.stack([_chunk(inp["mla_w_in"][j][:, gcols]) for j in range(2)]))
        wk = np.zeros((2, 128, 2, nh, 64), np.float32)
        wv = np.zeros((2, 128, 2, nh * 64), np.float32)
        for j in range(2):
            w = inp["mla_w_kvb"][j].reshape(KV_LORA, NH, 128)
            wk[j] = _chunk(w[:, heads, :64])
            wv[j] = _chunk(w[:, heads, 64:].reshape(KV_LORA, nh * 64))
        out["mla_wk_" + tag] = wk
        out["mla_wv_" + tag] = wv
        wq = np.stack([inp["mla_w_qb"][j].reshape(Q_LORA, NH, 96)[:, heads] for j in range(2)])
        if tag == "p":
            out["mla_wq_p"] = f(np.stack([_chunk(wq[j]) for j in range(2)]))
        else:
            wsw = np.concatenate([wq[..., :64], wq[..., 64:][..., sw]], axis=-1)
            both = np.stack([wq, wsw], axis=3)
            out["mla_wq_s"] = f(np.stack([_chunk(both[j]) for j in range(2)]))

    hy_group(np.arange(D), "p", common)
    mla_group(list(range(NH)), "p", common)
    maps = []
    for c in range(NCORES):
        g, r = c // 4, c % 4
        m = dict(common)
        xp = inp["x_prompt"][2 * c:2 * c + 2].reshape(512, D)
        m["xp"] = f(_chunk(xp.T))
        m["xs"] = f(_chunk(inp["x_sample"][g].T))
        m["ada_w"] = f(np.stack([_chunk(inp["ada_w"][i][:, 768 * r:768 * (r + 1)]) for i in range(4)]))
        m["ada_b"] = f(inp["ada_b"][:, 768 * r:768 * (r + 1)].reshape(4, 6, 128).transpose(2, 0, 1))
        sel = np.zeros((128, 2), np.float32)
        sel[:, g] = 1.0
        m["sel"] = sel
        hy_group(np.arange(256 * r, 256 * (r + 1)), "s", m)
        mla_group(list(range(4 * r, 4 * r + 4)), "s", m)
        m["cckv"] = f(np.stack([_chunk(inp["cache_ckv"][g, j].T) for j in range(2)]))
        ck = np.zeros((2, 128, 512), np.float32)
        for j in range(2):
            ck[j, 64:96] = inp["cache_kpe"][g, j].T
        m["ckpe"] = ck
        maps.append(m)
    return maps


_NC_CACHE = {}


def _get_nc(depth=DEPTH):
    if depth not in _NC_CACHE:
        _NC_CACHE[depth] = Builder(depth).build()
    return _NC_CACHE[depth]


def kernel(**inputs):
    inp = {k: np.asarray(v) for k, v in inputs.items()}
    maps = _prep(inp)
    nc = _get_nc()
    res = run_bass_kernel_spmd(nc, maps, core_ids=list(range(NCORES)))
    R = res.results
    y_prompt = np.zeros((BATCH, SEQ, D), np.float32)
    state_ckv = np.zeros((BATCH, 2, SEQ, KV_LORA), np.float32)
    state_kpe = np.zeros((BATCH, 2, SEQ, QK_ROPE), np.float32)
    y_sample = np.zeros((DEC_BATCH, DEC_SEQ, D), np.float32)
    for c in range(NCORES):
        yp = np.asarray(R[c]["yp"]).transpose(2, 1, 0).reshape(512, D)
        y_prompt[2 * c:2 * c + 2] = yp.reshape(2, SEQ, D)
        ck = np.asarray(R[c]["ckv_out"])
        kp = np.asarray(R[c]["kpe_out"])
        for j in range(2):
            t = ck[j].transpose(2, 1, 0).reshape(512, KV_LORA)
            state_ckv[2 * c:2 * c + 2, j] = t.reshape(2, SEQ, KV_LORA)
            state_kpe[2 * c:2 * c + 2, j] = kp[j].T.reshape(2, SEQ, QK_ROPE)
    for g in range(DEC_BATCH):
        ys = np.asarray(R[4 * g]["ys"]).transpose(2, 1, 0).reshape(DEC_SEQ, D)
        y_sample[g] = ys
    return (y_prompt, y_sample, state_ckv, state_kpe)
```

```python
import math
from contextlib import ExitStack

import numpy as np
import ml_dtypes

import concourse.bass as bass
import concourse.mybir as mybir
from concourse.bass_utils import run_bass_kernel_spmd

F32 = mybir.dt.float32
BF16 = mybir.dt.bfloat16
I32 = mybir.dt.int32
AF = mybir.ActivationFunctionType
ALU = mybir.AluOpType

NCORES = 8
D = 1024
DEPTH = 4
BATCH, SEQ = 16, 256
DEC_BATCH, DEC_SEQ = 2, 2048
PAST = 512
EPS = 1e-6
NH = 16
Q_LORA, KV_LORA, QK_NOPE, QK_ROPE, V_HEAD = 384, 256, 64, 32, 64
GRID_W = 64
ROPE_THETA = 10000.0
ENGS = ("pe", "act", "dve", "pool", "sp")
GROUP4 = [[0, 1, 2, 3], [4, 5, 6, 7]]
GROUP8 = [[0, 1, 2, 3, 4, 5, 6, 7]]


class _Op:
    __slots__ = ("eng", "fn", "deps", "signals", "val", "dma_sem", "inc", "persist", "idx")

    def __init__(self, eng, fn, deps, dma_sem=None):
        self.eng = eng
        self.fn = fn
        self.deps = deps
        self.signals = False
        self.val = None
        self.dma_sem = dma_sem
        self.inc = 16
        self.persist = False


class _Rec:
    def __init__(self):
        self.call = None

    def __getattr__(self, name):
        def f(*a, **kw):
            self.call = (name, a, kw)
            return self
        return f


class _Reg:
    __slots__ = ("w", "r")

    def __init__(self):
        self.w = None
        self.r = []


class Prog:
    def __init__(self, nc):
        self.nc = nc
        self.ops = {e: [] for e in ENGS}
        self.regs = {}
        self.dma_counts = {}
        self.last = {e: None for e in ENGS}
        self.dma_since_bar = {}
        self.tile_keys = {}
        self.hazards = {}

    def _reg(self, k):
        r = self.regs.get(k)
        if r is None:
            r = self.regs[k] = _Reg()
            tk = k if isinstance(k, str) else k[0]
            self.tile_keys.setdefault(tk, set()).add(k)
        return r

    def tile_ops(self, tkey):
        out = []
        for k in self.tile_keys.get(tkey, ()):
            rg = self.regs[k]
            if rg.w is not None:
                out.append(rg.w)
            out.extend(rg.r)
        return out

    muted = False

    def op(self, eng, fn, reads=(), writes=(), dma_sem=None, inc=16, persist=False):
        if self.muted:
            return None
        deps = []
        seen = set()

        def add(d):
            if d is None or id(d) in seen:
                return
            if eng == "pe" and d.eng == "pe" and d.dma_sem is None:
                return
            seen.add(id(d))
            deps.append(d)

        if self.hazards:
            for k in list(reads) + list(writes):
                hz = self.hazards.get(k if isinstance(k, str) else k[0])
                if hz:
                    for d in hz:
                        add(d)
        for k in reads:
            add(self._reg(k).w)
        for k in writes:
            rg = self._reg(k)
            w = rg.w
            if w is not None and dma_sem is not None and w.dma_sem == dma_sem:
                for d in w.deps:
                    add(d)
            else:
                add(w)
            for d in rg.r:
                add(d)
        rec = _Rec()
        fn(rec)
        call = rec.call
        import sys as _sys
        fr = _sys._getframe(2)
        where = "%s:%d" % (fr.f_code.co_name, fr.f_lineno)

        def _do(e, c=call, where=where):
            try:
                return getattr(e, c[0])(*c[1], **c[2])
            except Exception as ex:
                raise RuntimeError("emit failed for op recorded at %s: %s %s" % (where, c[0], ex)) from ex
        o = _Op(eng, _do, deps, dma_sem)
        self.nops = getattr(self, "nops", 0) + 1
        o.idx = self.nops
        if dma_sem is not None:
            v = self.dma_counts.get(dma_sem, 0) + inc
            self.dma_counts[dma_sem] = v
            o.val = v
            o.inc = inc
            o.persist = persist
            if not persist:
                self.dma_since_bar[dma_sem] = o
        for k in reads:
            self._reg(k).r.append(o)
        for k in writes:
            rg = self._reg(k)
            rg.w = o
            rg.r = []
        self.ops[eng].append(o)
        if dma_sem is None:
            self.last[eng] = o
        return o

    def barrier(self):
        lasts = [o for o in self.last.values() if o is not None]
        dmas = list(self.dma_since_bar.values())
        self.dma_since_bar = {}
        self.hazards = {}
        new = {}
        for e in ENGS:
            deps = [o for o in lasts if o.eng != e or e != "pe"] + dmas
            if e == "pe":
                deps = [o for o in deps if not (o.eng == "pe" and o.dma_sem is None)]
            b = _Op(e, None, deps)
            self.ops[e].append(b)
            new[e] = b

    def emit(self, final_dma_sems=()):
        nc = self.nc
        for e in ENGS:
            for o in self.ops[e]:
                for d in o.deps:
                    d.signals = True
        with ExitStack() as st:
            esem = {e: st.enter_context(nc.semaphore("s_" + e)) for e in ENGS}
            dsem = {k: st.enter_context(nc.semaphore("d_%d" % i))
                    for i, k in enumerate(self.dma_counts)}
            for e in ENGS:
                c = 0
                for o in self.ops[e]:
                    if o.dma_sem is None and o.signals:
                        c += 1
                        o.val = c
            block = st.enter_context(nc.Block())
            engobj = {"pe": "tensor", "act": "scalar", "dve": "vector", "pool": "gpsimd", "sp": "sync"}

            def run(e, eng):
                waited = {}
                for o in self.ops[e]:
                    need = {}
                    for d in o.deps:
                        if d.dma_sem is not None:
                            key = ("d", d.dma_sem)
                            sem = dsem[d.dma_sem]
                        else:
                            key = ("e", d.eng)
                            sem = esem[d.eng]
                        if waited.get(key, 0) >= d.val:
                            continue
                        if key not in need or need[key][1] < d.val:
                            need[key] = (sem, d.val)
                    for key, (sem, v) in need.items():
                        eng.wait_ge(sem, v)
                        waited[key] = v
                    if o.fn is None:
                        continue
                    ins = o.fn(eng)
                    if o.dma_sem is not None:
                        ins.then_inc(dsem[o.dma_sem], o.inc)
                    elif o.signals:
                        ins.then_inc(esem[e], 1)
                if e == "sp":
                    for k in final_dma_sems:
                        if k in dsem:
                            eng.wait_ge(dsem[k], self.dma_counts[k])

            for e in ENGS:
                def mk(e):
                    def f(eng):
                        run(e, eng)
                    return f
                getattr(block, engobj[e])(mk(e))


class T:
    _n = 0

    def __init__(self, ap, name):
        T._n += 1
        self.ap = ap
        self.key = "%s#%d" % (name, T._n)

    def __getitem__(self, idx):
        return self.ap[idx]

    def k(self, *sub):
        return (self.key,) + tuple(sub) if sub else self.key


class Arena:
    def __init__(self, nc, st, words):
        self.t = st.enter_context(nc.sbuf_tensor("arena", [128, words], F32))
        self.words = words
        self.top = 0
        self.stack = []
        self.peak = 0
        self.live = []
        self.dead = []
        self.prog = None

    def alloc(self, name, free_shape, dt=F32):
        n = int(np.prod(free_shape))
        w = n if dt in (F32, I32) else (n + 1) // 2
        w = (w + 7) // 8 * 8
        off = self.top
        self.top += w
        self.peak = max(self.peak, self.top)
        assert self.top <= self.words, "SBUF arena overflow: %s needs %d words, top %d" % (name, w, self.top)
        ap = self.t[:, off:off + w]
        if dt != F32:
            ap = ap.bitcast(dt)
        ap = ap[:, 0:n]
        if len(free_shape) == 2:
            ap = ap.rearrange("p (a b) -> p a b", a=free_shape[0])
        elif len(free_shape) == 3:
            ap = ap.rearrange("p (a b c) -> p a b c", a=free_shape[0], b=free_shape[1])
        elif len(free_shape) == 4:
            ap = ap.rearrange("p (a b c d) -> p a b c d", a=free_shape[0], b=free_shape[1], c=free_shape[2])
        t = T(ap, name)
        if self.prog is not None:
            hz = []
            seen = set()
            for (s0, e0, old) in self.dead:
                if s0 < off + w and off < e0:
                    for o in self.prog.tile_ops(old.key):
                        if id(o) not in seen:
                            seen.add(id(o))
                            hz.append(o)
            if hz:
                best = {}
                for o in hz:
                    kk = ("d", o.dma_sem) if o.dma_sem is not None else ("e", o.eng)
                    if kk not in best or best[kk].idx < o.idx:
                        best[kk] = o
                self.prog.hazards[t.key] = list(best.values())
        self.live.append((off, off + w, t))
        return t

    def push(self):
        self.stack.append(self.top)

    def pop(self):
        self.top = self.stack.pop()
        keep = []
        for it in self.live:
            if it[0] >= self.top:
                self.dead.append(it)
            else:
                keep.append(it)
        self.live = keep

    def clear_dead(self):
        self.dead = []


def _bf(a):
    return np.ascontiguousarray(np.asarray(a, np.float32).astype(ml_dtypes.bfloat16))


def _f32(a):
    return np.ascontiguousarray(np.asarray(a, np.float32))


def _dft(L):
    n = 2 * L
    t = np.arange(L, dtype=np.float64)[:, None]
    kre = np.arange(0, L + 1, dtype=np.float64)[None, :]
    kim = np.arange(1, L, dtype=np.float64)[None, :]
    return np.concatenate([np.cos(2 * np.pi * kre * t / n), -np.sin(2 * np.pi * kim * t / n)], axis=1)


def _zemb(L):
    f32 = np.float32
    t = np.linspace(0.0, 1.0, L, dtype=f32)[:, None]
    w = (f32(2.0 * math.pi / L) * np.arange(L, dtype=f32))[:, None]
    bands = np.linspace(1e-4, 15, 16, dtype=f32)[None, :]
    z = np.concatenate([t, np.cos(bands * w), -np.sin(bands * w)], axis=-1)
    return z.T


def _chunk(w):
    K = w.shape[0]
    return w.reshape(K // 128, 128, *w.shape[1:]).swapaxes(0, 1)


_CONST = {}


def _consts():
    if _CONST:
        return _CONST
    Ls, Lp = DEC_SEQ, SEQ
    F = _dft(Ls)
    _CONST["Fs"] = _bf(F.reshape(16, 128, 32, 128).transpose(2, 1, 0, 3))
    G = F.T
    _CONST["Gs"] = _bf(G.reshape(4, 8, 128, 4, 512).transpose(3, 0, 2, 1, 4))
    Fq = _dft(Lp)
    _CONST["Fp"] = _bf(Fq.reshape(2, 128, 512).transpose(1, 0, 2))
    _CONST["Gp"] = _bf(Fq.T.reshape(4, 128, 256).transpose(1, 0, 2))
    _CONST["zemb_s"] = _bf(_zemb(Ls))
    _CONST["zemb_p"] = _bf(_zemb(Lp))
    _CONST["tv_s"] = _f32(np.linspace(0.0, 1.0, Ls, dtype=np.float32).reshape(16, 128).T)
    _CONST["tv_p"] = _f32(np.linspace(0.0, 1.0, Lp, dtype=np.float32).reshape(2, 128).T)
    maxd = math.log(1e-2) / 0.3
    mind = math.log(1e-2) / 1.5
    _CONST["ndelta"] = -np.abs(np.linspace(mind, maxd, D, dtype=np.float32))
    ident = np.eye(128, dtype=np.float32)
    _CONST["ident"] = _bf(ident)
    _CONST["ones"] = _bf(np.ones((128, 128), np.float32))
    _CONST["swap"] = _bf(np.roll(ident, 64, axis=1))
    rows = Ls // GRID_W
    axis_dim = QK_ROPE // 2
    inv = (ROPE_THETA ** (-np.arange(0, axis_dim, 2, dtype=np.float32) / axis_dim)).astype(np.float32)
    row = np.repeat(np.arange(rows, dtype=np.float32), GRID_W)
    col = np.tile(np.arange(GRID_W, dtype=np.float32), rows)
    ar = (row[:, None] * inv).astype(np.float32)
    ac = (col[:, None] * inv).astype(np.float32)
    cs = np.zeros((128, 2, Ls), np.float32)
    cs[64:72, 0] = np.cos(ar).T
    cs[72:80, 0] = np.cos(ar).T
    cs[80:88, 0] = np.cos(ac).T
    cs[88:96, 0] = np.cos(ac).T
    cs[64:72, 1] = -np.sin(ar).T
    cs[72:80, 1] = np.sin(ar).T
    cs[80:88, 1] = -np.sin(ac).T
    cs[88:96, 1] = np.sin(ac).T
    _CONST["rope"] = cs
    _CONST["ropesw"] = np.concatenate([np.arange(8, 16), np.arange(0, 8), np.arange(24, 32), np.arange(16, 24)])
    return _CONST


class Grp:
    def __init__(self, name, NT, nseq, ncb, nh):
        self.name = name
        self.NT = NT
        self.nseq = nseq
        self.L = NT // nseq
        self.TB = max(1, NT // 512)
        self.NB = self.L // 128
        self.ncb = ncb
        self.C = ncb * 128
        self.nh = nh
        self.SBW = min(512, self.C)
        self.nsb = self.C // self.SBW
        self.NFB = 2 * self.L // 128
        self.samp = name == "s"


GP = Grp("p", 512, 2, 8, 16)
GS = Grp("s", 2048, 1, 2, 4)


class Builder:
    def __init__(self, depth=DEPTH):
        self.depth = depth
        self.nc = bass.Bass("TRN2", target_bir_lowering=False)
        self.P = Prog(self.nc)
        self.dr = {}
        self._bank = 0
        self._b4 = 0
        self.outs = []

    def din(self, name, shape, dt=F32):
        self.dr[name] = self.nc.dram_tensor(name, list(shape), dt, kind="ExternalInput").ap()
        return self.dr[name]

    def dout(self, name, shape, dt=F32):
        self.dr[name] = self.nc.dram_tensor(name, list(shape), dt, kind="ExternalOutput").ap()
        return self.dr[name]

    def dint(self, name, shape, dt=F32):
        self.dr[name] = self.nc.dram_tensor(name, list(shape), dt).ap()
        return self.dr[name]

    def bank(self):
        b = self._bank
        self._bank = (self._bank + 1) % 8
        return b

    def bank4(self):
        b = self._b4 * 4
        self._b4 ^= 1
        return b

    def psb(self, b, rows=slice(0, 128), n=512, off=0):
        return self.ps[rows, b * 512 + off:b * 512 + off + n]

    def pk(self, b):
        return ("ps", b)

    def load(self, q, dst, src, writes, sem, reads=(), persist=False):
        return self.P.op(q, lambda e: e.dma_start(out=dst, in_=src), reads=reads, writes=writes,
                         dma_sem=sem, persist=persist)

    def mm(self, out, lhsT, rhs, start, stop, reads, writes):
        self.P.op("pe", lambda e: e.matmul(out, lhsT, rhs, start=start, stop=stop), reads=reads, writes=writes)

    stop_at = None
    _cp = 0
    use_barriers = False

    def cp(self, name=""):
        import os
        self._cp += 1
        if self.stop_at is not None and self._cp >= self.stop_at:
            self.P.muted = True
        mr = os.environ.get("K_DBG_MUTE")
        if mr:
            a, b = [int(v) for v in mr.split(",")]
            self.P.muted = a <= self._cp < b

    def scope(self):
        b = self

        class _S:
            def __enter__(s):
                b.A.push()

            def __exit__(s, *a):
                if b.use_barriers:
                    b.P.barrier()
                    b.A.pop()
                    b.A.clear_dead()
                else:
                    b.A.pop()
        return _S()

    def declare(self):
        d = self.din
        d("xp", [128, 8, 512]); d("xs", [128, 8, 2048])
        d("cond", [128, 8, 3]); d("ada_w", [4, 128, 8, 768]); d("ada_b", [128, 4, 6])
        d("sel", [128, 2]); d("normw", [128, 4, 8]); d("fnorm", [128, 8])
        d("ident", [128, 128], BF16); d("ones", [128, 128], BF16); d("swap", [128, 128], BF16)
        for g in (GP, GS):
            n = g.name
            d("hy_win_" + n, [2, g.ncb, 128, 8, 512])
            d("hy_conv_" + n, [2, 128, g.ncb, 3, 4])
            d("hy_fw3_" + n, [2, 64, 2 * g.C])
            d("hy_fbias_" + n, [2, 128, g.C])
            d("ndelta_" + n, [128, g.C])
            d("zemb_" + n, [33, g.L], BF16)
            d("tv_" + n, [128, g.NB])
            d("mla_wg_" + n, [2, 128, 8, g.nh * 64])
            d("mla_wk_" + n, [2, 128, 2, g.nh, 64])
            d("mla_wv_" + n, [2, 128, 2, g.nh * 64])
        d("mla_wq_p", [2, 128, 3, 16, 96]); d("mla_wq_s", [2, 128, 3, 4, 2, 96])
        d("hy_fw1", [2, 33, 64]); d("hy_fvec", [2, 64, 4]); d("hy_fw2", [2, 64, 64])
        d("hy_wout", [2, 128, 8, 1024])
        d("Fs", [32, 128, 16, 128], BF16); d("Gs", [4, 4, 128, 8, 512], BF16)
        d("Fp", [128, 2, 512], BF16); d("Gp", [128, 4, 256], BF16)
        d("rope", [128, 2, 2048])
        d("mla_wa", [2, 128, 8, 832]); d("mla_qn", [128, 2, 3]); d("mla_kvn", [128, 2, 2])
        d("mla_wo", [2, 128, 8, 1024])
        d("cckv", [2, 128, 2, 512]); d("ckpe", [2, 128, 512])
        self.dout("yp", [128, 8, 512]); self.dout("ys", [128, 8, 2048])
        self.dout("ckv_out", [2, 128, 2, 512]); self.dout("kpe_out", [2, 32, 512])
        self.dint("cc_ada_in", [128, 72]); self.dint("cc_ada_out", [512, 72])
        for i in range(DEPTH):
            self.dint("cc_in%d" % i, [256, 2048], BF16)
            self.dint("cc_out%d" % i, [1024, 2048], BF16)

    def build(self):
        nc, P = self.nc, self.P
        self.declare()
        dr = self.dr
        with ExitStack() as st:
            self.A = A = Arena(nc, st, 51200)
            A.prog = P
            self.ps = st.enter_context(nc.psum_tensor("ps", [128, 4096], F32))
            self.xg_ = {}
            self.x = {"p": A.alloc("xp", [8, 512]), "s": A.alloc("xs", [8, 2048])}
            self.ident = A.alloc("ident", [128], BF16)
            self.ones = A.alloc("ones", [128], BF16)
            self.swap = A.alloc("swap", [128], BF16)
            self.modt = A.alloc("modt", [24, 4, 3])
            self.mods = A.alloc("mods", [24, 4])
            self.modA = A.alloc("modA", [2, 4, 8])
            self.normw = A.alloc("normw", [4, 8])
            self.fnorm = A.alloc("fnorm", [8])
            self.cst = A.alloc("cst", [8])
            self.sel = A.alloc("sel", [2])
            self.Fp = A.alloc("Fp", [2, 512], BF16)
            self.Gp = A.alloc("Gp", [4, 256], BF16)
            self.qn_w = A.alloc("qn_w", [2, 3])
            self.kvn_w = A.alloc("kvn_w", [2, 2])
            self.tv = {"p": A.alloc("tv_p", [2]), "s": A.alloc("tv_s", [16])}
            ld = self.load
            ld("sp", self.x["p"][:], dr["xp"], [self.x["p"].k(k, 0) for k in range(8)], "ld_xp")
            ld("sp", self.x["s"][:], dr["xs"], [self.x["s"].k(k, tb) for k in range(8) for tb in range(4)], "ld_xs")
            for t, n in ((self.ident, "ident"), (self.ones, "ones"), (self.swap, "swap"), (self.normw, "normw"),
                         (self.fnorm, "fnorm"), (self.sel, "sel"), (self.Fp, "Fp"), (self.Gp, "Gp"),
                         (self.qn_w, "mla_qn"), (self.kvn_w, "mla_kvn"), (self.tv["p"], "tv_p"), (self.tv["s"], "tv_s")):
                ld("sp", t[:], dr[n], [t.k()], "ld_" + n)
            P.op("dve", lambda e: e.memset(self.cst[:, 0:1], EPS), writes=[self.cst.k()])
            P.op("dve", lambda e: e.memset(self.cst[:, 1:2], 0.0), reads=[self.cst.k()], writes=[self.cst.k()])
            self.adaln()
            for i in range(self.depth):
                j = i // 2
                fn = self.hyena if i % 2 == 0 else self.mla
                fn(GS, i, j, part=1)
                self.A.push()
                self._wout = A.alloc("wout", [8, 1024], BF16)
                self.load("pool", self._wout[:], dr["hy_wout" if i % 2 == 0 else "mla_wo"][j], [self._wout.k()], "ld_wout")
                fn(GP, i, j, part="pre")
                fn(GS, i, j, part=2)
                fn(GP, i, j, part="post")
                self.A.pop()
            self.P.muted = False
            self.final(GP, "yp")
            self.final(GS, "ys")
            P.emit(final_dma_sems=self.outs)
        return nc

    def adaln(self):
        P, A, dr = self.P, self.A, self.dr
        with self.scope():
            cnd = A.alloc("cond", [8, 3])
            cb = A.alloc("condb", [8, 3], BF16)
            adb = A.alloc("adb", [4, 6])
            res = A.alloc("adares", [6, 4, 3])
            W = [A.alloc("adaW%d" % i, [8, 768], BF16) for i in range(2)]
            self.load("sp", cnd[:], dr["cond"], [cnd.k()], "ld_cond")
            self.load("sp", adb[:], dr["ada_b"], [adb.k()], "ld_adb")
            P.op("act", lambda e: e.activation(cb[:], cnd[:], AF.Silu), reads=[cnd.k()], writes=[cb.k()])
            b = self.bank()
            for i in range(4):
                Wt = W[i % 2]
                self.load("pool", Wt[:], dr["ada_w"][i], [Wt.k()], "ld_adaW%d" % (i % 2))
                for m in range(6):
                    c0 = (i * 6 + m) * 3
                    for k in range(8):
                        self.mm(self.psb(b, n=3, off=c0), Wt[:, k, 128 * m:128 * m + 128], cb[:, k, :],
                                k == 0, k == 7, [Wt.k(), cb.k()], [self.pk(b)])
            for i in range(4):
                for m in range(6):
                    c0 = (i * 6 + m) * 3
                    P.op("dve", lambda e, i=i, m=m, c0=c0: e.tensor_scalar(
                        res[:, m, i, :], self.psb(b, n=3, off=c0), adb[:, i, m:m + 1], None, ALU.add),
                        reads=[self.pk(b), adb.k()], writes=[res.k(i, m)])
            rk = [res.k(i, m) for i in range(4) for m in range(6)]
            self.load("sp", dr["cc_ada_in"], res[:].rearrange("p a b c -> p (a b c)"), ["cc_ada_in"], "st_ada", reads=rk)
            P.op("pool", lambda e: e.collective_compute("AllGather", ALU.bypass, replica_groups=GROUP4,
                                                        ins=[dr["cc_ada_in"]], outs=[dr["cc_ada_out"]]),
                 reads=["cc_ada_in"], writes=["cc_ada_out"], dma_sem="cc_ada", inc=1)
            self.load("sp", self.modt[:].rearrange("p (j m) i c -> p j (m i c)", j=4),
                      dr["cc_ada_out"].rearrange("(j p) f -> p j f", p=128), [self.modt.k()], "ld_modt",
                      reads=["cc_ada_out"])
            mt, ms = self.modt, self.mods
            P.op("dve", lambda e: e.tensor_scalar(ms[:], mt[:, :, :, 1], self.sel[:, 0:1], None, ALU.mult),
                 reads=[mt.k(), self.sel.k()], writes=[ms.k()])
            P.op("dve", lambda e: e.scalar_tensor_tensor(ms[:], mt[:, :, :, 2], self.sel[:, 1:2], ms[:], ALU.mult, ALU.add),
                 reads=[mt.k(), self.sel.k(), ms.k()], writes=[ms.k()])
            mA = self.modA
            for gi in range(2):
                for i in range(4):
                    src = mt[:, 8:16, i, 0] if gi == 0 else ms[:, 8:16, i]
                    P.op("dve", lambda e, gi=gi, i=i, src=src: e.scalar_tensor_tensor(
                        mA[:, gi, i, :], src, 1.0, self.normw[:, i, :], ALU.add, ALU.mult),
                        reads=[mt.k(), ms.k(), self.normw.k(), mA.k()], writes=[mA.k()])

    def mod(self, g, i, part, k):
        c = part * 8 + k
        if g.samp:
            return self.mods[:, c, i:i + 1]
        return self.modt[:, c, i, 0:1]

    def modkeys(self):
        return [self.modt.k(), self.mods.k(), self.modA.k()]

    def norm_scratch(self):
        A = self.A
        return {"sq": A.alloc("sq", [8, 512], BF16), "rs": [A.alloc("rstd%d" % q, [512]) for q in range(2)],
                "tmp": [A.alloc("nt%d" % q, [512]) for q in range(2)], "n": 0}

    def norm_tb(self, g, i, tb, out_fn, wk, sc):
        P = self.P
        x = self.x[g.name]
        gi = 1 if g.samp else 0
        ts = slice(tb * 512, (tb + 1) * 512)
        s_, r_ = sc["sq"], sc["rs"][tb % 2]
        P.op("act", lambda e: e.activation(s_[:, 0:4, :], x[:, 0:4, ts], AF.Square),
             reads=[x.k(k, tb) for k in range(4)], writes=[s_.k(0)])
        P.op("dve", lambda e: e.tensor_tensor(s_[:, 4:8, :], x[:, 4:8, ts], x[:, 4:8, ts], ALU.mult),
             reads=[x.k(k, tb) for k in range(4, 8)], writes=[s_.k(1)])
        b = self.bank()
        for k in range(8):
            self.mm(self.psb(b), self.ones[:], s_[:, k, :], k == 0, k == 7, [self.ones.k(), s_.k(k // 4)], [self.pk(b)])
        P.op("act", lambda e: e.activation(r_[:], self.psb(b), AF.Ln, bias=self.cst[:, 0:1], scale=1.0 / D),
             reads=[self.pk(b), self.cst.k()], writes=[r_.k()])
        P.op("act", lambda e: e.activation(r_[:], r_[:], AF.Exp, scale=-0.5), reads=[r_.k()], writes=[r_.k()])
        for k in range(8):
            t_ = sc["tmp"][sc["n"] % 2]
            sc["n"] += 1
            P.op("dve", lambda e, k=k, t_=t_: e.tensor_tensor(t_[:], x[:, k, ts], r_[:], ALU.mult),
                 reads=[x.k(k, tb), r_.k()], writes=[t_.k()])
            P.op("act", lambda e, k=k, t_=t_: e.activation(out_fn(k), t_[:], AF.Identity,
                                                          bias=self.mod(g, i, 0, k), scale=self.modA[:, gi, i, k:k + 1]),
                 reads=[t_.k()] + self.modkeys(), writes=wk(k))

    def norm_mod(self, g, i, h):
        def body():
            sc = self.norm_scratch()
            for tb in range(g.TB):
                self.norm_tb(g, i, tb, lambda k, tb=tb: h[:, k, tb * 512:(tb + 1) * 512],
                             lambda k, tb=tb: [h.k(k, tb)], sc)
        if g.samp:
            with self.scope():
                body()
        else:
            body()

    def resid_update(self, g, i, yga, w, TBs):
        P = self.P
        x = self.x[g.name]
        for tb in range(TBs):
            for m in range(8):
                b = self.bank()
                for k in range(8):
                    self.mm(self.psb(b), w[:, k, 128 * m:128 * m + 128], yga[:, k, tb * 512:(tb + 1) * 512],
                            k == 0, k == 7, [w.k(), yga.k(k, tb)], [self.pk(b)])
                P.op("dve", lambda e, b=b, m=m, tb=tb: e.scalar_tensor_tensor(
                    x[:, m, tb * 512:(tb + 1) * 512], self.psb(b), self.mod(g, i, 2, m),
                    x[:, m, tb * 512:(tb + 1) * 512], ALU.mult, ALU.add),
                    reads=[self.pk(b), x.k(m, tb)] + self.modkeys(), writes=[x.k(m, tb)])

    def gather_and_project(self, g, i, j, yg, wname, part):
        P, A, dr = self.P, self.A, self.dr
        if part == 1:
            self.load("sp", dr["cc_in%d" % i].rearrange("(c p) t -> p c t", p=128), yg[:],
                      ["cc_in%d" % i], "st_cc%d" % i, reads=[yg.k(c, tb) for c in range(2) for tb in range(4)])
            P.op("pool", lambda e: e.collective_compute("AllGather", ALU.bypass, replica_groups=GROUP4,
                                                        ins=[dr["cc_in%d" % i]], outs=[dr["cc_out%d" % i]]),
                 reads=["cc_in%d" % i], writes=["cc_out%d" % i], dma_sem="cc%d" % i, inc=1, persist=True)
            return
        with self.scope():
            yga = A.alloc("yga", [8, 2048], BF16)
            w = self._wout
            src = dr["cc_out%d" % i].rearrange("(k p) t -> p k t", p=128)
            for tb in range(4):
                self.load("sp", yga[:, :, tb * 512:(tb + 1) * 512], src[:, :, tb * 512:(tb + 1) * 512],
                          [yga.k(k, tb) for k in range(8)], "ld_yga%d" % tb, reads=["cc_out%d" % i])
            import os
            if os.environ.get("K_DBG_P2") == "loads":
                return
            self.resid_update(g, i, yga, w, 4)

    def sin_layer(self, ps_ap, rows, n, bvec, fvec, out_ap, reads, writes, scr):
        P = self.P
        t1, ki, kf = scr
        P.op("dve", lambda e: e.tensor_scalar(t1[rows, 0:n], ps_ap, bvec, fvec, ALU.add, ALU.mult),
             reads=reads, writes=[t1.k()])
        P.op("dve", lambda e: e.tensor_copy(ki[rows, 0:n], t1[rows, 0:n]), reads=[t1.k()], writes=[ki.k()])
        P.op("dve", lambda e: e.tensor_tensor(t1[rows, 0:n], t1[rows, 0:n], ki[rows, 0:n], ALU.subtract),
             reads=[t1.k(), ki.k()], writes=[t1.k()])
        P.op("act", lambda e: e.activation(out_ap, t1[rows, 0:n], AF.Sin, scale=2 * math.pi * (1 - 1e-6)),
             reads=[t1.k()], writes=writes)

    def hyena(self, g, i, j, part):
        P, A, dr = self.P, self.A, self.dr
        n = g.name
        if part == 2:
            self.gather_and_project(g, i, j, None, "hy_wout", 2)
            return
        NT, L, NB, TB, C, SBW, NFB = g.NT, g.L, g.NB, g.TB, g.C, g.SBW, g.NFB
        nseq = g.nseq
        HP = NFB // 2
        if part == "post":
            yg = self._carry
            with self.scope():
                w = self._wout
                ygk = _KeyAll(yg, [(cb, s) for cb in range(g.ncb) for s in range(nseq)])
                self.resid_update(g, i, ygk, w, 1)
            self.cp("Wp")
            if self.use_barriers:
                self.P.barrier()
            self.A.pop()
            return
        if not g.samp:
            self.A.push()
            self._carry = A.alloc("yg", [g.ncb, NT], BF16)
        with self.scope():
            xg = A.alloc("xg", [g.ncb, NT])
            zT = A.alloc("zT", [nseq * NB, C], BF16)
            with self.scope():
                h = A.alloc("h", [8, NT], BF16)
                Wt = [A.alloc("win%d" % q, [8, 512], BF16) for q in range(2)]
                cw = A.alloc("convw", [g.ncb, 3, 4])
                U = [A.alloc("u%d" % q, [NT]) for q in range(2)]
                zc = A.alloc("zc", [NT], BF16)
                self.load("sp", cw[:], dr["hy_conv_" + n][j], [cw.k()], "ld_convw")
                for cb in range(min(2, g.ncb)):
                    self.load("pool", Wt[cb][:], dr["hy_win_" + n][j, cb], [Wt[cb].k()], "ld_win%d" % cb)
                self.norm_mod(g, i, h)
                for cb in range(g.ncb):
                    W = Wt[cb % 2]
                    if cb >= 2:
                        self.load("pool", W[:], dr["hy_win_" + n][j, cb], [W.k()], "ld_win%d" % (cb % 2))

                    def proj(gi):
                        b0 = self.bank4() if TB == 4 else self.bank()
                        for tb in range(TB):
                            for k in range(8):
                                self.mm(self.psb(b0 + tb), W[:, k, 128 * gi:128 * gi + 128],
                                        h[:, k, tb * 512:(tb + 1) * 512], k == 0, k == 7,
                                        [W.k(), h.k(k, tb)], [self.pk(b0 + tb)])
                        return b0

                    def conv(gi, b0, Ut):
                        pk = [self.pk(b0 + tb) for tb in range(TB)]
                        Pf = self.ps[:, b0 * 512:b0 * 512 + NT]
                        c_ = lambda q: cw[:, cb, gi, q:q + 1]
                        P.op("act", lambda e: e.activation(Ut[:], Pf, AF.Identity, bias=c_(3), scale=c_(1)),
                             reads=pk + [cw.k()], writes=[Ut.k()])
                        Pv = Pf.rearrange("p (s l) -> p s l", s=nseq)
                        Uv = Ut[:].rearrange("p (s l) -> p s l", s=nseq)
                        P.op("dve", lambda e: e.scalar_tensor_tensor(Uv[:, :, 1:L], Pv[:, :, 0:L - 1], c_(0), Uv[:, :, 1:L],
                                                                     ALU.mult, ALU.add),
                             reads=pk + [cw.k(), Ut.k()], writes=[Ut.k()])
                        P.op("dve", lambda e: e.scalar_tensor_tensor(Uv[:, :, 0:L - 1], Pv[:, :, 1:L], c_(2), Uv[:, :, 0:L - 1],
                                                                     ALU.mult, ALU.add),
                             reads=pk + [cw.k(), Ut.k()], writes=[Ut.k()])

                    bv = proj(2)
                    bx1 = proj(1)
                    conv(2, bv, U[0])
                    conv(1, bx1, U[1])
                    bx0 = proj(0)
                    P.op("dve", lambda e: e.tensor_tensor(zc[:], U[0][:], U[1][:], ALU.mult),
                         reads=[U[0].k(), U[1].k()], writes=[zc.k()])
                    bg = proj(3)
                    conv(0, bx0, U[0])
                    P.op("act", lambda e, bg=bg: e.activation(U[1][:], self.ps[:, bg * 512:bg * 512 + NT], AF.Silu),
                         reads=[self.pk(bg + tb) for tb in range(TB)], writes=[U[1].k()])
                    P.op("dve", lambda e, cb=cb: e.tensor_tensor(xg[:, cb, :], U[0][:], U[1][:], ALU.mult),
                         reads=[U[0].k(), U[1].k()], writes=[xg.k(cb)])
                    nblk = NT // 128
                    for b8 in range(0, nblk, 8):
                        nb_ = min(8, nblk - b8)
                        bt = self.bank()
                        pt = self.psb(bt).bitcast(BF16)
                        for q in range(nb_):
                            P.op("pe", lambda e, q=q, b8=b8, pt=pt: e.transpose(
                                pt[:, q * 128:(q + 1) * 128], zc[:, (b8 + q) * 128:(b8 + q + 1) * 128], self.ident[:]),
                                reads=[zc.k(), self.ident.k()], writes=[self.pk(bt)])
                        P.op("act", lambda e, b8=b8, nb_=nb_, pt=pt, cb=cb: e.activation(
                            zT[:, b8:b8 + nb_, cb * 128:(cb + 1) * 128],
                            pt[:, 0:nb_ * 128].rearrange("p (a b) -> p a b", a=nb_), AF.Copy),
                            reads=[self.pk(bt)], writes=[zT.k(cb, b8)])
            zTk = [zT.k(cb, b8) for cb in range(g.ncb) for b8 in range(0, NT // 128, 8)]
            self.cp("A")
            with self.scope():
                FC = 2 * C
                filtT = A.alloc("filtT", [NB, FC], BF16)
                rn2 = A.alloc("rn2", [FC])
                bias2 = A.alloc("bias2", [C])
                Y = A.alloc("Y", [NFB, nseq, C], BF16)
                yg = A.alloc("yg", [g.ncb, NT], BF16) if g.samp else self._carry
                with self.scope():
                    zemb = A.alloc("zemb", [L], BF16)
                    w1 = A.alloc("fw1", [64], BF16)
                    w2 = A.alloc("fw2", [64], BF16)
                    w3 = A.alloc("fw3", [FC], BF16)
                    fv = A.alloc("fvec", [4])
                    nd = A.alloc("ndelta", [C])
                    hd1 = A.alloc("hd1", [L], BF16)
                    hd2 = A.alloc("hd2", [L], BF16)
                    tqs = list(range(0, L, 512))
                    scrs = [(A.alloc("sn_t%d" % q, [512]), A.alloc("sn_i%d" % q, [512], I32), None) for q in range(len(tqs))]
                    dec = [A.alloc("dec%d" % q, [C]) for q in range(2)]
                    absb = [A.alloc("absb%d" % q, [512], BF16) for q in range(3)]
                    self.load("sp", zemb[0:33], dr["zemb_" + n], [zemb.k()], "ld_zemb")
                    self.load("pool", w1[0:33], dr["hy_fw1"][j], [w1.k()], "ld_fw1")
                    self.load("pool", w2[0:64], dr["hy_fw2"][j], [w2.k()], "ld_fw2")
                    self.load("pool", w3[0:64], dr["hy_fw3_" + n][j], [w3.k()], "ld_fw3")
                    self.load("sp", fv[0:64], dr["hy_fvec"][j], [fv.k()], "ld_fvec")
                    self.load("sp", nd[:], dr["ndelta_" + n], [nd.k()], "ld_nd")
                    self.load("sp", bias2[:], dr["hy_fbias_" + n][j], [bias2.k()], "ld_fbias")
                    P.op("dve", lambda e: e.tensor_scalar(fv[0:64, 3:4], fv[0:64, 1:2], 1.0 / (2 * math.pi), None, ALU.mult),
                         reads=[fv.k()], writes=[fv.k()])
                    P.op("dve", lambda e: e.tensor_scalar(bias2[:], bias2[:], 2.0 / (2 * L), None, ALU.mult),
                         reads=[bias2.k()], writes=[bias2.k()])
                    R64 = slice(0, 64)
                    for ti, tq in enumerate(tqs):
                        nn = min(512, L - tq)
                        b = self.bank()
                        self.mm(self.psb(b, R64, nn), w1[0:33, :], zemb[0:33, tq:tq + nn], True, True,
                                [w1.k(), zemb.k()], [self.pk(b)])
                        self.sin_layer(self.psb(b, R64, nn), R64, nn, fv[0:64, 0:1], fv[0:64, 3:4], hd1[0:64, tq:tq + nn],
                                       [self.pk(b), fv.k()], [hd1.k(tq)], scrs[ti])
                    for ti, tq in enumerate(tqs):
                        nn = min(512, L - tq)
                        b = self.bank()
                        self.mm(self.psb(b, R64, nn), w2[0:64, :], hd1[0:64, tq:tq + nn], True, True,
                                [w2.k(), hd1.k(tq)], [self.pk(b)])
                        self.sin_layer(self.psb(b, R64, nn), R64, nn, fv[0:64, 2:3], fv[0:64, 3:4], hd2[0:64, tq:tq + nn],
                                       [self.pk(b), fv.k()], [hd2.k(tq)], scrs[ti])
                    SW = min(512, C)
                    nseg = FC // SW
                    nbank = [self.bank() for _ in range(nseg)]
                    pend = []
                    na = 0

                    def nsum(cq_, blk_, ab_):
                        self.mm(self.psb(nbank[cq_], n=SW), self.ones[:], ab_[:, 0:SW], blk_ == 0, blk_ == NB - 1,
                                [self.ones.k(), ab_.k()], [self.pk(nbank[cq_])])
                    for blk in range(NB):
                        dc = dec[blk % 2]
                        P.op("act", lambda e, dc=dc, blk=blk: e.activation(dc[:], nd[:], AF.Exp, scale=self.tv[n][:, blk:blk + 1]),
                             reads=[nd.k(), self.tv[n].k()], writes=[dc.k()])
                        for cq in range(nseg):
                            b = self.bank()
                            while b in nbank:
                                b = self.bank()
                            self.mm(self.psb(b, n=SW), hd2[0:64, blk * 128:(blk + 1) * 128], w3[0:64, cq * SW:(cq + 1) * SW],
                                    True, True, [hd2.k((blk * 128) // 512 * 512), w3.k()], [self.pk(b)])
                            dcol = (cq * SW) % C
                            P.op("dve", lambda e, b=b, blk=blk, cq=cq, dc=dc, dcol=dcol: e.tensor_tensor(
                                filtT[:, blk, cq * SW:(cq + 1) * SW], self.psb(b, n=SW), dc[:, dcol:dcol + SW], ALU.mult),
                                reads=[self.pk(b), dc.k()], writes=[filtT.k(blk, cq)])
                            ab = absb[na % 3]
                            na += 1
                            P.op("act", lambda e, ab=ab, blk=blk, cq=cq: e.activation(
                                ab[:, 0:SW], filtT[:, blk, cq * SW:(cq + 1) * SW], AF.Abs),
                                reads=[filtT.k(blk, cq)], writes=[ab.k()])
                            pend.append((cq, blk, ab))
                            if len(pend) > 2:
                                nsum(*pend.pop(0))
                    for pp in pend:
                        nsum(*pp)
                    for cq in range(nseg):
                        P.op("act", lambda e, cq=cq: e.activation(rn2[:, cq * SW:(cq + 1) * SW], self.psb(nbank[cq], n=SW), AF.Ln),
                             reads=[self.pk(nbank[cq])], writes=[rn2.k(cq)])
                        P.op("act", lambda e, cq=cq: e.activation(rn2[:, cq * SW:(cq + 1) * SW], rn2[:, cq * SW:(cq + 1) * SW],
                                                                 AF.Exp, scale=-1.0),
                             reads=[rn2.k(cq)], writes=[rn2.k(cq)])
                        P.op("dve", lambda e, cq=cq: e.tensor_scalar(rn2[:, cq * SW:(cq + 1) * SW], rn2[:, cq * SW:(cq + 1) * SW],
                                                                    2.0 / (2 * L), None, ALU.mult),
                             reads=[rn2.k(cq)], writes=[rn2.k(cq)])
                self.cp("F")
                SW = min(512, C)
                nseg = FC // SW
                rnk = [rn2.k(cq) for cq in range(nseg)]
                ftk = lambda tb: [filtT.k(tb, q) for q in range(nseg)]
                with self.scope():
                    if g.samp:
                        Ft = [A.alloc("Ft%d" % q, [16, 128], BF16) for q in range(4)]
                    Hr = [A.alloc("Hr%d" % q, [SBW]) for q in range(2)]
                    Hb = [A.alloc("Hbt%d" % q, [SBW]) for q in range(2)]
                    Zs = [[A.alloc("Zs%d_%d" % (q, s), [SBW]) for s in range(nseq)] for q in range(2)]
                    tt = [A.alloc("yt%d" % q, [SBW]) for q in range(4)]
                    fx = A.alloc("fx", [SBW])
                    nf = 0
                    for sb in range(g.nsb):
                        c0 = sb * SBW
                        for bp in range(HP):
                            for half in range(2):
                                fb = bp + half * HP
                                if g.samp:
                                    F_ = Ft[nf % 4]
                                    nf += 1
                                    self.load("sp", F_[:], dr["Fs"][fb], [F_.k()], "ld_F%d" % ((nf - 1) % 4))
                                    Fap = lambda tb, F_=F_: F_[:, tb, :]
                                    Fk = F_.k()
                                else:
                                    Fap = lambda tb, fb=fb: self.Fp[:, tb, fb * 128:(fb + 1) * 128]
                                    Fk = self.Fp.k()
                                bHf, bHb = self.bank(), self.bank()
                                bZ = [self.bank() for _ in range(nseq)]
                                for tb in range(NB):
                                    st_, sp_ = tb == 0, tb == NB - 1
                                    self.mm(self.psb(bHf, n=SBW), Fap(tb), filtT[:, tb, c0:c0 + SBW], st_, sp_,
                                            [Fk] + ftk(tb), [self.pk(bHf)])
                                    self.mm(self.psb(bHb, n=SBW), Fap(tb), filtT[:, tb, C + c0:C + c0 + SBW], st_, sp_,
                                            [Fk] + ftk(tb), [self.pk(bHb)])
                                    for s in range(nseq):
                                        self.mm(self.psb(bZ[s], n=SBW), Fap(tb), zT[:, s * NB + tb, c0:c0 + SBW], st_, sp_,
                                                [Fk] + zTk, [self.pk(bZ[s])])
                                H_, T_ = Hr[half], Hb[half]
                                P.op("dve", lambda e, T_=T_, bHb=bHb: e.tensor_tensor(T_[:], self.psb(bHb, n=SBW), rn2[:, C + c0:C + c0 + SBW], ALU.mult),
                                     reads=[self.pk(bHb)] + rnk, writes=[T_.k()])
                                P.op("dve", lambda e, H_=H_, bHf=bHf: e.tensor_tensor(H_[:], self.psb(bHf, n=SBW), rn2[:, c0:c0 + SBW], ALU.mult),
                                     reads=[self.pk(bHf)] + rnk, writes=[H_.k()])
                                fix = bp == 0 and half == 1
                                if fix:
                                    P.op("dve", lambda e, H_=H_, T_=T_: e.tensor_tensor(fx[0:1, :], H_[0:1, :], T_[0:1, :], ALU.add),
                                         reads=[H_.k(), T_.k()], writes=[fx.k()])
                                P.op("dve", lambda e, H_=H_, T_=T_, half=half: e.tensor_tensor(
                                    H_[:], H_[:], T_[:], ALU.add if half == 0 else ALU.subtract),
                                    reads=[H_.k(), T_.k()], writes=[H_.k()])
                                if fix:
                                    P.op("dve", lambda e, H_=H_: e.tensor_copy(H_[0:1, :], fx[0:1, :]),
                                         reads=[fx.k(), H_.k()], writes=[H_.k()])
                                if half == 0:
                                    P.op("dve", lambda e, H_=H_: e.tensor_tensor(H_[:], H_[:], bias2[:, c0:c0 + SBW], ALU.add),
                                         reads=[H_.k(), bias2.k()], writes=[H_.k()])
                                elif bp == 0:
                                    P.op("dve", lambda e, H_=H_: e.tensor_tensor(H_[0:1, :], H_[0:1, :], bias2[0:1, c0:c0 + SBW], ALU.add),
                                         reads=[H_.k(), bias2.k()], writes=[H_.k()])
                                if bp == 0:
                                    P.op("dve", lambda e, H_=H_: e.tensor_scalar(H_[0:1, :], H_[0:1, :], 0.5, None, ALU.mult),
                                         reads=[H_.k()], writes=[H_.k()])
                                for s in range(nseq):
                                    Z_ = Zs[half][s]
                                    P.op("act", lambda e, Z_=Z_, s=s, bZ=bZ: e.activation(Z_[:], self.psb(bZ[s], n=SBW), AF.Copy),
                                         reads=[self.pk(bZ[s])], writes=[Z_.k()])
                            for s in range(nseq):
                                Zr, Zi = Zs[0][s], Zs[1][s]
                                Yre = Y[:, bp, s, c0:c0 + SBW]
                                Yim = Y[:, bp + HP, s, c0:c0 + SBW]
                                rk_ = [Zr.k(), Zi.k(), Hr[0].k(), Hr[1].k()]
                                P.op("dve", lambda e, Zr=Zr: e.tensor_tensor(tt[0][:], Zr[:], Hr[0][:], ALU.mult), reads=rk_, writes=[tt[0].k()])
                                P.op("dve", lambda e, Zi=Zi: e.tensor_tensor(tt[1][:], Zi[:], Hr[1][:], ALU.mult), reads=rk_, writes=[tt[1].k()])
                                P.op("dve", lambda e, Yre=Yre: e.tensor_tensor(Yre, tt[0][:], tt[1][:], ALU.subtract),
                                     reads=[tt[0].k(), tt[1].k()], writes=[Y.k(bp, s, sb)])
                                P.op("pool", lambda e, Zr=Zr: e.tensor_tensor(tt[2][:], Zr[:], Hr[1][:], ALU.mult), reads=rk_, writes=[tt[2].k()])
                                P.op("pool", lambda e, Zi=Zi: e.tensor_tensor(tt[3][:], Zi[:], Hr[0][:], ALU.mult), reads=rk_, writes=[tt[3].k()])
                                P.op("pool", lambda e, Yim=Yim: e.tensor_tensor(Yim, tt[2][:], tt[3][:], ALU.add),
                                     reads=[tt[2].k(), tt[3].k()], writes=[Y.k(bp + HP, s, sb)])
                                if bp == 0:
                                    P.op("dve", lambda e, Zr=Zr, s=s: e.tensor_tensor(Y[0:1, 0, s, c0:c0 + SBW], Zr[0:1, :], Hr[0][0:1, :], ALU.mult),
                                         reads=rk_ + [Y.k(bp, s, sb)], writes=[Y.k(bp, s, sb)])
                                    P.op("pool", lambda e, Zi=Zi, s=s: e.tensor_tensor(Y[0:1, HP, s, c0:c0 + SBW], Zi[0:1, :], Hr[1][0:1, :], ALU.mult),
                                         reads=rk_ + [Y.k(bp + HP, s, sb)], writes=[Y.k(bp + HP, s, sb)])
                self.cp("D")
                with self.scope():
                    if g.samp:
                        Gt = [A.alloc("Gt%d" % q, [8, 512], BF16) for q in range(3)]
                        ng = 0
                        for tq in range(4):
                            bo = [self.bank() for _ in range(g.ncb)]
                            for fg in range(4):
                                G_ = Gt[ng % 3]
                                ng += 1
                                self.load("sp", G_[:], dr["Gs"][tq, fg], [G_.k()], "ld_G%d" % ((ng - 1) % 3))
                                for q in range(8):
                                    fb = fg * 8 + q
                                    for cb in range(g.ncb):
                                        self.mm(self.psb(bo[cb]), Y[:, fb, 0, cb * 128:(cb + 1) * 128], G_[:, q, :],
                                                fb == 0, fb == NFB - 1,
                                                [G_.k(), Y.k(fb, 0, 0)], [self.pk(bo[cb])])
                            for cb in range(g.ncb):
                                P.op("dve", lambda e, cb=cb, tq=tq, bo=bo: e.tensor_tensor(
                                    yg[:, cb, tq * 512:(tq + 1) * 512], self.psb(bo[cb]), xg[:, cb, tq * 512:(tq + 1) * 512], ALU.mult),
                                    reads=[self.pk(bo[cb]), xg.k(cb)], writes=[yg.k(cb, tq)])
                    else:
                        for cb in range(g.ncb):
                            sb = (cb * 128) // SBW
                            for s in range(nseq):
                                b = self.bank()
                                for fb in range(NFB):
                                    self.mm(self.psb(b, n=L), Y[:, fb, s, cb * 128:(cb + 1) * 128], self.Gp[:, fb, :],
                                            fb == 0, fb == NFB - 1, [self.Gp.k(), Y.k(fb, s, sb)], [self.pk(b)])
                                P.op("dve", lambda e, cb=cb, s=s, b=b: e.tensor_tensor(
                                    yg[:, cb, s * L:(s + 1) * L], self.psb(b, n=L), xg[:, cb, s * L:(s + 1) * L], ALU.mult),
                                    reads=[self.pk(b), xg.k(cb)], writes=[yg.k(cb, s)])
                self.cp("E")
                if g.samp:
                    self.gather_and_project(g, i, j, yg, "hy_wout", 1)
                    self.cp("W")

    def mla(self, g, i, j, part):
        P, A, dr = self.P, self.A, self.dr
        n = g.name
        if part == 2:
            self.gather_and_project(g, i, j, None, "mla_wo", 2)
            return
        NT, TB, nh, nseq, L = g.NT, g.TB, g.nh, g.nseq, g.L
        NK = NT + (PAST if g.samp else 0)
        KTB = NK // 512
        NKB = NK // 128
        HG = nh * 64
        ncg = HG // 128
        scale = 1.0 / math.sqrt(QK_NOPE + QK_ROPE)
        R64, R96a, R96 = slice(0, 64), slice(0, 96), slice(64, 96)
        if part == "post":
            og = self._carry
            with self.scope():
                w = self._wout
                ogk = _KeyAll(og, [(c, 0, par) for c in range(ncg) for par in range(2)])
                self.resid_update(g, i, ogk, w, 1)
            if self.use_barriers:
                self.P.barrier()
            self.A.pop()
            return
        if not g.samp:
            self.A.push()
            self._carry = A.alloc("og", [ncg, NT], BF16)
        with self.scope():
            sg = A.alloc("sg", [ncg, NT], BF16)
            qn = A.alloc("qn", [3, NT], BF16)
            ckvT = A.alloc("ckvT", [2, NK], BF16)
            kpeT = A.alloc("kpeT", [NK], BF16)
            with self.scope():
                hb = [A.alloc("h%d" % q, [8, 512], BF16) for q in range(2)]
                sc = self.norm_scratch()
                Wa = A.alloc("wa", [8, 832], BF16)
                Wg = A.alloc("wg", [8, HG], BF16)
                self.load("pool", Wa[:], dr["mla_wa"][j], [Wa.k()], "ld_wa")
                self.load("pool", Wg[:], dr["mla_wg_" + n][j], [Wg.k()], "ld_wg")
                sqb = A.alloc("sqb", [3, 512], BF16)
                rq = A.alloc("rq", [512])
                if g.samp:
                    rope = [A.alloc("rope%d" % q, [2, 512]) for q in range(2)]
                    kt1 = A.alloc("kt1", [512])
                    kt2 = A.alloc("kt2", [512])
                    self.load("pool", ckvT[:, :, NT:NK], dr["cckv"][j], [ckvT.k(0, TB), ckvT.k(1, TB)], "ld_cckv")
                    self.load("pool", kpeT[64:96, NT:NK], dr["ckpe"][j, 64:96], [kpeT.k(TB)], "ld_ckpe")
                else:
                    cko = A.alloc("cko", [2, 512])
                    kpo = A.alloc("kpo", [512])
                for tb in range(TB):
                    ts = slice(tb * 512, (tb + 1) * 512)
                    h = hb[tb % 2]
                    self.norm_tb(g, i, tb, lambda k: h[:, k, :], lambda k: [h.k(k)], sc)
                    if g.samp:
                        rp = rope[tb % 2]
                        self.load("sp", rp[:], dr["rope"][:, :, ts], [rp.k()], "ld_rope%d" % (tb % 2))

                    def proj(c0, M):
                        b = self.bank()
                        for k in range(8):
                            self.mm(self.psb(b, slice(0, M)), Wa[:, k, c0:c0 + M], h[:, k, :], k == 0, k == 7,
                                    [Wa.k(), h.k(k)], [self.pk(b)])
                        return b

                    def rms(banks, nch, inv_n, wv, dst, dkey, extra=None):
                        for m, b in enumerate(banks):
                            P.op("act", lambda e, m=m, b=b: e.activation(sqb[:, m, :], self.psb(b), AF.Square),
                                 reads=[self.pk(b)], writes=[sqb.k(m)])
                        bs = self.bank()
                        for m in range(nch):
                            self.mm(self.psb(bs), self.ones[:], sqb[:, m, :], m == 0, m == nch - 1,
                                    [self.ones.k(), sqb.k(m)], [self.pk(bs)])
                        P.op("act", lambda e: e.activation(rq[:], self.psb(bs), AF.Ln, bias=self.cst[:, 0:1], scale=inv_n),
                             reads=[self.pk(bs), self.cst.k()], writes=[rq.k()])
                        P.op("act", lambda e: e.activation(rq[:], rq[:], AF.Exp, scale=-0.5), reads=[rq.k()], writes=[rq.k()])
                        for m, b in enumerate(banks):
                            P.op("dve", lambda e, m=m, b=b: e.scalar_tensor_tensor(
                                dst[:, m, ts], self.psb(b), wv[:, j, m:m + 1], rq[:], ALU.mult, ALU.mult),
                                reads=[self.pk(b), rq.k(), wv.k()], writes=[dkey(m)])
                            if extra is not None:
                                extra(m, b)

                    bq = [proj(128 * m, 128) for m in range(3)]
                    bkv = [proj(384 + 128 * m, 128) for m in range(2)]
                    rms(bq, 3, 1.0 / Q_LORA, self.qn_w, qn, lambda m: qn.k(m, tb))
                    bk = proj(640, 96)
                    if g.samp:
                        bk2 = proj(736, 96)
                    if g.samp:
                        rms(bkv, 2, 1.0 / KV_LORA, self.kvn_w, ckvT, lambda m: ckvT.k(m, tb))
                    else:
                        def extra(m, b):
                            P.op("dve", lambda e: e.scalar_tensor_tensor(
                                cko[:, m, :], self.psb(b), self.kvn_w[:, j, m:m + 1], rq[:], ALU.mult, ALU.mult),
                                reads=[self.pk(b), rq.k(), self.kvn_w.k()], writes=[cko.k(m)])
                        rms(bkv, 2, 1.0 / KV_LORA, self.kvn_w, ckvT, lambda m: ckvT.k(m, tb), extra)
                        self.load("sp", dr["ckv_out"][j], cko[:], ["ckv_out%d" % j], "st_ckv%d" % j, reads=[cko.k(0), cko.k(1)])
                        self.outs.append("st_ckv%d" % j)
                    if g.samp:
                        P.op("dve", lambda e: e.tensor_tensor(kt1[R96, :], self.psb(bk, R96), rp[R96, 0, :], ALU.mult),
                             reads=[self.pk(bk), rp.k()], writes=[kt1.k()])
                        P.op("dve", lambda e: e.tensor_tensor(kt2[R96, :], self.psb(bk2, R96), rp[R96, 1, :], ALU.mult),
                             reads=[self.pk(bk2), rp.k()], writes=[kt2.k()])
                        P.op("dve", lambda e: e.tensor_tensor(kpeT[R96, ts], kt1[R96, :], kt2[R96, :], ALU.add),
                             reads=[kt1.k(), kt2.k()], writes=[kpeT.k(tb)])
                    else:
                        P.op("act", lambda e: e.activation(kpeT[R96, ts], self.psb(bk, R96), AF.Copy),
                             reads=[self.pk(bk)], writes=[kpeT.k(tb)])
                        P.op("act", lambda e: e.activation(kpo[R96, :], self.psb(bk, R96), AF.Copy),
                             reads=[self.pk(bk)], writes=[kpo.k()])
                        self.load("sp", dr["kpe_out"][j], kpo[R96, :], ["kpe_out%d" % j], "st_kpe%d" % j, reads=[kpo.k()])
                        self.outs.append("st_kpe%d" % j)
                    for m in range(ncg):
                        b = self.bank()
                        for k in range(8):
                            self.mm(self.psb(b), Wg[:, k, 128 * m:128 * m + 128], h[:, k, :], k == 0, k == 7,
                                    [Wg.k(), h.k(k)], [self.pk(b)])
                        P.op("act", lambda e, b=b, m=m: e.activation(sg[:, m, ts], self.psb(b), AF.Silu),
                             reads=[self.pk(b)], writes=[sg.k(m, tb)])
            self.cp("P1")
            QT = A.alloc("QT", [nh, NT], BF16)
            KT = A.alloc("KT", [nh, NK], BF16)
            Va = A.alloc("Vaug", [NKB, nh, 128], BF16)
            with self.scope():
                if g.samp:
                    Wq = A.alloc("wq", [3, nh, 2, 96], BF16)
                    rope = [A.alloc("rope%d" % q, [2, 512]) for q in range(2)]
                    qt1 = A.alloc("qt1", [512])
                    qt2 = A.alloc("qt2", [512])
                else:
                    Wq = A.alloc("wq", [3, nh, 96], BF16)
                Wk = A.alloc("wk", [2, nh, 64], BF16)
                Wv = A.alloc("wv", [2, HG], BF16)
                self.load("pool", Wq[:], dr["mla_wq_" + n][j], [Wq.k()], "ld_wq")
                self.load("pool", Wk[:], dr["mla_wk_" + n][j], [Wk.k()], "ld_wk")
                self.load("pool", Wv[:], dr["mla_wv_" + n][j], [Wv.k()], "ld_wv")
                P.op("pool", lambda e: e.memset(Va[:], 1.0), writes=[Va.k()])
                for tb in range(TB):
                    ts = slice(tb * 512, (tb + 1) * 512)
                    if g.samp:
                        rp = rope[tb % 2]
                        self.load("sp", rp[:], dr["rope"][:, :, ts], [rp.k()], "ld_rope%d" % (tb % 2))
                    for hh in range(nh):
                        b = self.bank()
                        for k in range(3):
                            lw = Wq[:, k, hh, 0, :] if g.samp else Wq[:, k, hh, :]
                            self.mm(self.psb(b, R96a), lw, qn[:, k, ts], k == 0, k == 2,
                                    [Wq.k(), qn.k(k, tb)], [self.pk(b)])
                        if g.samp:
                            b2 = self.bank()
                            for k in range(3):
                                self.mm(self.psb(b2, R96a), Wq[:, k, hh, 1, :], qn[:, k, ts], k == 0, k == 2,
                                        [Wq.k(), qn.k(k, tb)], [self.pk(b2)])
                            P.op("dve", lambda e, b=b, hh=hh, ts=ts: e.tensor_copy(QT[R64, hh, ts], self.psb(b, R64)),
                                 reads=[self.pk(b)], writes=[QT.k(hh, tb, 0)])
                            P.op("dve", lambda e, b=b, rp=rp: e.tensor_tensor(qt1[R96, :], self.psb(b, R96), rp[R96, 0, :], ALU.mult),
                                 reads=[self.pk(b), rp.k()], writes=[qt1.k()])
                            P.op("dve", lambda e, b2=b2, rp=rp: e.tensor_tensor(qt2[R96, :], self.psb(b2, R96), rp[R96, 1, :], ALU.mult),
                                 reads=[self.pk(b2), rp.k()], writes=[qt2.k()])
                            P.op("dve", lambda e, hh=hh, ts=ts: e.tensor_tensor(QT[R96, hh, ts], qt1[R96, :], qt2[R96, :], ALU.add),
                                 reads=[qt1.k(), qt2.k()], writes=[QT.k(hh, tb, 1)])
                        else:
                            P.op("act", lambda e, b=b, hh=hh, ts=ts: e.activation(QT[R96a, hh, ts], self.psb(b, R96a), AF.Copy),
                                 reads=[self.pk(b)], writes=[QT.k(hh, tb, 0), QT.k(hh, tb, 1)])
                import os
                if os.environ.get("K_DBG_P2STOP") == "q":
                    P.muted = True
                for tb in range(KTB):
                    ts = slice(tb * 512, (tb + 1) * 512)
                    for hh in range(nh):
                        b = self.bank()
                        for k in range(2):
                            self.mm(self.psb(b, R64), Wk[:, k, hh, :], ckvT[:, k, ts], k == 0, k == 1,
                                    [Wk.k(), ckvT.k(k, tb)], [self.pk(b)])
                        P.op("dve", lambda e, b=b, hh=hh, ts=ts: e.tensor_copy(KT[R64, hh, ts], self.psb(b, R64)),
                             reads=[self.pk(b)], writes=[KT.k("n", tb, hh)])
                        P.op("pool", lambda e, hh=hh, ts=ts: e.tensor_copy(KT[R96, hh, ts], kpeT[R96, ts]),
                             reads=[kpeT.k(tb)], writes=[KT.k("pe", tb, hh)])
                if os.environ.get("K_DBG_P2STOP") == "k":
                    P.muted = True
                for kb in range(min(NKB, int(os.environ.get("K_DBG_NKB", NKB)))):
                    for c0 in range(0, HG, 512):
                        w_ = min(512, HG - c0)
                        b = self.bank()
                        for k in range(2):
                            self.mm(self.psb(b, n=w_), ckvT[:, k, kb * 128:(kb + 1) * 128], Wv[:, k, c0:c0 + w_],
                                    k == 0, k == 1, [Wv.k(), ckvT.k(k, kb // 4)], [self.pk(b)])
                        nhh = w_ // 64
                        h0 = c0 // 64
                        for q in range(nhh):
                            hh = h0 + q
                            par = hh % 2
                            src = self.psb(b, n=64, off=q * 64)
                            dst = Va[:, kb, hh, par * 64:(par + 1) * 64]
                            if kb % 2 == 0 or os.environ.get("K_DBG_VACT"):
                                P.op("act", lambda e, src=src, dst=dst: e.activation(dst, src, AF.Copy),
                                     reads=[self.pk(b), Va.k()], writes=[Va.k(kb, hh)])
                            else:
                                P.op("dve", lambda e, src=src, dst=dst: e.tensor_copy(dst, src),
                                     reads=[self.pk(b), Va.k()], writes=[Va.k(kb, hh)])
            self.cp("P2")
            with self.scope():
                og = A.alloc("og", [ncg, NT], BF16) if g.samp else self._carry
                NPT = 3
                PT = [A.alloc("PT%d" % q, [1024], BF16) for q in range(NPT)]
                NIF = 2 if g.samp else 4
                rdb = [A.alloc("rdb%d" % q, [512], BF16) for q in range(NIF)]
                if g.samp:
                    ot = [A.alloc("ot", [512])] * NIF
                    rdf = [A.alloc("rdf", [512])] * NIF
                else:
                    ot = [A.alloc("ot%d" % q, [512]) for q in range(NIF)]
                    rdf = [A.alloc("rdf%d" % q, [512]) for q in range(NIF)]
                for q in range(NIF):
                    P.op("dve", lambda e, q=q: e.memset(rdb[q][:], 0.0), writes=[rdb[q].k()])
                QB = min(512, L)
                npt = 0
                nat = 0
                nsp = 0
                tails = []
                DEFER = 1 if g.samp else 2

                def emit_tail(tl):
                    bO, rd, o_, rf, Ro, Rd, hh, qs, qtb, par = tl
                    if g.samp:
                        P.op("dve", lambda e: e.reciprocal(rf[Rd, 0:QB], self.psb(bO, Rd, QB)),
                             reads=[self.pk(bO)], writes=[rf.k()])
                        P.op("dve", lambda e: e.tensor_copy(rd[Rd, 0:QB], rf[Rd, 0:QB]),
                             reads=[rf.k()], writes=[rd.k()])
                    else:
                        P.op("act", lambda e: e.activation(rf[Rd, 0:QB], self.psb(bO, Rd, QB), AF.Ln),
                             reads=[self.pk(bO)], writes=[rf.k()])
                        P.op("act", lambda e: e.activation(rd[Rd, 0:QB], rf[Rd, 0:QB], AF.Exp, scale=-1.0),
                             reads=[rf.k()], writes=[rd.k()])
                    nonlocal nsp
                    if g.samp:
                        bR = 2 * (nsp % 3)
                        nsp += 1
                    else:
                        bR = 7
                    self.mm(self.psb(bR, n=QB), self.swap[:], rd[:, 0:QB], True, True,
                            [self.swap.k(), rd.k()], [self.pk(bR)])
                    P.op("dve", lambda e: e.tensor_tensor(o_[Ro, 0:QB], self.psb(bO, Ro, QB), sg[Ro, hh // 2, qs], ALU.mult),
                         reads=[self.pk(bO), sg.k(hh // 2, qtb), rf.k()], writes=[o_.k()])
                    P.op("dve", lambda e: e.tensor_tensor(og[Ro, hh // 2, qs], o_[Ro, 0:QB], self.psb(bR, Ro, QB), ALU.mult),
                         reads=[self.pk(bR), o_.k()], writes=[og.k(hh // 2, qtb, par)])
                for s in range(nseq):
                    if g.samp:
                        kbs = list(range(NKB))
                    else:
                        kbs = [s * (L // 128) + q for q in range(L // 128)]
                    npair = len(kbs) // 2
                    for hh in range(nh):
                        par = hh % 2
                        Ro = slice(par * 64, par * 64 + 64)
                        Rd = slice((1 - par) * 64, (1 - par) * 64 + 64)
                        for qb in range(L // QB):
                            q0 = s * L + qb * QB
                            qs = slice(q0, q0 + QB)
                            qtb = q0 // 512
                            bO = (6 + (nat % 2)) if g.samp else (3 + (nat % 4))
                            pend = []
                            for ip in range(npair):
                                if g.samp:
                                    b0 = 2 * (nsp % 3)
                                    sb_ = [(b0, 0), (b0 + 1, 0)]
                                    Sv = self.ps[:, b0 * 512:b0 * 512 + 1024].rearrange("p (k n) -> p k n", k=2)[:, :, 0:QB]
                                    spk = [self.pk(b0), self.pk(b0 + 1)]
                                else:
                                    b0 = nsp % 3
                                    sb_ = [(b0, 0), (b0, QB)]
                                    Sv = self.ps[:, b0 * 512:b0 * 512 + 2 * QB].rearrange("p (k n) -> p k n", k=2)
                                    spk = [self.pk(b0)]
                                nsp += 1
                                for u in range(2):
                                    kb = kbs[2 * ip + u]
                                    ktb = kb // 4
                                    self.mm(self.psb(sb_[u][0], n=QB, off=sb_[u][1]), KT[0:96, hh, kb * 128:(kb + 1) * 128],
                                            QT[0:96, hh, qs], True, True,
                                            [KT.k("n", ktb, hh), KT.k("pe", ktb, hh), QT.k(hh, qtb, 0), QT.k(hh, qtb, 1)],
                                            [self.pk(sb_[u][0])])
                                pt = PT[npt % NPT]
                                npt += 1
                                Pv = pt[:, 0:2 * QB].rearrange("p (k n) -> p k n", k=2)
                                P.op("act", lambda e, Sv=Sv, Pv=Pv: e.activation(Pv, Sv, AF.Exp, scale=scale),
                                     reads=spk, writes=[pt.k()])
                                pend.append((ip, kbs[2 * ip], kbs[2 * ip + 1], hh, pt))
                                if len(pend) > 2:
                                    self._pv2(bO, Va, pend.pop(0), QB, npair)
                            for pp in pend:
                                self._pv2(bO, Va, pp, QB, npair)
                            tails.append((bO, rdb[nat % NIF], ot[nat % NIF], rdf[nat % NIF], Ro, Rd, hh, qs, qtb, par))
                            nat += 1
                            if len(tails) > DEFER:
                                emit_tail(tails.pop(0))
                for tl in tails:
                    emit_tail(tl)
                self.cp("ATT")
                if g.samp:
                    P.op("sp", lambda e: e.dma_start(out=dr["cc_in%d" % i].rearrange("(c p) t -> p c t", p=128), in_=og[:]),
                         reads=[og.k(c, tb, par) for c in range(ncg) for tb in range(TB) for par in range(2)],
                         writes=["cc_in%d" % i], dma_sem="st_cc%d" % i)
                    P.op("pool", lambda e: e.collective_compute("AllGather", ALU.bypass, replica_groups=GROUP4,
                                                                ins=[dr["cc_in%d" % i]], outs=[dr["cc_out%d" % i]]),
                         reads=["cc_in%d" % i], writes=["cc_out%d" % i], dma_sem="cc%d" % i, inc=1, persist=True)

    def _pv2(self, bO, Va, prev, QB, npair):
        ip, kb0, kb1, hh, pt = prev
        for u, kb in enumerate((kb0, kb1)):
            self.mm(self.psb(bO, n=QB), Va[:, kb, hh, :], pt[:, u * QB:(u + 1) * QB], ip == 0 and u == 0,
                    ip == npair - 1 and u == 1, [Va.k(kb, hh), Va.k(), pt.k()], [self.pk(bO)])

    def _pv(self, bO, Va, prev, QB, nk):
        ix, kb, hh, pt = prev
        self.mm(self.psb(bO, n=QB), Va[:, kb, hh, :], pt[:, 0:QB], ix == 0, ix == nk - 1,
                [Va.k(kb, hh), Va.k(), pt.k()], [self.pk(bO)])


class _KeyAll:
    def __init__(self, t, subs):
        self.t = t
        self.subs = subs
        self.ap = t.ap

    def __getitem__(self, idx):
        return self.t.ap[idx]

    def k(self, *sub):
        if not sub:
            return self.t.k()
        kk = sub[0]
        return _MultiKey([self.t.k(*s) for s in self.subs if s[0] == kk])


class _MultiKey(list):
    pass


_orig_op = Prog.op


def _op_flat(self, eng, fn, reads=(), writes=(), **kw):
    def flat(ks):
        out = []
        for k in ks:
            if isinstance(k, _MultiKey):
                out.extend(k)
            else:
                out.append(k)
        return out
    return _orig_op(self, eng, fn, flat(reads), flat(writes), **kw)


Prog.op = _op_flat


def _final(self, g, oname):
    P, A = self.P, self.A
    x = self.x[g.name]
    dst = self.dr[oname]
    with self.scope():
        yo = [A.alloc("yo%d" % q, [8, 512]) for q in range(2)]
        sq = [A.alloc("fsq%d" % q, [8, 512], BF16) for q in range(2)]
        rs = [A.alloc("frs%d" % q, [512]) for q in range(2)]
        for tb in range(g.NT // 512):
            y_, s_, r_ = yo[tb % 2], sq[tb % 2], rs[tb % 2]
            ts = slice(tb * 512, (tb + 1) * 512)
            P.op("act", lambda e, s_=s_, ts=ts: e.activation(s_[:], x[:, :, ts], AF.Square),
                 reads=[x.k(k, tb) for k in range(8)], writes=[s_.k()])
            b = self.bank()
            for k in range(8):
                self.mm(self.psb(b), self.ones[:], s_[:, k, :], k == 0, k == 7, [self.ones.k(), s_.k()], [self.pk(b)])
            P.op("act", lambda e, b=b, r_=r_: e.activation(r_[:], self.psb(b), AF.Ln, bias=self.cst[:, 0:1], scale=1.0 / D),
                 reads=[self.pk(b), self.cst.k()], writes=[r_.k()])
            P.op("act", lambda e, r_=r_: e.activation(r_[:], r_[:], AF.Exp, scale=-0.5), reads=[r_.k()], writes=[r_.k()])
            for k in range(8):
                P.op("dve", lambda e, k=k, ts=ts, y_=y_, r_=r_: e.scalar_tensor_tensor(
                    y_[:, k, :], x[:, k, ts], self.fnorm[:, k:k + 1], r_[:], ALU.mult, ALU.mult),
                    reads=[x.k(k, tb), r_.k(), self.fnorm.k(), y_.k()], writes=[y_.k()])
            self.load("sp", dst[:, :, ts], y_[:], [(oname, tb)], "st_%s%d" % (oname, tb % 2), reads=[y_.k()])
            if ("st_%s%d" % (oname, tb % 2)) not in self.outs:
                self.outs.append("st_%s%d" % (oname, tb % 2))


Builder.final = _final


def _prep(inp):
    C = _consts()
    f = _f32
    common = {}
    common["cond"] = f(_chunk(np.stack([inp["c_ctx"], inp["c"][0], inp["c"][1]], axis=1)))
    common["normw"] = f(_chunk(inp["norm_w"].T))
    common["normw"] = f(common["normw"].transpose(0, 2, 1))
    common["fnorm"] = f(_chunk(inp["final_norm"][:, None])[:, :, 0])
    for k in ("ident", "ones", "swap", "Fs", "Gs", "Fp", "Gp", "zemb_s", "zemb_p", "tv_s", "tv_p", "rope"):
        common[k] = C[k]
    common["hy_fw1"] = f(inp["hy_f_w1"])
    common["hy_fw2"] = f(inp["hy_f_w2"])
    fv = np.zeros((2, 64, 4), np.float32)
    fv[:, :, 0] = inp["hy_f_b1"]
    fv[:, :, 1] = inp["hy_f_freq"]
    fv[:, :, 2] = inp["hy_f_b2"]
    common["hy_fvec"] = fv
    common["hy_wout"] = f(np.stack([_chunk(inp["hy_w_out"][j]) for j in range(2)]))
    common["mla_wo"] = f(np.stack([_chunk(inp["mla_w_o"][j]) for j in range(2)]))
    common["mla_qn"] = f(_chunk(inp["mla_q_norm"].T))
    common["mla_qn"] = f(common["mla_qn"].transpose(0, 2, 1))
    common["mla_kvn"] = f(_chunk(inp["mla_kv_norm"].T).transpose(0, 2, 1))
    sw = C["ropesw"]
    wa = []
    for j in range(2):
        w = inp["mla_w_in"][j]
        kpe = w[:, 640:672]
        z64 = np.zeros((D, 64), np.float32)
        wa.append(_chunk(np.concatenate([w[:, :640], z64, kpe, z64, kpe[:, sw]], axis=1)))
    common["mla_wa"] = f(np.stack(wa))

    def hy_group(chs, tag, out):
        cbs = [chs[q * 128:(q + 1) * 128] for q in range(len(chs) // 128)]
        win = np.zeros((2, len(cbs), 128, 8, 512), np.float32)
        cv = np.zeros((2, 128, len(cbs), 3, 4), np.float32)
        for j in range(2):
            w = inp["hy_w_in"][j]
            for q, cb in enumerate(cbs):
                cols = np.concatenate([gi * D + cb for gi in range(4)])
                win[j, q] = _chunk(w[:, cols])
                for gi in range(3):
                    cv[j, :, q, gi, 0:3] = inp["hy_conv_w"][j][:, gi * D + cb].T
                    cv[j, :, q, gi, 3] = inp["hy_conv_b"][j][gi * D + cb]
        out["hy_win_" + tag] = win
        out["hy_conv_" + tag] = cv
        out["hy_fw3_" + tag] = f(np.stack([inp["hy_f_w3"][j][:, np.concatenate([chs, D + chs])] for j in range(2)]))
        out["hy_fbias_" + tag] = f(np.stack([np.broadcast_to(inp["hy_f_bias"][j][chs], (128, len(chs))) for j in range(2)]))
        out["ndelta_" + tag] = f(np.broadcast_to(C["ndelta"][chs], (128, len(chs))))

    def mla_group(heads, tag, out):
        nh = len(heads)
        gcols = np.concatenate([672 + h * 64 + np.arange(64) for h in heads])
        out["mla_wg_" + tag] = f(np.stack([_chunk(inp["mla_w_in"][j][:, gcols]) for j in range(2)]))
        wk = np.zeros((2, 128, 2, nh, 64), np.float32)
        wv = np.zeros((2, 128, 2, nh * 64), np.float32)
        for j in range(2):
            w = inp["mla_w_kvb"][j].reshape(KV_LORA, NH, 128)
            wk[j] = _chunk(w[:, heads, :64])
            wv[j] = _chunk(w[:, heads, 64:].reshape(KV_LORA, nh * 64))
        out["mla_wk_" + tag] = wk
        out["mla_wv_" + tag] = wv
        wq = np.stack([inp["mla_w_qb"][j].reshape(Q_LORA, NH, 96)[:, heads] for j in range(2)])
        if tag == "p":
            out["mla_wq_p"] = f(np.stack([_chunk(wq[j]) for j in range(2)]))
        else:
            wsw = np.concatenate([wq[..., :64], wq[..., 64:][..., sw]], axis=-1)
            both = np.stack([wq, wsw], axis=3)
            out["mla_wq_s"] = f(np.stack([_chunk(both[j]) for j in range(2)]))

    hy_group(np.arange(D), "p", common)
    mla_group(list(range(NH)), "p", common)
    maps = []
    for c in range(NCORES):
        g, r = c // 4, c % 4
        m = dict(common)
        xp = inp["x_prompt"][2 * c:2 * c + 2].reshape(512, D)
        m["xp"] = f(_chunk(xp.T))
        m["xs"] = f(_chunk(inp["x_sample"][g].T))
        m["ada_w"] = f(np.stack([_chunk(inp["ada_w"][i][:, 768 * r:768 * (r + 1)]) for i in range(4)]))
        m["ada_b"] = f(inp["ada_b"][:, 768 * r:768 * (r + 1)].reshape(4, 6, 128).transpose(2, 0, 1))
        sel = np.zeros((128, 2), np.float32)
        sel[:, g] = 1.0
        m["sel"] = sel
        hy_group(np.arange(256 * r, 256 * (r + 1)), "s", m)
        mla_group(list(range(4 * r, 4 * r + 4)), "s", m)
        m["cckv"] = f(np.stack([_chunk(inp["cache_ckv"][g, j].T) for j in range(2)]))
        ck = np.zeros((2, 128, 512), np.float32)
        for j in range(2):
            ck[j, 64:96] = inp["cache_kpe"][g, j].T
        m["ckpe"] = ck
        maps.append(m)
    return maps


_NC_CACHE = {}


def _get_nc(depth=DEPTH):
    if depth not in _NC_CACHE:
        _NC_CACHE[depth] = Builder(depth).build()
    return _NC_CACHE[depth]


def kernel(**inputs):
    inp = {k: np.asarray(v) for k, v in inputs.items()}
    maps = _prep(inp)
    nc = _get_nc()
    res = run_bass_kernel_spmd(nc, maps, core_ids=list(range(NCORES)))
    R = res.results
    y_prompt = np.zeros((BATCH, SEQ, D), np.float32)
    state_ckv = np.zeros((BATCH, 2, SEQ, KV_LORA), np.float32)
    state_kpe = np.zeros((BATCH, 2, SEQ, QK_ROPE), np.float32)
    y_sample = np.zeros((DEC_BATCH, DEC_SEQ, D), np.float32)
    for c in range(NCORES):
        yp = np.asarray(R[c]["yp"]).transpose(2, 1, 0).reshape(512, D)
        y_prompt[2 * c:2 * c + 2] = yp.reshape(2, SEQ, D)
        ck = np.asarray(R[c]["ckv_out"])
        kp = np.asarray(R[c]["kpe_out"])
        for j in range(2):
            t = ck[j].transpose(2, 1, 0).reshape(512, KV_LORA)
            state_ckv[2 * c:2 * c + 2, j] = t.reshape(2, SEQ, KV_LORA)
            state_kpe[2 * c:2 * c + 2, j] = kp[j].T.reshape(2, SEQ, QK_ROPE)
    for g in range(DEC_BATCH):
        ys = np.asarray(R[4 * g]["ys"]).transpose(2, 1, 0).reshape(DEC_SEQ, D)
        y_sample[g] = ys
    return (y_prompt, y_sample, state_ckv, state_kpe)
```
